# Optimizing a Trainium2 kernel written in Bass

```python
import jax, jax.numpy as jnp
from jax import lax
import numpy as np

D_MODEL = 1024
BATCH = 2
SEQ = 8192
DEPTH = 1
DEC_BATCH = 32
DEC_SEQ = 1
PAST_LEN = 8192
PAGE_SIZE = 128

HEAD_DIM = 64
HEADS_PER_GROUP = 8
WINDOWS = (128, 512, 2048)
DILATIONS = (1, 4, 16)
N_GROUPS = 3
GROUP_WIDTH = HEADS_PER_GROUP * HEAD_DIM
ATTN_WIDTH = N_GROUPS * GROUP_WIDTH
D_CONV = D_MODEL
CONV_WIDTH = 3
D_FF = 4 * D_MODEL
Q_BLOCK = 128
EPS = 1e-6
NEG = -1e30
IN_WIDTH = 3 * ATTN_WIDTH + 3 * D_CONV + 2 * D_MODEL
SPLIT_POINTS = (ATTN_WIDTH, 2 * ATTN_WIDTH, 3 * ATTN_WIDTH,
                3 * ATTN_WIDTH + D_CONV, 3 * ATTN_WIDTH + 2 * D_CONV,
                3 * ATTN_WIDTH + 3 * D_CONV, 3 * ATTN_WIDTH + 3 * D_CONV + D_MODEL)

kernel_name = "hybrid_dilated_attn_shortconv_decoder_step"


def _rms_norm(x, g):
    xf = x.astype(jnp.float32)
    y = xf * lax.rsqrt(jnp.mean(xf * xf, axis=-1, keepdims=True) + EPS)
    return (y * g.astype(jnp.float32)).astype(x.dtype)


def _project(h, w_in):
    b, t, _ = h.shape
    p = jnp.einsum('btd,de->bte', h, w_in)
    q, k, v, c_b, c_c, c_h, g_a, g_c = jnp.split(p, SPLIT_POINTS, axis=-1)
    heads = (b, t, N_GROUPS, HEADS_PER_GROUP, HEAD_DIM)
    return q.reshape(heads), k.reshape(heads), v.reshape(heads), c_b, c_c * c_h, g_a, g_c


def _band_attention(q, k, v, band):
    n, mlen, h, d = q.shape
    nb = -(-mlen // Q_BLOCK)
    pad = nb * Q_BLOCK - mlen
    qb = jnp.pad(q, ((0, 0), (0, pad), (0, 0), (0, 0))).reshape(n, nb, Q_BLOCK, h, d)

    def key_blocks(t):
        tp = jnp.pad(t, ((0, 0), (Q_BLOCK, pad), (0, 0), (0, 0))).reshape(n, nb + 1, Q_BLOCK, h, d)
        return jnp.concatenate([tp[:, :-1], tp[:, 1:]], axis=2)

    kb, vb = key_blocks(k), key_blocks(v)
    s = jnp.einsum('nbqhd,nbkhd->nbhqk', qb, kb, preferred_element_type=jnp.float32) * (d ** -0.5)
    qi = jnp.arange(Q_BLOCK)[:, None]
    ki = jnp.arange(2 * Q_BLOCK)[None, :]
    rel = qi + Q_BLOCK - ki
    kpos = jnp.arange(nb)[:, None, None] * Q_BLOCK - Q_BLOCK + ki
    valid = (rel >= 0) & (rel <= band) & (kpos >= 0)
    s = jnp.where(valid[None, :, None], s, NEG)
    m = jnp.max(s, axis=-1)
    p = jnp.exp(s - m[..., None])
    den = jnp.sum(p, axis=-1)
    o = jnp.einsum('nbhqk,nbkhd->nbqhd', p, vb.astype(jnp.float32))
    o = o / jnp.swapaxes(den, -1, -2)[..., None]
    lse = jnp.swapaxes(m + jnp.log(den), -1, -2)
    o = o.reshape(n, nb * Q_BLOCK, h, d)[:, :mlen]
    lse = lse.reshape(n, nb * Q_BLOCK, h)[:, :mlen]
    return o, lse


def _dilated_prompt(q, k, v, dil, band):
    b, s, h, d = q.shape

    def split(t):
        return t.reshape(b, s // dil, dil, h, d).transpose(0, 2, 1, 3, 4).reshape(b * dil, s // dil, h, d)

    o, lse = _band_attention(split(q), split(k), split(v), band)
    o = o.reshape(b, dil, s // dil, h, d).transpose(0, 2, 1, 3, 4).reshape(b, s, h, d)
    lse = lse.reshape(b, dil, s // dil, h).transpose(0, 2, 1, 3).reshape(b, s, h)
    return o, lse


def _dilated_sample(q, kcat, vcat, n_past, dil, band):
    t = q.shape[1]
    idx = n_past + jnp.arange(t)[:, None] - dil * jnp.arange(band + 1)[None, :]
    valid = idx >= 0
    idx = jnp.maximum(idx, 0)
    kg = kcat[:, idx]
    vg = vcat[:, idx]
    s = jnp.einsum('bthd,btjhd->bthj', q, kg, preferred_element_type=jnp.float32) * (q.shape[-1] ** -0.5)
    s = jnp.where(valid[None, :, None, :], s, NEG)
    m = jnp.max(s, axis=-1)
    p = jnp.exp(s - m[..., None])
    den = jnp.sum(p, axis=-1)
    o = jnp.einsum('bthj,btjhd->bthd', p, vg.astype(jnp.float32)) / den[..., None]
    return o, m + jnp.log(den)


def _merge_groups(outs, lses):
    o = jnp.stack(outs, axis=0)
    alpha = jax.nn.softmax(jnp.stack(lses, axis=0), axis=0)
    a = jnp.sum(alpha[..., None] * o, axis=0)
    b, t = a.shape[:2]
    return a.reshape(b, t, GROUP_WIDTH)


def _causal_conv(u_ext, w):
    t = u_ext.shape[1] - (CONV_WIDTH - 1)
    y = w[0] * u_ext[:, 0:t]
    for j in range(1, CONV_WIDTH):
        y = y + w[j] * u_ext[:, j:j + t]
    return y


def _block_output(x, attn, conv_v, c_b, g_a, g_c, w_attn_o, w_conv_o, w_o,
                  w_ff1, w_ff2, g_mix_post, g_ffn_pre, g_ffn_post):
    a = attn.astype(x.dtype) @ w_attn_o
    c = (c_b * conv_v) @ w_conv_o
    mix = (jax.nn.sigmoid(g_a) * a + jax.nn.sigmoid(g_c) * c) @ w_o
    x = x + _rms_norm(mix, g_mix_post)
    h = _rms_norm(x, g_ffn_pre)
    f = jnp.square(jax.nn.relu(h @ w_ff1)) @ w_ff2
    return x + _rms_norm(f, g_ffn_post)


def setup_inputs(seed: int = 0) -> dict:
    key = jax.random.key(seed)
    ks = jax.random.split(key, 20)
    f32 = jnp.float32
    nrm = lambda k, shape, scale: jax.random.normal(k, shape, f32) * scale
    cache_shape = lambda w: (DEPTH, DEC_BATCH, min(w, PAST_LEN), 2, HEADS_PER_GROUP, HEAD_DIM)
    return {
        "x_prompt": nrm(ks[0], (BATCH, SEQ, D_MODEL), 1.0),
        "x_sample": nrm(ks[1], (DEC_BATCH, DEC_SEQ, D_MODEL), 1.0),
        "cache_kv_w128": nrm(ks[2], cache_shape(WINDOWS[0]), 1.0),
        "cache_kv_w512": nrm(ks[3], cache_shape(WINDOWS[1]), 1.0),
        "cache_kv_w2048": nrm(ks[4], cache_shape(WINDOWS[2]), 1.0),
        "state_conv": nrm(ks[5], (DEPTH, DEC_BATCH, CONV_WIDTH - 1, D_CONV), 1.0),
        "w_in": nrm(ks[6], (DEPTH, D_MODEL, IN_WIDTH), D_MODEL ** -0.5),
        "conv_w": nrm(ks[7], (DEPTH, CONV_WIDTH, D_CONV), CONV_WIDTH ** -0.5),
        "w_attn_o": nrm(ks[8], (DEPTH, GROUP_WIDTH, D_MODEL), GROUP_WIDTH ** -0.5),
        "w_conv_o": nrm(ks[9], (DEPTH, D_CONV, D_MODEL), D_CONV ** -0.5),
        "w_o": nrm(ks[10], (DEPTH, D_MODEL, D_MODEL), D_MODEL ** -0.5),
        "w_ff1": nrm(ks[11], (DEPTH, D_MODEL, D_FF), D_MODEL ** -0.5),
        "w_ff2": nrm(ks[12], (DEPTH, D_FF, D_MODEL), D_FF ** -0.5),
        "g_mix_pre": 1.0 + nrm(ks[13], (DEPTH, D_MODEL), 0.02),
        "g_mix_post": 1.0 + nrm(ks[14], (DEPTH, D_MODEL), 0.02),
        "g_ffn_pre": 1.0 + nrm(ks[15], (DEPTH, D_MODEL), 0.02),
        "g_ffn_post": 1.0 + nrm(ks[16], (DEPTH, D_MODEL), 0.02),
    }


def reference(x_prompt, x_sample, cache_kv_w128, cache_kv_w512, cache_kv_w2048, state_conv,
              w_in, conv_w, w_attn_o, w_conv_o, w_o, w_ff1, w_ff2,
              g_mix_pre, g_mix_post, g_ffn_pre, g_ffn_post):
    caches = (cache_kv_w128, cache_kv_w512, cache_kv_w2048)
    yp, ys = x_prompt, x_sample
    kv_p = [[] for _ in range(N_GROUPS)]
    kv_s = [[] for _ in range(N_GROUPS)]
    conv_p, conv_s = [], []
    for l in range(DEPTH):
        h = _rms_norm(yp, g_mix_pre[l])
        q, k, v, c_b, u, g_a, g_c = _project(h, w_in[l])
        outs, lses = [], []
        for g in range(N_GROUPS):
            o, lse = _dilated_prompt(q[:, :, g], k[:, :, g], v[:, :, g],
                                     DILATIONS[g], WINDOWS[g] // DILATIONS[g])
            outs.append(o)
            lses.append(lse)
            kv = jnp.stack([k[:, :, g], v[:, :, g]], axis=2)
            kv_p[g].append(kv[:, -min(WINDOWS[g], kv.shape[1]):])
        u_ext = jnp.pad(u, ((0, 0), (CONV_WIDTH - 1, 0), (0, 0)))
        conv_v = _causal_conv(u_ext, conv_w[l])
        conv_p.append(u_ext[:, -(CONV_WIDTH - 1):])
        yp = _block_output(yp, _merge_groups(outs, lses), conv_v, c_b, g_a, g_c,
                           w_attn_o[l], w_conv_o[l], w_o[l], w_ff1[l], w_ff2[l],
                           g_mix_post[l], g_ffn_pre[l], g_ffn_post[l])

        h = _rms_norm(ys, g_mix_pre[l])
        q, k, v, c_b, u, g_a, g_c = _project(h, w_in[l])
        outs, lses = [], []
        for g in range(N_GROUPS):
            buf = caches[g][l]
            n_past = buf.shape[1]
            cat = jnp.concatenate([buf, jnp.stack([k[:, :, g], v[:, :, g]], axis=2).astype(buf.dtype)], axis=1)
            o, lse = _dilated_sample(q[:, :, g], cat[:, :, 0], cat[:, :, 1], n_past,
                                     DILATIONS[g], WINDOWS[g] // DILATIONS[g])
            outs.append(o)
            lses.append(lse)
            kv_s[g].append(cat[:, -n_past:])
        u_ext = jnp.concatenate([state_conv[l].astype(u.dtype), u], axis=1)
        conv_v = _causal_conv(u_ext, conv_w[l])
        conv_s.append(u_ext[:, -(CONV_WIDTH - 1):])
        ys = _block_output(ys, _merge_groups(outs, lses), conv_v, c_b, g_a, g_c,
                           w_attn_o[l], w_conv_o[l], w_o[l], w_ff1[l], w_ff2[l],
                           g_mix_post[l], g_ffn_pre[l], g_ffn_post[l])

    kv_w128_prompt = jnp.stack(kv_p[0], axis=0)
    kv_w512_prompt = jnp.stack(kv_p[1], axis=0)
    kv_w2048_prompt = jnp.stack(kv_p[2], axis=0)
    conv_prompt = jnp.stack(conv_p, axis=0)
    kv_w128_sample = jnp.stack(kv_s[0], axis=0)
    kv_w512_sample = jnp.stack(kv_s[1], axis=0)
    kv_w2048_sample = jnp.stack(kv_s[2], axis=0)
    conv_sample = jnp.stack(conv_s, axis=0)
    return (yp, ys, kv_w128_prompt, kv_w512_prompt, kv_w2048_prompt, conv_prompt,
            kv_w128_sample, kv_w512_sample, kv_w2048_sample, conv_sample)
```

```python
import contextlib
import os
import types
import numpy as np
import ml_dtypes
import concourse.bass as bass
import concourse.mybir as mybir
from concourse.bass_utils import run_bass_kernel_spmd

F32 = mybir.dt.float32
BF16 = mybir.dt.bfloat16
AF = mybir.ActivationFunctionType
ALU = mybir.AluOpType
AP = bass.AP

NT = 2048
NH = 2048
NS = 4
HW = NH + NT + NS
D = 1024
RS = (1, 4, 16)
NQ = (16, 4, 1)
WIN = (128, 512, 2048)


class _Op:
    __slots__ = ("eng", "fn", "reads", "writes", "idx", "dma", "deps", "sig", "sigval",
                 "dsem", "dval", "k")

    def __init__(self, eng, fn, reads, writes, dma):
        self.eng = eng
        self.fn = fn
        self.reads = reads
        self.writes = writes
        self.dma = dma
        self.deps = set()
        self.sig = False
        self.sigval = 0
        self.dsem = None
        self.dval = 0
        self.k = 0


class Sched:
    ENG = ("pe", "act", "dve", "pool", "sp")
    RING = int(os.environ.get("MK_RING", "12"))

    def __init__(self, nc):
        self.nc = nc
        self.ops = []
        self.res_w = {}
        self.res_r = {}
        self.pending_barrier = {}
        self._bar_start = 0

    def _add(self, eng, fn, reads, writes, dma):
        isb = lambda r: isinstance(r, tuple) and r[0] == "bank"
        excl = [r for r in tuple(reads) + tuple(writes) if isb(r)]
        reads = [r for r in reads if not isb(r)]
        writes = [r for r in writes if not isb(r)]
        if fn.__closure__:
            cells = []
            for c in fn.__closure__:
                try:
                    cells.append(types.CellType(c.cell_contents))
                except ValueError:
                    cells.append(c)
            fn = types.FunctionType(fn.__code__, fn.__globals__, fn.__name__, fn.__defaults__,
                                    tuple(cells))
        op = _Op(eng, fn, tuple(reads), tuple(writes), dma)
        op.idx = len(self.ops)
        deps = set()
        for r in op.reads:
            w = self.res_w.get(r)
            if w is not None:
                deps.add(w)
        for w_ in op.writes:
            w = self.res_w.get(w_)
            if w is not None:
                deps.add(w)
            for r in self.res_r.get(w_, ()):
                deps.add(r)
        for x in excl:
            w = self.res_w.get(x)
            if w is not None:
                deps.add(w)
        if eng in self.pending_barrier:
            deps |= self.pending_barrier.pop(eng)
        deps.discard(op.idx)
        op.deps = deps
        for r in op.reads:
            self.res_r.setdefault(r, []).append(op.idx)
        for w_ in op.writes:
            self.res_w[w_] = op.idx
            self.res_r[w_] = []
        for x in excl:
            self.res_w[x] = op.idx
        self.ops.append(op)
        return op

    enabled = True

    def op(self, eng, fn, reads=(), writes=()):
        if not self.enabled:
            return None
        return self._add(eng, fn, reads, writes, False)

    def dma(self, eng, fn, reads=(), writes=()):
        if not self.enabled:
            return None
        return self._add(eng, fn, reads, writes, True)

    def barrier_one(self, eng):
        last = {}
        dmas = set()
        for o in self.ops:
            if o.dma:
                dmas.add(o.idx)
            else:
                last[o.eng] = o.idx
        self.pending_barrier[eng] = set(last.values()) | dmas | self.pending_barrier.get(eng, set())

    def barrier(self):
        last = {}
        dmas = set()
        for o in self.ops[self._bar_start:]:
            if o.dma:
                dmas.add(o.idx)
            else:
                last[o.eng] = o.idx
        self._bar_start = len(self.ops)
        deps = set(last.values()) | dmas
        for e in self.ENG:
            self.pending_barrier[e] = set(deps) | self.pending_barrier.get(e, set())

    def build(self):
        nc = self.nc
        ops = self.ops
        for o in ops:
            nd = set()
            best = {}
            for d in o.deps:
                p = ops[d]
                if p.dma:
                    nd.add(d)
                    continue
                if p.eng == o.eng and not o.dma:
                    if p.eng == "pe":
                        continue
                    if not (set(p.writes) & (set(o.reads) | set(o.writes))):
                        continue
                if p.eng not in best or best[p.eng] < d:
                    best[p.eng] = d
            nd |= set(best.values())
            o.deps = nd
        for o in ops:
            for d in o.deps:
                ops[d].sig = True
        cnt = {e: 0 for e in self.ENG}
        dcnt = {e: 0 for e in self.ENG}
        for o in ops:
            if o.dma:
                o.k = dcnt[o.eng]
                dcnt[o.eng] += 1
            elif o.sig:
                cnt[o.eng] += 1
                o.sigval = cnt[o.eng]
        self.cnt = dict(cnt)
        self.dcnt = dict(dcnt)
        if os.environ.get("MK_VERBOSE"):
            print("SIGNALS", cnt, "DMAS", dcnt, "OPS", len(ops))
            import collections
            print("PERENG", dict(collections.Counter(o.eng for o in ops)))
        with contextlib.ExitStack() as st:
            esem = {e: st.enter_context(nc.semaphore("s_" + e)) for e in self.ENG}
            rings = {}
            for e in self.ENG:
                if dcnt[e]:
                    rings[e] = [st.enter_context(nc.semaphore("d_%s_%d" % (e, i)))
                                for i in range(min(self.RING, dcnt[e]))]
            for o in ops:
                if o.dma:
                    R = len(rings[o.eng])
                    o.dsem = rings[o.eng][o.k % R]
                    o.dval = 16 * (o.k // R + 1)
            block = st.enter_context(nc.Block())
            engobj = {"pe": "tensor", "act": "scalar", "dve": "vector", "pool": "gpsimd",
                      "sp": "sync"}
            ENG = self.ENG

            def make(ename):
                def body(eng):
                    waited = {}
                    dwaited = set()
                    for o in ops:
                        if o.eng != ename:
                            continue
                        if o.dma and o.dval > 16:
                            key = (id(o.dsem), o.dval - 16)
                            if key not in dwaited:
                                eng.wait_ge(o.dsem, o.dval - 16)
                                dwaited.add(key)
                        for d in sorted(o.deps):
                            p = ops[d]
                            if p.dma:
                                key = (id(p.dsem), p.dval)
                                if key in dwaited:
                                    continue
                                eng.wait_ge(p.dsem, p.dval)
                                dwaited.add(key)
                            else:
                                if waited.get(p.eng, 0) >= p.sigval:
                                    continue
                                eng.wait_ge(esem[p.eng], p.sigval)
                                waited[p.eng] = p.sigval
                        ins = o.fn(eng)
                        if o.dma:
                            ins.then_inc(o.dsem, 16)
                        elif o.sig:
                            ins.then_inc(esem[ename], 1)
                    if ename == "sp":
                        for e2 in ENG:
                            if dcnt[e2]:
                                R = len(rings[e2])
                                for i in range(R):
                                    n = (dcnt[e2] - 1 - i) // R + 1
                                    if n > 0:
                                        eng.wait_ge(rings[e2][i], 16 * n)
                            if cnt[e2] and e2 != "sp":
                                eng.wait_ge(esem[e2], cnt[e2])
                return body

            for e in ENG:
                getattr(block, engobj[e])(make(e))


NCF = 256
NCB = 1536
CF_G1 = 0
CF_G3 = 8
CF_CW = 16
CF_ST = 40
CF_HV = 104
CF_OH = 105
CF_EPS = 109
CF_BM = 112
CB_MASK = 0
CB_ID = 256
CB_ONE = 384
CB_ZERO = 512
CB_BLK = 1024
CB_END = 1536


def build_program():
    nc = bass.Bass("TRN2", target_bir_lowering=False)
    S = Sched(nc)
    STOP = os.environ.get("MK_STOP", "")
    ONLY = os.environ.get("MK_ONLY", "")
    if ONLY:
        S.enabled = False
    BSTOP = os.environ.get("MK_BSTOP", "")
    ESTOP = os.environ.get("MK_ESTOP", "")

    def fin():
        for i in range(int(os.environ.get("MK_DUMMY", "0"))):
            S.op("pe", lambda e: e.matmul(pb(7, 0, [[1, 8]]), lhsT=cBa(CB_ZERO, 128), rhs=cBa(CB_ZERO, 8),
                                          start=True, stop=True), reads=["cB"], writes=[("bank", 7)])
        S.build()
        return nc

    def din(name, shape, dt=F32):
        return nc.dram_tensor(name, list(shape), dt, kind="ExternalInput")

    def dout(name, shape, dt=F32):
        return nc.dram_tensor(name, list(shape), dt, kind="ExternalOutput")

    x_all = din("x_all", [NH + NT, D])
    xs_d = din("xs", [NS, D])
    c128_d = din("c128", [NS * 128, 1024])
    c512_d = din("c512", [NS * 512, 1024])
    c2048_d = din("c2048", [NS * 2048, 1024])
    cache_d = (c128_d, c512_d, c2048_d)
    st1_d = din("st1", [NS, D])
    wB_d = din("wB", [4, 128, 9216])
    wC1_d = din("wC1", [8, 128, 3072])
    wC2_d = din("wC2", [8, 128, 3584])
    wO_d = din("wO", [128, 8192])
    wF_d = din("wF", [8, 128, 8192])
    g1_d = din("g1", [1, D])
    g3_d = din("g3", [1, D])
    g2_d = din("g2", [1, D])
    g4_d = din("g4", [1, D])
    cF_d = din("constF", [128, NCF])
    cB_d = din("constB", [128, NCB], BF16)

    y_d = dout("y", [NT, D])
    ys_d = dout("ys", [NS, D])
    kvp_d = (dout("kvp128", [128, 1024]), dout("kvp512", [512, 1024]), dout("kvp2048", [2048, 1024]))
    convp_d = dout("convp", [128, 16])
    kvs_d = (dout("kvs128", [NS * 128, 1024]), dout("kvs512", [NS * 512, 1024]),
             dout("kvs2048", [NS * 2048, 1024]))
    convs0_d = dout("convs0", [NS, D])
    convs1_d = dout("convs1", [128, 32])

    def sbt(name, cols, dt, off):
        return nc.alloc_sbuf_tensor_at(name, [128, cols], dt, offset=off + 16384)

    R0 = 0
    R1 = 65600
    R2 = 171200
    R3 = 187616
    HT_F = 8 * HW
    hT = sbt("hT", HT_F, BF16, R0)
    facc = sbt("facc", 16 * 1024, F32, R0)
    o = R1
    QF = 3 * 2048
    qT = sbt("qT", QF, BF16, o); o += QF * 2
    KOFF = (0, 2176, 4736)
    LK = (2176, 640, 256)
    KF = 8832
    kT = sbt("kT", KF, BF16, o); o += KF * 2
    NBLK = 69
    VF = NBLK * 192
    Vg = sbt("Vaug", VF, BF16, o); o += VF * 2
    PT2F = 32 * 128
    PT2 = sbt("PT2", PT2F, BF16, o); o += PT2F * 2
    PT0F = 10 * 256
    PT0 = sbt("PT0", PT0F, BF16, o); o += PT0F * 2
    PT1F = 4 * 3 * 256
    PT1 = sbt("PT1", PT1F, BF16, o); o += PT1F * 2
    WBF = 9216
    wB = sbt("wB", WBF, BF16, o); o += WBF * 2
    vT = sbt("vT", 4096, BF16, o); o += 8192
    kvst_extra = [sbt("kvstx%d" % i, 256, F32, o + i * 1024) for i in range(2)]; o += 2048
    assert o <= R2, o
    CF_ = 8 * 2052
    cbvT = sbt("cbvT", CF_, BF16, R1)
    mixT = sbt("mixT", CF_, BF16, R1 + 32832)
    h2T = sbt("h2T", CF_, BF16, R1 + 65664)
    R1E = R1 + 98496
    KVs2 = [sbt("KVs%d" % i, 1024, F32, R1 + i * 4096) for i in range(2)]
    KVb2 = [sbt("KVb%d" % i, 1024, BF16, R1 + 8192 + i * 2048) for i in range(2)]
    KTs2 = [sbt("KTs%d" % i, 512, BF16, R1 + 12288 + i * 1024) for i in range(2)]
    Qbd = sbt("Qbd", 4 * 3 * 4 * 8, BF16, R1 + 14336)
    PTs2 = [sbt("PTs%d" % i, 8, BF16, R1 + 15104 + i * 32) for i in range(2)]
    PTn2 = [sbt("PTn%d" % i, 8, F32, R1 + 15168 + i * 32) for i in range(2)]
    PTnm2 = [sbt("PTnm%d" % i, 8, BF16, R1 + 15232 + i * 32) for i in range(2)]
    rdn = sbt("rdn", 1, F32, R1 + 15296)
    Om = sbt("Om", 512, BF16, R1 + 15328)
    o = R1 + 32832
    wC1 = [sbt("wC1_%d" % i, 3072, BF16, o + i * 6144) for i in range(2)]; o += 12288
    UF = 2 + 2048 + 4
    ub = [sbt("ub%d" % i, UF, F32, o + i * 8224) for i in range(2)]; o += 16448
    ccs = [sbt("ccs%d" % i, 512, F32, o + i * 2048) for i in range(2)]; o += 4096
    t1b = [sbt("t1b%d" % i, 512, F32, o + i * 2048) for i in range(2)]; o += 4096
    assert o <= R2
    o = R1 + 69888
    wC2 = [sbt("wC2_%d" % i, 3584, BF16, o + i * 7168) for i in range(2)]; o += 14336
    sab = [sbt("sa%d" % i, 512, F32, o + i * 2048) for i in range(2)]; o += 4096
    scb = [sbt("sc%d" % i, 512, F32, o + i * 2048) for i in range(2)]; o += 4096
    tab = [sbt("ta%d" % i, 512, F32, o + i * 2048) for i in range(2)]; o += 4096
    tcb = [sbt("tc%d" % i, 512, F32, o + i * 2048) for i in range(2)]; o += 4096
    assert o <= R2
    o = R0
    gB2 = sbt("gB2", 1024, F32, o); o += 4096
    xtD = [sbt("xtD%d" % i, 1024, F32, o + i * 4096) for i in range(4)]; o += 16384
    tDb = [sbt("tD%d" % i, 1024, F32, o + i * 4096) for i in range(3)]; o += 12288
    x1t = [sbt("x1t%d" % i, 1024, F32, o + i * 4096) for i in range(4)]; o += 16384
    xn2 = [sbt("xn2_%d" % i, 1024, BF16, o + i * 2048) for i in range(4)]; o += 8192
    junkD = sbt("junkD", 1024, BF16, o); o += 2048
    gB3 = sbt("gB3", 1024, F32, o); o += 4096
    assert o <= R1
    o = R1
    wF = [sbt("wF0", 8192, BF16, R2), sbt("wF1", 8192, BF16, o)]; o += 16384
    f1T = sbt("f1T", 4 * 2052, BF16, o); o += 16416
    gB4 = sbt("gB4", 1024, F32, o); o += 4096
    xrE = [sbt("xrE%d" % i, 1024, F32, o + i * 4096) for i in range(2)]; o += 8192
    tE = sbt("tE", 1024, F32, o); o += 4096
    ytE = [sbt("ytE%d" % i, 1024, F32, o + i * 4096) for i in range(2)]; o += 8192
    faccs = sbt("faccs", 1024, F32, o); o += 4096
    junkE = sbt("junkE", 1024, BF16, o); o += 2048
    assert o <= R1 + 65664
    rlb = [sbt("rl%d" % i, 512, F32, R1E + i * 2048) for i in range(2)]
    assert R1E + 4096 <= R2
    AT_F = 4 * 2052
    attnT = sbt("attnT", AT_F, BF16, R2)
    o = R3
    cF = sbt("cF", NCF, F32, o); o += NCF * 4
    cB = sbt("cB", NCB, BF16, o); o += NCB * 2
    stat2 = sbt("stat2", 256, F32, o); o += 1024
    qkvs = sbt("qkvs", 4 * 6 * 4, BF16, o); o += 192
    vnb = sbt("vnb", 3 * 512, BF16, o); o += 3072
    convp_sb = sbt("convp_sb", 16, F32, o); o += 64
    convs_sb = sbt("convs_sb", 32, F32, o); o += 128
    o3 = o
    RA = R1 + 12288 + 17664 + 26496
    xt = [sbt("xt%d" % i, 1024, F32, R2 + i * 4096) for i in range(4)] + [sbt("xt4", 1024, F32, RA + 12288)]
    xn = [sbt("xn%d" % i, 1024, BF16, RA + i * 2048) for i in range(3)]
    junk = sbt("junk", 1024, BF16, RA + 6144)
    gB1 = sbt("gB1", 1024, F32, RA + 8192)
    assert 16384 <= 19456
    kvst = [sbt("kvst%d" % i, 256, F32, o3 + i * 1024) for i in range(2)] + kvst_extra
    dnb = sbt("dnb", 512, F32, o3 + 2048)
    hib = sbt("hib", 512, BF16, o3 + 4096)
    lob = sbt("lob", 512, BF16, o3 + 5120)
    bcs = sbt("bcs", 512, F32, o3 + 6144)
    maskh = sbt("maskh", 128, BF16, o3 + 8192)
    kvn_st = sbt("kvn_st", 768, F32, o3 + 8448)
    vTb = sbt("vTb", 2560, BF16, o3 + 11520)
    assert o3 + 11520 + 5120 <= 212992
    assert 8448 + 3072 <= 12288
    wo = sbt("wo", 8192, BF16, o3)
    assert o3 + 16384 <= 212992, o3

    PS = [nc.alloc_psum_tensor("ps%d" % i, [128, 1024], F32) for i in range(4)]
    PSB = [p.bitcast(BF16) for p in PS]

    def pb(bank, col, dims, p0=0, np_=128):
        return AP(PS[bank // 2], p0 * 1024 + (bank % 2) * 512 + col, [[1024, np_]] + dims)

    def pbb(bank, col, dims, p0=0, np_=128):
        return AP(PSB[bank // 2], p0 * 2048 + (bank % 2) * 1024 + col, [[2048, np_]] + dims)

    def sb(t, col, dims, p0=0, np_=128):
        F = t.shape[1]
        return AP(t, p0 * F + col, [[F, np_]] + dims)

    def hTa(kc, col, dims, np_=128):
        return sb(hT, kc * HW + col, dims, 0, np_)

    def cFa(col, n=1, p0=0, np_=128):
        return sb(cF, col, [[1, n]], p0, np_)

    def cBa(col, n, p0=0, np_=128):
        return sb(cB, col, [[1, n]], p0, np_)

    ident = lambda n: sb(cB, CB_ID, [[1, n]], 0, n)
    eps_ap = lambda n: cFa(CF_EPS, 1, 0, n)

    def hT_tiles(col0, n, stride=1):
        lo = col0 // 128
        hi = (col0 + (n - 1) * stride) // 128
        return [("hT", t) for t in range(lo, hi + 1)]

    def load_wB(hp):
        S.dma("pool", lambda e, hp=hp: e.dma_start(
            out=sb(wB, 0, [[1536, 6], [1, 1536]]),
            in_=AP(wB_d, hp * 128 * 9216, [[9216, 128], [1536, 6], [1, 1536]])),
            writes=["wB"])

    load_wB(0)
    S.dma("sp", lambda e: e.dma_start(out=cF[:], in_=cF_d.ap()), writes=["cF"])
    S.dma("sp", lambda e: e.dma_start(out=cB[:], in_=cB_d.ap()), writes=["cB"])
    S.dma("sp", lambda e: e.dma_start(out=convs0_d.ap(), in_=st1_d.ap()))

    cur_junk = [junk]

    def norm_p1(src_ap_fn, src_res, np_, xnb, xn_res, gB, gB_res, statcol):
        jk = cur_junk[0]
        ss = sb(stat2, statcol, [[1, 1]], 0, np_)
        sd = sb(stat2, statcol + 1, [[1, 1]], 0, np_)
        rs = sb(stat2, statcol + 2, [[1, 1]], 0, np_)
        kss = ("st", statcol)
        S.op("act", lambda e: e.activation(out=sb(jk, 0, [[1, 1024]], 0, np_), in_=src_ap_fn(),
                                           func=AF.Square, accum_out=ss),
             reads=[src_res], writes=[kss, "junk"])
        S.op("act", lambda e: e.activation(out=sd, in_=ss, func=AF.Sqrt, bias=eps_ap(np_),
                                           scale=1.0 / D),
             reads=[kss, "cF"], writes=[("sd", statcol)])
        S.op("dve", lambda e: e.reciprocal(out=rs, in_=sd), reads=[("sd", statcol)],
             writes=[("rs", statcol)])
        S.op("dve", lambda e: e.scalar_tensor_tensor(out=sb(xnb, 0, [[1, 1024]], 0, np_), in0=src_ap_fn(),
                                                     scalar=rs, in1=sb(gB, 0, [[1, 1024]], 0, np_),
                                                     op0=ALU.mult, op1=ALU.mult),
             reads=[src_res, ("rs", statcol), gB_res], writes=[xn_res])

    D_P2_COUNT = [0]

    def norm_p2(np_, xnb, xn_res, tpbank, dst_fn, dst_res, idx):
        if dst_res[0] == "h2T":
            D_P2_COUNT[0] += 1
        for c in range(8):
            S.op("pe", lambda e, c=c: e.transpose(out=pbb(tpbank, c * 128, [[1, np_]]),
                                                  in_=sb(xnb, c * 128, [[1, 128]], 0, np_),
                                                  identity=ident(np_)),
                 reads=[xn_res, "cB"], writes=[("bank", tpbank)])
        if idx % 2 == 0:
            S.op("act", lambda e: e.activation(out=dst_fn(), in_=pbb(tpbank, 0, [[128, 8], [1, np_]]),
                                               func=AF.Copy),
                 reads=[("bank", tpbank)], writes=[dst_res])
        else:
            S.op("dve", lambda e: e.tensor_copy(out=dst_fn(), in_=pbb(tpbank, 0, [[128, 8], [1, np_]])),
                 reads=[("bank", tpbank)], writes=[dst_res])

    S.dma("sp", lambda e: e.dma_start(out=gB1[:], in_=AP(g1_d, 0, [[0, 128], [1, D]])), writes=["gB1"])
    a_done = []

    def emit_A_tile(ti):
        it_ = len(a_done)
        a_done.append(ti)
        np_ = 128 if ti < 32 else NS
        b = it_ % 3
        bx = it_ % 5
        if ti < 32:
            src = AP(x_all, ti * 128 * D, [[D, 128], [1, D]])
        else:
            src = xs_d.ap()
        S.dma("sp", lambda e: e.dma_start(out=sb(xt[bx], 0, [[1, 1024]], 0, np_), in_=src),
              writes=[("xt", bx)])
        norm_p1(lambda: sb(xt[bx], 0, [[1, 1024]], 0, np_), ("xt", bx), np_,
                xn[b], ("xn", b), gB1, "gB1", (it_ % 8) * 4)
        norm_p2(np_, xn[b], ("xn", b), 6 + it_ % 2,
                lambda: hTa(0, ti * 128, [[HW, 8], [1, np_]]), ("hT", ti), it_)

    A_ORDER = list(range(16, 32)) + [15, 12, 13, 14] + list(range(12)) + [32]
    A_LOOK = int(os.environ.get("MK_ALOOK", "6"))

    def need_tiles(res_list):
        for r in res_list:
            if isinstance(r, tuple) and r[0] == "hT":
                pos = A_ORDER.index(r[1])
                for t in A_ORDER[:min(len(A_ORDER), pos + 1 + A_LOOK)]:
                    if t not in a_done:
                        emit_A_tile(t)

    if STOP == "A" or ONLY:
        for ti in range(33):
            need_tiles([("hT", ti)])
    if STOP == "A":
        return fin()
    if ONLY == "B":
        S.enabled = True
    S.op("dve", lambda e: e.tensor_scalar(out=maskh[:], in0=cBa(CB_MASK + 128, 128),
                                          scalar1=cFa(CF_HV), scalar2=None, op0=ALU.mult),
         reads=["cF", "cB"], writes=["maskh"])

    S.op("pool", lambda e: e.memset(sb(Vg, 64, [[192, NBLK], [1, 64]]), 1.0), writes=["Vones"])

    blocks = []
    vi = 0
    for g in range(3):
        for rho in range(RS[g]):
            for kb in range(NQ[g] + 1):
                blocks.append((g, rho, kb, vi))
                vi += 1
    assert vi == NBLK
    VI = {(g, rho, kb): v for (g, rho, kb, v) in blocks}
    all_own = [("hT", t) for t in range(16, 32)]
    sc_slot = [0]
    proj_bank = [0]
    kv_bank = [0]

    def evac(i, fn_out, fn_in, reads, writes):
        if i % 2 == 0:
            S.op("act", lambda e: e.activation(out=fn_out(), in_=fn_in(), func=AF.Copy),
                 reads=reads, writes=writes)
        else:
            S.op("dve", lambda e: e.tensor_copy(out=fn_out(), in_=fn_in()), reads=reads, writes=writes)

    def issue_cache_copies():
        for g in range(3):
            W = WIN[g]
            for b in range(NS):
                if os.environ.get("MK_NOCOPY"):
                    continue
                n = (W - 1) * 64
                S.dma("act", lambda e, g=g, b=b, W=W, n=n: e.dma_start(
                    out=AP(kvs_d[g], b * W * 1024, [[n, 16], [1, n]]),
                    in_=AP(cache_d[g], b * W * 1024 + 1024, [[n, 16], [1, n]])),
                    writes=[("kvs", g, b)])

    ring_owner = {}
    kvst_i = [0]
    ev_i = [0]
    BSKIP = os.environ.get("MK_BSKIP", "")
    for hp in range(int(os.environ.get("MK_HPS", "4"))):
        def proj(blkcol, hcol, n, out_fn, in_dims, reads, writes):
            need_tiles(reads)
            bank = proj_bank[0] % 2
            proj_bank[0] += 1
            for kc in range(8):
                S.op("pe", lambda e, kc=kc: e.matmul(pb(bank, 0, [[1, n]]),
                                                     lhsT=sb(wB, kc * 1152 + blkcol, [[1, 128]]),
                                                     rhs=hTa(kc, hcol, [[1, n]]),
                                                     start=(kc == 0), stop=(kc == 7)),
                     reads=["wB"] + reads, writes=[("bank", bank)])
            ev_i[0] += 1
            evac(ev_i[0], out_fn, lambda: pb(bank, 0, in_dims), [("bank", bank)], writes)

        for w in range(4):
            hres = hT_tiles(NH + 512 * w, 512)
            for g in range(3):
                r = RS[g]
                proj(g * 128, NH + 512 * w, 512,
                     lambda g=g, r=r, w=w: sb(qT, g * 2048 + 512 * w // r, [[2048 // r, r], [1, 512 // r]]),
                     [[1, r], [r, 512 // r]], hres, [("qT", g, w)])
                proj(384 + g * 128, NH + 512 * w, 512,
                     lambda g=g, r=r, w=w: sb(kT, KOFF[g] + 128 + 512 * w // r, [[LK[g], r], [1, 512 // r]]),
                     [[1, r], [r, 512 // r]], hres, [("kT", g, w)])

        def v_list(g):
            r = RS[g]
            vTg = vTb if g == 1 else vT
            vid = 1 if g == 1 else 0
            lst = []
            for w in range(4):
                lst.append(lambda w=w: proj(
                    768 + g * 128, NH + 512 * w, 512,
                    lambda: sb(vTg, 128 + 512 * w // r, [[LK[g], r], [1, 512 // r]]),
                    [[1, r], [r, 512 // r]], hT_tiles(NH + 512 * w, 512), [("vT", vid, w)]))
            if g == 0:
                lst.append(lambda: proj(768, NH - 128, 128, lambda: sb(vT, 0, [[1, 128]]), [[1, 128]],
                                        hT_tiles(NH - 128, 128), [("vTh", vid, 0)]))
            elif g == 1:
                lst.append(lambda: proj(768 + 128, NH - 512, 512, lambda: sb(vTb, 0, [[640, 4], [1, 128]]),
                                        [[1, 4], [4, 128]], hT_tiles(NH - 512, 512), [("vTh", vid, 0)]))
            else:
                for hw in range(4):
                    lst.append(lambda hw=hw: proj(
                        768 + 256, 512 * hw, 512, lambda: sb(vT, 32 * hw, [[256, 16], [1, 32]]),
                        [[1, 16], [16, 32]], hT_tiles(512 * hw, 512), [("vTh", vid, hw)]))
            return lst

        kh_list = [
            lambda: proj(384, NH - 128, 128, lambda: sb(kT, KOFF[0], [[1, 128]]), [[1, 128]],
                         hT_tiles(NH - 128, 128), [("kTh", 0, 0)]),
            lambda: proj(384 + 128, NH - 512, 512, lambda: sb(kT, KOFF[1], [[640, 4], [1, 128]]),
                         [[1, 4], [4, 128]], hT_tiles(NH - 512, 512), [("kTh", 1, 0)])]
        for hw in range(4):
            kh_list.append(lambda hw=hw: proj(
                384 + 256, 512 * hw, 512, lambda: sb(kT, KOFF[2] + 32 * hw, [[256, 16], [1, 32]]),
                [[1, 16], [16, 32]], hT_tiles(512 * hw, 512), [("kTh", 2, hw)]))

        def tblock(g, rho, kb, v):
            r = RS[g]
            vTg = vTb if g == 1 else vT
            vid = 1 if g == 1 else 0
            vres = [("vT", vid, w) for w in range(4)] + [("vTh", vid, hw) for hw in range(4)]
            kres_ = [("kT", g, w) for w in range(4)]
            needk = (kb == NQ[g])
            bank = 2 + kv_bank[0] % 4
            kv_bank[0] += 1
            ccol = rho * LK[g] + 128 * kb
            vcol = 128 if needk else 0
            if needk:
                S.op("pe", lambda e: e.transpose(
                    out=pbb(bank, 0, [[1, 128]]), in_=sb(kT, KOFF[g] + ccol, [[1, 128]]), identity=ident(128)),
                    reads=kres_ + ["cB"], writes=[("bank", bank)])
            S.op("pe", lambda e: e.transpose(
                out=pbb(bank, vcol, [[1, 128]]), in_=sb(vTg, ccol, [[1, 128]]), identity=ident(128)),
                reads=vres + ["cB"], writes=[("bank", bank)])
            S.op("dve", lambda e: e.tensor_copy(
                out=sb(Vg, v * 192, [[128, 2], [1, 64]]), in_=pbb(bank, vcol, [[64, 2], [1, 64]])),
                reads=[("bank", bank), "Vones"], writes=[("V", v)])
            if needk:
                sbuf_i = kvst_i[0] % 4
                kvst_i[0] += 1
                S.op("act", lambda e: e.activation(
                    out=kvst[sbuf_i][:], in_=pbb(bank, 0, [[1, 256]]), func=AF.Copy),
                    reads=[("bank", bank)], writes=[("kvst", sbuf_i)])
                S.dma("sp", lambda e: e.dma_start(
                    out=AP(kvp_d[g], rho * 1024 + hp * 128, [[r * 1024, 128], [512, 2], [1, 128]]),
                    in_=sb(kvst[sbuf_i], 0, [[128, 2], [1, 128]])),
                    reads=[("kvst", sbuf_i)])

        def interleave(plist, blist):
            per = -(-len(blist) // max(1, len(plist)))
            for i, pj in enumerate(plist):
                pj()
                for bl in blist[i * per:(i + 1) * per]:
                    tblock(*bl)
            for bl in blist[len(plist) * per:]:
                tblock(*bl)

        for pj in v_list(0):
            pj()
        interleave(v_list(1), [bl for bl in blocks if bl[0] == 0])
        interleave(v_list(2), [bl for bl in blocks if bl[0] == 1])
        interleave(kh_list, [bl for bl in blocks if bl[0] == 2])
        if BSTOP == "proj":
            return fin()
        need_tiles([("hT", 32)])
        bank = proj_bank[0] % 2
        proj_bank[0] += 1
        for blk in range(6):
            for kc in range(8):
                S.op("pe", lambda e, kc=kc, blk=blk: e.matmul(
                    pb(bank, blk * 4, [[1, 4]]), lhsT=sb(wB, kc * 1152 + blk * 128, [[1, 128]]),
                    rhs=hTa(kc, NH + NT, [[1, 4]]), start=(kc == 0), stop=(kc == 7)),
                    reads=["wB", ("hT", 32)], writes=[("bank", bank)])
        S.op("act", lambda e, hp=hp, bank=bank: e.activation(out=sb(qkvs, hp * 24, [[1, 24]]),
                                                          in_=pb(bank, 0, [[1, 24]]), func=AF.Copy),
             reads=[("bank", bank)], writes=[("qkvs", hp)])
        bankK = 2 + kv_bank[0] % 2
        kv_bank[0] += 1
        bankV = 2 + kv_bank[0] % 2
        kv_bank[0] += 1
        for (bk, c0) in ((bankK, 384), (bankV, 768)):
            for kc in range(8):
                S.op("pe", lambda e, kc=kc, bk=bk, c0=c0: e.matmul(
                    pb(bk, 0, [[1, 384]], 0, NS), lhsT=hTa(kc, NH + NT, [[1, 4]]),
                    rhs=sb(wB, kc * 1152 + c0, [[1, 384]]), start=(kc == 0), stop=(kc == 7)),
                    reads=["wB", ("hT", 32)], writes=[("bank", bk)])
        S.op("act", lambda e, bankK=bankK: e.activation(out=sb(kvn_st, 0, [[1, 384]], 0, NS),
                                                        in_=pb(bankK, 0, [[1, 384]], 0, NS), func=AF.Copy),
             reads=[("bank", bankK)], writes=["kvn_k"])
        S.op("dve", lambda e, bankV=bankV: e.tensor_copy(out=sb(kvn_st, 384, [[1, 384]], 0, NS),
                                                         in_=pb(bankV, 0, [[1, 384]], 0, NS)),
             reads=[("bank", bankV)], writes=["kvn_v"])
        S.op("dve", lambda e, bankV=bankV, hp=hp: e.tensor_copy(
            out=sb(vnb, hp * 128, [[512, 3], [1, 128]], 0, NS), in_=pb(bankV, 0, [[128, 3], [1, 128]], 0, NS)),
            reads=[("bank", bankV)], writes=[("vnb", hp)])
        for g in range(3):
            W = WIN[g]
            S.dma("sp", lambda e, g=g, W=W, hp=hp: e.dma_start(
                out=AP(kvs_d[g], (W - 1) * 1024 + hp * 128, [[W * 1024, NS], [512, 2], [1, 128]]),
                in_=sb(kvn_st, g * 128, [[384, 2], [1, 128]], 0, NS)),
                reads=["kvn_k", "kvn_v"])


        if BSTOP == "smp":
            return fin()
        if hp == 0:
            assert len(a_done) == 33, a_done
            S.barrier()
            issue_cache_copies()
        if hp + 1 < 4:
            load_wB(hp + 1)
        def score(g, rho, kb, X):
            r = RS[g]
            nq = NQ[g]
            slot = sc_slot[0] % 5
            sc_slot[0] += 1
            bank = slot
            pcol = 0
            kcol = KOFF[g] + rho * LK[g] + 128 * kb
            if kb == 0:
                qcol, n, half0 = 0, 128, 1
            elif kb == nq:
                qcol, n, half0 = 128 * (nq - 1), 128, 0
            else:
                qcol, n, half0 = 128 * (kb - 1), 256, 0
            qcol += g * 2048 + rho * (2048 // r)
            qw = [("qT", g, w) for w in range(4)]
            kres = [("kT", g, w) for w in range(4)] + [("kTh", g, hw) for hw in range(4)]
            S.op("pe", lambda e: e.matmul(
                pb(bank, pcol, [[1, n]]), lhsT=sb(kT, kcol, [[1, 128]], 64 * X, 64),
                rhs=sb(qT, qcol, [[1, n]], 64 * X, 64), start=True, stop=True),
                reads=qw + kres, writes=[("bank", bank)])
            if g == 2:
                t, tcol = PT2, (rho * 2 + kb) * 128
                res = ("PT2", rho, kb)
            elif g == 1:
                t, tcol = PT1, (rho * 3 + kb % 3) * 256 + half0 * 128
                res = ("PT1", rho, kb % 3)
            else:
                t, tcol = PT0, (kb % 10) * 256 + half0 * 128
                res = ("PT0", kb % 10)
            ring_owner[res] = kb
            S.op("act", lambda e: e.activation(
                out=sb(t, tcol, [[1, n]]), in_=pb(bank, pcol, [[1, n]]), func=AF.Exp, scale=0.125),
                reads=[("bank", bank), "Vones"], writes=[res])
            if kb == 0:
                mk = lambda: maskh[:]
                mres = ["maskh"]
            else:
                mk = lambda: cBa(CB_MASK, n)
                mres = ["cB"]
            S.op("dve", lambda e: e.tensor_tensor(
                out=sb(t, tcol, [[1, n]]), in0=sb(t, tcol, [[1, n]]), in1=mk(), op=ALU.mult),
                reads=[res] + mres, writes=[res])

        def ptap(g, rho, kb, half, c0, n):
            if g == 2:
                return sb(PT2, (rho * 2 + kb) * 128 + c0, [[1, n]]), ("PT2", rho, kb)
            if g == 1:
                assert ring_owner[("PT1", rho, kb % 3)] == kb
                return (sb(PT1, (rho * 3 + kb % 3) * 256 + half * 128 + c0, [[1, n]]), ("PT1", rho, kb % 3))
            assert ring_owner[("PT0", kb % 10)] == kb
            return sb(PT0, (kb % 10) * 256 + half * 128 + c0, [[1, n]]), ("PT0", kb % 10)

        for X in range(0 if "sc" in BSKIP else 2):
            dp = 64 if X == 0 else 0
            u0 = 0 if X == 0 else 64

            def Sw(w, X=X):
                lo_kb = 0 if w == 0 else 4 * w + 1
                for kb in range(lo_kb, 4 * w + 5):
                    score(0, 0, kb, X)
                for rho in range(4):
                    if w == 0:
                        score(1, rho, 0, X)
                    score(1, rho, w + 1, X)

            def PVw(w, X=X):
                ab = 6 + w % 2
                ares = ("bank", ab)
                S.op("pe", lambda e: e.matmul(pb(ab, 0, [[1, 512]]), lhsT=cBa(CB_ZERO, 128),
                                              rhs=cBa(CB_ZERO, 512), start=True, stop=False),
                     reads=["cB"], writes=[ares])

                def pv(g, rho, kb, half, c0, n, out_ap, last=False):
                    rhs, res = ptap(g, rho, kb, half, c0, n)
                    v = VI[(g, rho, kb)]
                    S.op("pe", lambda e: e.matmul(out_ap, lhsT=sb(Vg, v * 192 + 64 * X, [[1, 128]]), rhs=rhs,
                                                  start=False, stop=last),
                         reads=[res, ("V", v), "Vones"], writes=[ares])

                for qb in range(4 * w, 4 * w + 4):
                    oc = 128 * (qb - 4 * w)
                    pv(0, 0, qb + 1, 0, 0, 128, pb(ab, oc, [[1, 128]]))
                    pv(0, 0, qb, 1, 0, 128, pb(ab, oc, [[1, 128]]))
                for rho in range(4):
                    pv(1, rho, w + 1, 0, 0, 128, pb(ab, rho, [[4, 128]]))
                    pv(1, rho, w, 1, 0, 128, pb(ab, rho, [[4, 128]]))
                for rho in range(16):
                    pv(2, rho, 1, 0, 32 * w, 32, pb(ab, rho, [[16, 32]]))
                    pv(2, rho, 0, 0, 32 * w, 32, pb(ab, rho, [[16, 32]]), last=(rho == 15))

            def N1(w):
                ab = 6 + w % 2
                ares = ("bank", ab)
                S.op("act", lambda e: e.activation(out=sb(dnb, 0, [[1, 512]], dp, 1),
                                                   in_=pb(ab, 0, [[1, 512]], dp, 1), func=AF.Ln),
                     reads=[ares], writes=["dn"])
                S.op("act", lambda e: e.activation(out=sb(dnb, 0, [[1, 512]], dp, 1),
                                                   in_=sb(dnb, 0, [[1, 512]], dp, 1), func=AF.Exp, scale=-1.0),
                     reads=["dn"], writes=["dn"])
                S.op("dve", lambda e: e.tensor_copy(out=sb(hib, 0, [[1, 512]], dp, 1),
                                                    in_=sb(dnb, 0, [[1, 512]], dp, 1)),
                     reads=["dn"], writes=["hi"])
                S.op("dve", lambda e: e.tensor_tensor(out=sb(lob, 0, [[1, 512]], dp, 1),
                                                      in0=sb(dnb, 0, [[1, 512]], dp, 1),
                                                      in1=sb(hib, 0, [[1, 512]], dp, 1), op=ALU.subtract),
                     reads=["dn", "hi"], writes=["lo"])

            def N2(w, X=X):
                ab = 6 + w % 2
                ares = ("bank", ab)
                bb = 5
                S.op("pe", lambda e: e.matmul(pb(bb, 0, [[1, 512]]), lhsT=cBa(CB_ONE, 128, dp, 1),
                                              rhs=sb(hib, 0, [[1, 512]], dp, 1), start=True, stop=False),
                     reads=["hi", "cB"], writes=[("bank", bb)])
                S.op("pe", lambda e: e.matmul(pb(bb, 0, [[1, 512]]), lhsT=cBa(CB_ONE, 128, dp, 1),
                                              rhs=sb(lob, 0, [[1, 512]], dp, 1), start=False, stop=True),
                     reads=["lo", "cB"], writes=[("bank", bb)])
                S.op("act", lambda e: e.activation(out=sb(bcs, 0, [[1, 512]], u0, 64),
                                                   in_=pb(bb, 0, [[1, 512]], u0, 64), func=AF.Copy),
                     reads=[("bank", bb)], writes=["bcs"])
                S.op("dve", lambda e: e.tensor_tensor(
                    out=sb(attnT, hp * 2052 + 512 * w, [[1, 512]], u0, 64),
                    in0=pb(ab, 0, [[1, 512]], u0, 64), in1=sb(bcs, 0, [[1, 512]], u0, 64), op=ALU.mult),
                    reads=[ares, "bcs"], writes=[("attnT", hp, w, X)])

            for rho in range(16):
                for kb in range(2):
                    score(2, rho, kb, X)
            Sw(0)
            Sw(1)
            for w in range(4):
                PVw(w)
                N1(w)
                if w + 2 < 4:
                    Sw(w + 2)
                N2(w)

    if STOP == "B":
        return fin()
    if ONLY == "Bs":
        S.enabled = True
    S.barrier()
    for b in range(NS):
        S.op("dve", lambda e, b=b: e.tensor_tensor(
            out=sb(Qbd, b * 8, [[96, 4], [32, 3], [1, 8]]),
            in0=sb(cF, CF_BM, [[8, 4], [0, 3], [1, 8]]),
            in1=sb(qkvs, b, [[24, 4], [4, 3], [0, 8]]), op=ALU.mult),
            reads=[("qkvs", hp) for hp in range(4)] + ["cF"], writes=[("Qbd", b)])
    for b in range(NS):
        for g in range(3):
            W, dil = WIN[g], RS[g]
            si = (b * 3 + g) % 2
            KVs, KVb, KTs, PTs, PTn, PTnm = KVs2[si], KVb2[si], KTs2[si], PTs2[si], PTn2[si], PTnm2[si]
            tb_, sbk = (0, 1) if si == 0 else (5, 6)
            S.dma("sp", lambda e, g=g, b=b, W=W, dil=dil: e.dma_start(
                out=KVs[:], in_=AP(cache_d[g], b * W * 1024, [[dil * 1024, 128], [1, 1024]])),
                writes=[("KVs", si)])
            S.op("dve", lambda e: e.tensor_copy(out=KVb[:], in_=KVs[:]), reads=[("KVs", si)], writes=[("KVb", si)])
            for c in range(4):
                S.op("pe", lambda e, c=c: e.transpose(out=pbb(tb_, c * 128, [[1, 128]]),
                                                      in_=sb(KVb, c * 128, [[1, 128]]), identity=ident(128)),
                     reads=[("KVb", si), "cB"], writes=[("bank", tb_)])
            S.op("act", lambda e: e.activation(out=KTs[:], in_=pbb(tb_, 0, [[1, 512]]), func=AF.Copy),
                 reads=[("bank", tb_)], writes=[("KTs", si)])
            for c in range(4):
                S.op("pe", lambda e, c=c, g=g, b=b: e.matmul(
                    pb(sbk, 0, [[1, 8]]), lhsT=sb(KTs, c * 128, [[1, 128]]),
                    rhs=sb(Qbd, c * 96 + g * 32 + b * 8, [[1, 8]]), start=(c == 0), stop=(c == 3)),
                    reads=[("KTs", si), ("Qbd", b)], writes=[("bank", sbk)])
            for c in range(4):
                S.op("pe", lambda e, c=c, g=g, b=b: e.matmul(
                    pb(sbk, 8, [[1, 8]], 0, NS), lhsT=sb(qkvs, c * 24 + (3 + g) * 4, [[1, 4]]),
                    rhs=sb(Qbd, c * 96 + g * 32 + b * 8, [[1, 8]]), start=(c == 0), stop=(c == 3)),
                    reads=[("qkvs", hp) for hp in range(4)] + [("Qbd", b)], writes=[("bank", sbk)])
            S.op("act", lambda e: e.activation(out=PTs[:], in_=pb(sbk, 0, [[1, 8]]), func=AF.Exp, scale=0.125),
                 reads=[("bank", sbk)], writes=[("PTs", si)])
            S.op("act", lambda e: e.activation(out=sb(PTn, 0, [[1, 8]], 0, NS), in_=pb(sbk, 8, [[1, 8]], 0, NS),
                                               func=AF.Exp, scale=0.125),
                 reads=[("bank", sbk)], writes=[("PTn", si)])
            S.op("dve", lambda e, b=b: e.tensor_scalar(out=sb(PTnm, 0, [[1, 8]], 0, NS),
                                                       in0=sb(PTn, 0, [[1, 8]], 0, NS),
                                                       scalar1=cFa(CF_OH + b, 1, 0, NS), scalar2=None, op0=ALU.mult),
                 reads=[("PTn", si), "cF"], writes=[("PTnm", si)])
            S.op("pe", lambda e, g=g: e.matmul(pb(2, 0, [[1, 512]], 0, 8), lhsT=PTs[:],
                                               rhs=sb(KVb, 512, [[1, 512]]), start=(g == 0), stop=False),
                 reads=[("PTs", si), ("KVb", si)], writes=[("bank", 2)])
            S.op("pe", lambda e, g=g: e.matmul(pb(2, 0, [[1, 512]], 0, 8), lhsT=sb(PTnm, 0, [[1, 8]], 0, NS),
                                               rhs=sb(vnb, g * 512, [[1, 512]], 0, NS), start=False, stop=(g == 2)),
                 reads=[("PTnm", si)] + [("vnb", hp) for hp in range(4)], writes=[("bank", 2)])
            S.op("pe", lambda e, g=g: e.matmul(pb(3, 0, [[1, 1]], 0, 8), lhsT=PTs[:], rhs=cBa(CB_ONE, 1),
                                               start=(g == 0), stop=False),
                 reads=[("PTs", si), "cB"], writes=[("bank", 3)])
            S.op("pe", lambda e, g=g: e.matmul(pb(3, 0, [[1, 1]], 0, 8), lhsT=sb(PTnm, 0, [[1, 8]], 0, NS),
                                               rhs=cBa(CB_ONE, 1, 0, NS), start=False, stop=(g == 2)),
                 reads=[("PTnm", si), "cB"], writes=[("bank", 3)])
        S.op("dve", lambda e: e.reciprocal(out=sb(rdn, 0, [[1, 1]], 0, 8), in_=pb(3, 0, [[1, 1]], 0, 8)),
             reads=[("bank", 3)], writes=["rdn"])
        S.op("dve", lambda e: e.scalar_tensor_tensor(out=sb(Om, 0, [[1, 512]], 0, 8), in0=pb(2, 0, [[1, 512]], 0, 8),
                                                     scalar=sb(rdn, 0, [[1, 1]], 0, 8),
                                                     in1=cBa(CB_BLK, 512, 0, 8), op0=ALU.mult, op1=ALU.mult),
             reads=[("bank", 2), "rdn", "cB"], writes=["Om"])
        for c in range(4):
            S.op("pe", lambda e, c=c: e.matmul(pb(4, c, [[1, 1]]), lhsT=sb(Om, c * 128, [[1, 128]], 0, 8),
                                               rhs=cBa(CB_ONE, 1, 0, 8), start=True, stop=True),
                 reads=["Om", "cB"], writes=[("bank", 4)])
        S.op("act", lambda e, b=b: e.activation(out=sb(attnT, 2048 + b, [[2052, 4]]), in_=pb(4, 0, [[1, 4]]),
                                                func=AF.Copy),
             reads=[("bank", 4)], writes=[("attnTs", b)])

    if STOP == "Bs":
        return fin()
    if ONLY == "C1":
        S.enabled = True
    S.barrier()
    def load_wC2(oc):
        S.dma("pool", lambda e: e.dma_start(
            out=sb(wC2[oc % 2], 0, [[1792, 2], [1, 1792]]),
            in_=AP(wC2_d, oc * 128 * 3584, [[3584, 128], [1792, 2], [1, 1792]])), writes=[("wC2", oc % 2)])

    def load_wC1(fc):
        S.dma("pool", lambda e: e.dma_start(
            out=sb(wC1[fc % 2], 0, [[1536, 2], [1, 1536]]),
            in_=AP(wC1_d, fc * 128 * 3072, [[3072, 128], [1536, 2], [1, 1536]])), writes=[("wC1", fc % 2)])

    wins = [(NH - 2, 2, "h")] + [(NH + 512 * w, 512, w) for w in range(4)] + [(NH + NT, NS, "s")]
    wi = 0
    for fc in range(8):
        wb = wC1[fc % 2]
        if fc == 0:
            load_wC1(0)
        if fc + 1 < 8:
            load_wC1(fc + 1)
        if fc == 6:
            load_wC2(0)
        u = ub[fc % 2]
        ures = ("u", fc % 2)
        for (hcol, n, kind) in wins:
            bks = {0: wi % 3, 1: 3 + wi % 2, 2: 5 + wi % 2}
            wi += 1
            blks = (1, 2) if kind == "h" else (1, 2, 0)
            for bi in blks:
                for kc in range(8):
                    S.op("pe", lambda e, kc=kc, bi=bi, bk=bks[bi], hcol=hcol, n=n, wb=wb: e.matmul(
                        pb(bk, 0, [[1, n]]), lhsT=sb(wb, kc * 384 + bi * 128, [[1, 128]]),
                        rhs=hTa(kc, hcol, [[1, n]]), start=(kc == 0), stop=(kc == 7)),
                        reads=[("wC1", fc % 2)] + hT_tiles(hcol, n), writes=[("bank", bks[bi])])
            cs = ccs[wi % 2]
            csr = ("ccs", wi % 2)
            S.op("act", lambda e, cs=cs, bk=bks[1], n=n: e.activation(out=sb(cs, 0, [[1, n]]), in_=pb(bk, 0, [[1, n]]),
                                                                func=AF.Copy),
                 reads=[("bank", bks[1])], writes=[csr])
            ucol = 0 if kind == "h" else (2050 if kind == "s" else 2 + 512 * kind)
            uw = (ures, kind)
            S.op("dve", lambda e, u=u, ucol=ucol, n=n, bk=bks[2], cs=cs: e.tensor_tensor(
                out=sb(u, ucol, [[1, n]]), in0=pb(bk, 0, [[1, n]]), in1=sb(cs, 0, [[1, n]]), op=ALU.mult),
                reads=[("bank", bks[2]), csr], writes=[uw])
            if kind == "h":
                continue
            tb = t1b[wi % 2]
            tr = ("t1", wi % 2)
            cw = lambda j, fc=fc: cFa(CF_CW + fc * 3 + j)
            if kind == "s":
                prev = [ures, "s"]
                S.op("dve", lambda e, u=u, tb=tb, cw=cw: e.tensor_scalar(
                    out=sb(tb, 0, [[1, NS]]), in0=sb(u, 2050, [[1, NS]]), scalar1=cw(2), scalar2=None, op0=ALU.mult),
                    reads=[uw, "cF"], writes=[tr])
                for j in range(2):
                    S.op("dve", lambda e, tb=tb, cw=cw, j=j, fc=fc: e.scalar_tensor_tensor(
                        out=sb(tb, 0, [[1, NS]]), in0=sb(cF, CF_ST + fc * 8 + j * 4, [[1, NS]]), scalar=cw(j),
                        in1=sb(tb, 0, [[1, NS]]), op0=ALU.mult, op1=ALU.add),
                        reads=[tr, "cF"], writes=[tr])
                ocol = fc * 2052 + 2048
            else:
                w = kind
                pr = [(ures, w - 1)] if w > 0 else [(ures, "h")]
                S.op("act", lambda e, u=u, tb=tb, cw=cw, ucol=ucol: e.activation(
                    out=sb(tb, 0, [[1, 512]]), in_=sb(u, ucol, [[1, 512]]), func=AF.Copy, scale=cw(2)),
                    reads=[uw, "cF"], writes=[tr])
                for j in (1, 0):
                    S.op("dve", lambda e, u=u, tb=tb, cw=cw, ucol=ucol, j=j: e.scalar_tensor_tensor(
                        out=sb(tb, 0, [[1, 512]]), in0=sb(u, ucol - (2 - j), [[1, 512]]), scalar=cw(j),
                        in1=sb(tb, 0, [[1, 512]]), op0=ALU.mult, op1=ALU.add),
                        reads=[uw, tr, "cF"] + pr, writes=[tr])
                ocol = fc * 2052 + 512 * w
            S.op("dve", lambda e, bk=bks[0], n=n, tb=tb, ocol=ocol: e.tensor_tensor(
                out=sb(cbvT, ocol, [[1, n]]), in0=pb(bk, 0, [[1, n]]), in1=sb(tb, 0, [[1, n]]), op=ALU.mult),
                reads=[("bank", bks[0]), tr], writes=[("cbvT", fc, kind)])
        S.op("pool", lambda e, u=u, fc=fc: e.tensor_copy(out=sb(convp_sb, fc * 2, [[1, 2]]), in_=sb(u, 2048, [[1, 2]])),
             reads=[(ures, 3)], writes=["convp_sb"])
        S.op("pool", lambda e, u=u, fc=fc: e.tensor_copy(out=sb(convs_sb, fc * 4, [[1, 4]]), in_=sb(u, 2050, [[1, 4]])),
             reads=[(ures, "s")], writes=["convs_sb"])
    S.dma("sp", lambda e: e.dma_start(out=convp_d.ap(), in_=convp_sb[:]), reads=["convp_sb"])
    S.dma("sp", lambda e: e.dma_start(out=convs1_d.ap(), in_=convs_sb[:]), reads=["convs_sb"])

    if STOP == "C1":
        return fin()
    if ONLY == "C2":
        S.enabled = True
    S.barrier()
    S.dma("pool", lambda e: e.dma_start(out=sb(wo, 0, [[2048, 4], [1, 2048]]),
                                        in_=AP(wO_d, 0, [[8192, 128], [2048, 4], [1, 2048]])), writes=["wo"])
    wins2 = [(512 * w, 512) for w in range(4)] + [(NT, NS)]
    wi = 0
    for oc in range(8):
        wb = wC2[oc % 2]
        wr = ("wC2", oc % 2)
        if oc + 1 < 8:
            load_wC2(oc + 1)
        for (c0, n) in wins2:
            s4 = 4 * (wi % 2)
            wi += 1
            k2 = wi % 2
            hres = hT_tiles(NH + c0, n)
            for gi in range(2):
                for kc in range(8):
                    S.op("pe", lambda e, kc=kc, gi=gi, s4=s4, c0=c0, n=n, wb=wb: e.matmul(
                        pb(s4 + gi, 0, [[1, n]]), lhsT=sb(wb, kc * 256 + gi * 128, [[1, 128]]),
                        rhs=hTa(kc, NH + c0, [[1, n]]), start=(kc == 0), stop=(kc == 7)),
                        reads=[wr] + hres, writes=[("bank", s4 + gi)])
            if n == 512:
                ares_ = [("attnT", hp, c0 // 512, X) for hp in range(4) for X in range(2)]
            else:
                ares_ = [("attnTs", b) for b in range(NS)]
            for kc in range(4):
                S.op("pe", lambda e, kc=kc, s4=s4, c0=c0, n=n, wb=wb: e.matmul(
                    pb(s4 + 3, 0, [[1, n]]), lhsT=sb(wb, 3072 + kc * 128, [[1, 128]]),
                    rhs=sb(attnT, kc * 2052 + c0, [[1, n]]), start=(kc == 0), stop=(kc == 3)),
                    reads=[wr] + ares_, writes=[("bank", s4 + 3)])
            cres = [("cbvT", fc, k) for fc in range(8) for k in ((c0 // 512,) if n == 512 else ("s",))]
            for kc in range(8):
                S.op("pe", lambda e, kc=kc, s4=s4, c0=c0, n=n, wb=wb: e.matmul(
                    pb(s4 + 2, 0, [[1, n]]), lhsT=sb(wb, 2048 + kc * 128, [[1, 128]]),
                    rhs=sb(cbvT, kc * 2052 + c0, [[1, n]]), start=(kc == 0), stop=(kc == 7)),
                    reads=[wr] + cres, writes=[("bank", s4 + 2)])
            S.op("act", lambda e, s4=s4, n=n, k2=k2: e.activation(out=sb(sab[k2], 0, [[1, n]]), in_=pb(s4, 0, [[1, n]]),
                                                              func=AF.Sigmoid),
                 reads=[("bank", s4)], writes=[("sa", k2)])
            S.op("act", lambda e, s4=s4, n=n, k2=k2: e.activation(out=sb(scb[k2], 0, [[1, n]]), in_=pb(s4 + 1, 0, [[1, n]]),
                                                              func=AF.Sigmoid),
                 reads=[("bank", s4 + 1)], writes=[("sc_", k2)])
            S.op("dve", lambda e, s4=s4, n=n, k2=k2: e.tensor_tensor(
                out=sb(tab[k2], 0, [[1, n]]), in0=pb(s4 + 3, 0, [[1, n]]), in1=sb(sab[k2], 0, [[1, n]]), op=ALU.mult),
                reads=[("bank", s4 + 3), ("sa", k2)], writes=[("ta", k2)])
            S.op("dve", lambda e, s4=s4, n=n, k2=k2: e.tensor_tensor(
                out=sb(tcb[k2], 0, [[1, n]]), in0=pb(s4 + 2, 0, [[1, n]]), in1=sb(scb[k2], 0, [[1, n]]), op=ALU.mult),
                reads=[("bank", s4 + 2), ("sc_", k2)], writes=[("tc", k2)])
            S.op("pool", lambda e, n=n, k2=k2, oc=oc, c0=c0: e.tensor_tensor(
                out=sb(mixT, oc * 2052 + c0, [[1, n]]), in0=sb(tab[k2], 0, [[1, n]]), in1=sb(tcb[k2], 0, [[1, n]]),
                op=ALU.add),
                reads=[("ta", k2), ("tc", k2)], writes=[("mixT", oc, c0)])

    if STOP == "C2":
        return fin()
    if ONLY == "D":
        S.enabled = True
    S.barrier()
    cur_junk[0] = junkD
    p2_args = []
    S.dma("sp", lambda e: e.dma_start(out=gB3[:], in_=AP(g3_d, 0, [[0, 128], [1, D]])), writes=["gB3"])
    S.dma("sp", lambda e: e.dma_start(out=gB2[:], in_=AP(g2_d, 0, [[0, 128], [1, D]])), writes=["gB2"])
    S.dma("pool", lambda e: e.dma_start(out=sb(wF[0], 0, [[2048, 4], [1, 2048]]),
                                        in_=AP(wF_d, 0, [[8192, 128], [2048, 4], [1, 2048]])), writes=[("wF", 0)])

    def rstd_from_psum(tt, np_, pt, col):
        for h in range(2):
            S.op("act", lambda e, h=h: e.activation(out=sb(junkD, 0, [[1, 512]], 0, np_),
                                                   in_=AP(PS[pt], h * 512, [[1024, np_], [1, 512]]),
                                                   func=AF.Square, accum_out=sb(stat2, col + h, [[1, 1]], 0, np_)),
                 reads=[("bank", 2 * pt + h)], writes=[("st", col + h), "junk"])
        S.op("dve", lambda e: e.tensor_tensor(out=sb(stat2, col + 2, [[1, 1]], 0, np_),
                                              in0=sb(stat2, col, [[1, 1]], 0, np_),
                                              in1=sb(stat2, col + 1, [[1, 1]], 0, np_), op=ALU.add),
             reads=[("st", col), ("st", col + 1)], writes=[("st", col + 2)])
        S.op("act", lambda e: e.activation(out=sb(stat2, col + 2, [[1, 1]], 0, np_),
                                           in_=sb(stat2, col + 2, [[1, 1]], 0, np_), func=AF.Sqrt,
                                           bias=eps_ap(np_), scale=1.0 / D),
             reads=[("st", col + 2), "cF"], writes=[("st", col + 2)])
        S.op("dve", lambda e: e.reciprocal(out=sb(stat2, col + 3, [[1, 1]], 0, np_),
                                           in_=sb(stat2, col + 2, [[1, 1]], 0, np_)),
             reads=[("st", col + 2)], writes=[("st", col + 3)])
        return sb(stat2, col + 3, [[1, 1]], 0, np_), ("st", col + 3)

    for tt in range(17):
        np_ = 128 if tt < 16 else NS
        c0 = tt * 128
        pt = tt % 3
        b2 = tt % 4
        bx = tt % 4
        tD = tDb[tt % 3]
        mres = [("mixT", oc, (c0 // 512) * 512 if tt < 16 else NT) for oc in range(8)]
        for h in range(2):
            for kc in range(8):
                S.op("pe", lambda e, kc=kc, h=h, pt=pt, c0=c0, np_=np_: e.matmul(
                    AP(PS[pt], h * 512, [[1024, np_], [1, 512]]), lhsT=sb(mixT, kc * 2052 + c0, [[1, np_]]),
                    rhs=sb(wo, kc * 1024 + h * 512, [[1, 512]]), start=(kc == 0), stop=(kc == 7)),
                    reads=["wo"] + mres, writes=[("bank", 2 * pt + h)])
        if tt >= 3 and tt % 2 == 1:
            norm_p2(*p2_args[tt - 3])
            norm_p2(*p2_args[tt - 2])
        rs_ap, rs_res = rstd_from_psum(tt, np_, pt, 64 + (tt % 4) * 8)
        src = AP(x_all, (NH + c0) * D, [[D, 128], [1, D]]) if tt < 16 else xs_d.ap()
        S.dma("sp", lambda e, bx=bx, np_=np_, src=src: e.dma_start(out=sb(xtD[bx], 0, [[1, 1024]], 0, np_), in_=src),
              writes=[("xtD", bx)])
        S.op("dve", lambda e, pt=pt, np_=np_, rs_ap=rs_ap: e.scalar_tensor_tensor(
            out=sb(tD, 0, [[1, 1024]], 0, np_), in0=AP(PS[pt], 0, [[1024, np_], [1, 1024]]), scalar=rs_ap,
            in1=sb(gB2, 0, [[1, 1024]], 0, np_), op0=ALU.mult, op1=ALU.mult),
            reads=[("bank", 2 * pt), ("bank", 2 * pt + 1), rs_res, "gB2"], writes=[("tD", tt % 3)])
        S.op("pool", lambda e, b2=b2, bx=bx, np_=np_: e.tensor_tensor(
            out=sb(x1t[b2], 0, [[1, 1024]], 0, np_), in0=sb(tD, 0, [[1, 1024]], 0, np_),
            in1=sb(xtD[bx], 0, [[1, 1024]], 0, np_), op=ALU.add),
            reads=[("tD", tt % 3), ("xtD", bx)], writes=[("x1t", b2)])
        dst = AP(y_d, c0 * D, [[D, 128], [1, D]]) if tt < 16 else ys_d.ap()
        S.dma("sp", lambda e, b2=b2, np_=np_, dst=dst: e.dma_start(out=dst, in_=sb(x1t[b2], 0, [[1, 1024]], 0, np_)),
              reads=[("x1t", b2)], writes=[("yd", tt)])
        norm_p1(lambda b2=b2, np_=np_: sb(x1t[b2], 0, [[1, 1024]], 0, np_), ("x1t", b2), np_,
                xn2[b2], ("xn2", b2), gB3, "gB3", 128 + (tt % 4) * 4)
        p2_args.append((np_, xn2[b2], ("xn2", b2), 6 + (tt % 2),
                        (lambda c0=c0, np_=np_: sb(h2T, c0, [[2052, 8], [1, np_]])), ("h2T", tt), tt))
    norm_p2(*p2_args[14])
    norm_p2(*p2_args[15])
    norm_p2(*p2_args[16])
    assert D_P2_COUNT[0] == 17

    if STOP == "D":
        return fin()
    if ONLY == "E":
        S.enabled = True
    S.barrier()
    S.dma("sp", lambda e: e.dma_start(out=gB4[:], in_=AP(g4_d, 0, [[0, 128], [1, D]])), writes=["gB4"])
    winsE = [(512 * w, 512, [("h2T", 4 * w + i) for i in range(4)]) for w in range(4)] + [(NT, NS, [("h2T", 16)])]
    fb = 0
    fin_q = []
    ytE3 = [tE] + ytE
    for cg in range(8):
        wb = wF[cg % 2]
        wr = ("wF", cg % 2)
        if cg + 1 < 8:
            S.dma("pool", lambda e, cg=cg: e.dma_start(
                out=sb(wF[(cg + 1) % 2], 0, [[2048, 4], [1, 2048]]),
                in_=AP(wF_d, (cg + 1) * 128 * 8192, [[8192, 128], [2048, 4], [1, 2048]])),
                writes=[("wF", (cg + 1) % 2)])
        if ESTOP == "f0":
            return fin()
        for (c0, n, hres) in winsE:
            if ESTOP == "f1a" and c0 > 0:
                return fin()
            for c in range(4):
                bank = fb % 4
                fb += 1
                for kc in range(8):
                    if ESTOP == "f1b" and (kc > 0 or c > 0):
                        return fin()
                    if ESTOP.startswith("n") and (c * 8 + kc) >= int(ESTOP[1:]):
                        return fin()
                    S.op("pe", lambda e, kc=kc, c=c, bank=bank, c0=c0, n=n, wb=wb: e.matmul(
                        pb(bank, 0, [[1, n]]), lhsT=sb(wb, kc * 512 + c * 128, [[1, 128]]),
                        rhs=sb(h2T, kc * 2052 + c0, [[1, n]]), start=(kc == 0), stop=(kc == 7)),
                        reads=[wr] + hres, writes=[("bank", bank)])
                rb = fb % 2
                if os.environ.get("MK_EV") == "c":
                    continue
                S.op("act", lambda e, bank=bank, n=n, rb=rb: e.activation(out=sb(rlb[rb], 0, [[1, n]]), in_=pb(bank, 0, [[1, n]]),
                                                                func=(AF.Copy if os.environ.get("MK_EV") == "b" else AF.Relu)),
                     reads=[("bank", bank)], writes=[("rl", rb)])
                S.op(("dve" if os.environ.get("MK_EV") == "a" else "pool"), lambda e, c=c, c0=c0, n=n, rb=rb: e.tensor_tensor(
                    out=sb(f1T, c * 2052 + c0, [[1, n]]), in0=sb(rlb[rb], 0, [[1, n]]), in1=sb(rlb[rb], 0, [[1, n]]),
                    op=ALU.mult),
                    reads=[("rl", rb)], writes=[("f1T", c, c0)])
        if ESTOP == "f1":
            return fin()
        for tt in range(17):
            if ESTOP == "f2" and cg == 1:
                return fin()
            np_ = 128 if tt < 16 else NS
            c0 = tt * 128
            pt = 2 + tt % 2
            fres = [("f1T", c, (c0 // 512) * 512 if tt < 16 else NT) for c in range(4)]
            for h in range(2):
                for c in range(4):
                    S.op("pe", lambda e, c=c, h=h, pt=pt, c0=c0, np_=np_, wb=wb: e.matmul(
                        AP(PS[pt], h * 512, [[1024, np_], [1, 512]]), lhsT=sb(f1T, c * 2052 + c0, [[1, np_]]),
                        rhs=sb(wb, 4096 + c * 1024 + h * 512, [[1, 512]]), start=(c == 0), stop=(c == 3)),
                        reads=[wr] + fres, writes=[("bank", 2 * pt + h)])
            fa = (lambda tt=tt, np_=np_: sb(facc, tt * 1024, [[1, 1024]], 0, np_)) if tt < 16 else \
                 (lambda np_=np_: sb(faccs, 0, [[1, 1024]], 0, np_))
            fr = ("facc", tt)
            pin = lambda pt=pt, np_=np_: AP(PS[pt], 0, [[1024, np_], [1, 1024]])
            breads = [("bank", 2 * pt), ("bank", 2 * pt + 1)]
            if cg == 0:
                S.op("act", lambda e, fa=fa, pin=pin: e.activation(out=fa(), in_=pin(), func=AF.Copy),
                     reads=breads, writes=[fr])
            else:
                S.op("dve", lambda e, fa=fa, pin=pin: e.tensor_tensor(out=fa(), in0=pin(), in1=fa(), op=ALU.add),
                     reads=breads + [fr], writes=[fr])
            if cg == 7 and ESTOP != "nofin":
                col = 192 + (tt % 4) * 8
                ss = sb(stat2, col, [[1, 1]], 0, np_)
                S.op("act", lambda e, fa=fa, np_=np_, ss=ss: e.activation(out=sb(junkE, 0, [[1, 1024]], 0, np_), in_=fa(),
                                                                     func=AF.Square, accum_out=ss),
                     reads=[fr], writes=[("st", col), "junk"])
                S.op("act", lambda e, ss=ss, np_=np_: e.activation(out=ss, in_=ss, func=AF.Sqrt, bias=eps_ap(np_),
                                                              scale=1.0 / D),
                     reads=[("st", col), "cF"], writes=[("st", col)])
                rs = sb(stat2, col + 1, [[1, 1]], 0, np_)
                S.op("dve", lambda e, ss=ss, rs=rs: e.reciprocal(out=rs, in_=ss), reads=[("st", col)],
                     writes=[("st", col + 1)])
                src = AP(y_d, c0 * D, [[D, 128], [1, D]]) if tt < 16 else ys_d.ap()
                bx = tt % 2
                by = tt % 3
                S.dma("sp", lambda e, bx=bx, np_=np_, src=src: e.dma_start(out=sb(xrE[bx], 0, [[1, 1024]], 0, np_), in_=src),
                      reads=[("yd", tt)], writes=[("xrE", bx)])

                def fin2(tt=tt, np_=np_, fa=fa, fr=fr, rs=rs, col=col, bx=bx, by=by, c0=c0):
                    S.op("dve", lambda e: e.scalar_tensor_tensor(
                        out=sb(ytE3[by], 0, [[1, 1024]], 0, np_), in0=fa(), scalar=rs, in1=sb(gB4, 0, [[1, 1024]], 0, np_),
                        op0=ALU.mult, op1=ALU.mult),
                        reads=[fr, ("st", col + 1), "gB4"], writes=[("ytE", by)])
                    S.op("pool", lambda e: e.tensor_tensor(
                        out=sb(ytE3[by], 0, [[1, 1024]], 0, np_), in0=sb(ytE3[by], 0, [[1, 1024]], 0, np_),
                        in1=sb(xrE[bx], 0, [[1, 1024]], 0, np_), op=ALU.add),
                        reads=[("ytE", by), ("xrE", bx)], writes=[("ytE", by)])
                    dst = AP(y_d, c0 * D, [[D, 128], [1, D]]) if tt < 16 else ys_d.ap()
                    S.dma("sp", lambda e: e.dma_start(out=dst, in_=sb(ytE3[by], 0, [[1, 1024]], 0, np_)),
                          reads=[("ytE", by), ("xrE", bx)], writes=[("yd", tt)])
                fin_q.append(fin2)
                if len(fin_q) >= 2:
                    fin_q.pop(0)()
    for f in fin_q:
        f()
    S.build()
    return nc


_NC_CACHE = {}


def _consts():
    cB = np.zeros((128, NCB), np.float32)
    p = np.arange(128)[:, None]
    c = np.arange(256)[None, :]
    m = np.where(c < 128, p <= c, p >= c - 128)
    cB[:, CB_MASK:CB_MASK + 256] = m
    cB[:, CB_ID:CB_ID + 128] = np.eye(128)
    cB[:, CB_ONE:CB_ONE + 128] = 1.0
    hh = np.arange(8)[:, None]
    cc = np.arange(512)[None, :]
    cB[0:8, CB_BLK:CB_BLK + 512] = (cc // 64 == hh)
    return cB.astype(ml_dtypes.bfloat16)


def kernel(x_prompt, x_sample, cache_kv_w128, cache_kv_w512, cache_kv_w2048, state_conv,
           w_in, conv_w, w_attn_o, w_conv_o, w_o, w_ff1, w_ff2,
           g_mix_pre, g_mix_post, g_ffn_pre, g_ffn_post):
    f32 = np.float32
    x_prompt = np.asarray(x_prompt, f32)
    x_sample = np.asarray(x_sample, f32)
    caches = [np.asarray(c, f32) for c in (cache_kv_w128, cache_kv_w512, cache_kv_w2048)]
    state_conv = np.asarray(state_conv, f32)
    w_in = np.asarray(w_in, f32)[0]
    conv_w = np.asarray(conv_w, f32)[0]
    w_attn_o = np.asarray(w_attn_o, f32)[0]
    w_conv_o = np.asarray(w_conv_o, f32)[0]
    w_o = np.asarray(w_o, f32)[0]
    w_ff1 = np.asarray(w_ff1, f32)[0]
    w_ff2 = np.asarray(w_ff2, f32)[0]
    g1 = np.asarray(g_mix_pre, f32)[0]
    g2 = np.asarray(g_mix_post, f32)[0]
    g3 = np.asarray(g_ffn_pre, f32)[0]
    g4 = np.asarray(g_ffn_post, f32)[0]

    def pack_kc(w):
        K, N = w.shape
        return np.ascontiguousarray(w.reshape(K // 128, 128, N).transpose(1, 0, 2))

    wB = np.empty((4, 128, 8, 1152), f32)
    for hp in range(4):
        cols = []
        for base in (0, 1536, 3072):
            for g in range(3):
                cols.append(np.arange(base + g * 512 + hp * 128, base + g * 512 + hp * 128 + 128))
        wB[hp] = pack_kc(w_in[:, np.concatenate(cols)])
    wB = wB.reshape(4, 128, 9216)
    wC1 = np.empty((8, 128, 8, 384), f32)
    for fc in range(8):
        cols = np.concatenate([np.arange(b0 + fc * 128, b0 + fc * 128 + 128) for b0 in (4608, 5632, 6656)])
        wC1[fc] = pack_kc(w_in[:, cols])
    wC1 = wC1.reshape(8, 128, 3072)
    wC2 = np.empty((8, 128, 3584), f32)
    for oc in range(8):
        cols = np.concatenate([np.arange(b0 + oc * 128, b0 + oc * 128 + 128) for b0 in (7680, 8704)])
        wC2[oc, :, 0:2048] = pack_kc(w_in[:, cols]).reshape(128, 2048)
        wC2[oc, :, 2048:3072] = pack_kc(w_conv_o[:, oc * 128:(oc + 1) * 128]).reshape(128, 1024)
        wC2[oc, :, 3072:3584] = pack_kc(w_attn_o[:, oc * 128:(oc + 1) * 128]).reshape(128, 512)
    wO = pack_kc(w_o).reshape(128, 8192)
    wF = np.empty((8, 128, 8192), f32)
    for cg in range(8):
        wF[cg, :, 0:4096] = pack_kc(w_ff1[:, cg * 512:(cg + 1) * 512]).reshape(128, 4096)
        wF[cg, :, 4096:8192] = pack_kc(w_ff2[cg * 512:(cg + 1) * 512, :]).reshape(128, 4096)
    cBc = _consts()

    in_maps = []
    for core in range(8):
        b, j = core // 4, core % 4
        xa = np.zeros((NH + NT, D), f32)
        if j > 0:
            xa[:NH] = x_prompt[b, NT * j - NH:NT * j]
        xa[NH:] = x_prompt[b, NT * j:NT * (j + 1)]
        sl = slice(NS * core, NS * core + NS)
        cFc = np.zeros((128, NCF), f32)
        cFc[:, CF_G1:CF_G1 + 8] = g1.reshape(8, 128).T
        cFc[:, CF_G3:CF_G3 + 8] = g3.reshape(8, 128).T
        cFc[:, CF_CW:CF_CW + 24] = conv_w.reshape(3, 8, 128).transpose(2, 1, 0).reshape(128, 24)
        stc = state_conv[0, sl]
        cFc[:, CF_ST:CF_ST + 64] = stc.reshape(NS, 2, 8, 128).transpose(3, 2, 1, 0).reshape(128, 64)
        cFc[:, CF_HV] = 1.0 if j > 0 else 0.0
        cFc[0:4, CF_OH:CF_OH + 4] = np.eye(4)
        cFc[:, CF_EPS] = 1e-6
        pp = np.arange(128)[:, None, None]
        cch = np.arange(4)[None, :, None]
        hh = np.arange(8)[None, None, :]
        cFc[:, CF_BM:CF_BM + 32] = (hh == 2 * cch + pp // 64).reshape(128, 32)
        in_maps.append({
            "x_all": xa,
            "xs": np.ascontiguousarray(x_sample[sl, 0]),
            "c128": np.ascontiguousarray(caches[0][0, sl]).reshape(NS * 128, 1024),
            "c512": np.ascontiguousarray(caches[1][0, sl]).reshape(NS * 512, 1024),
            "c2048": np.ascontiguousarray(caches[2][0, sl]).reshape(NS * 2048, 1024),
            "st1": np.ascontiguousarray(state_conv[0, sl, 1]),
            "wB": wB, "wC1": wC1, "wC2": wC2, "wO": wO, "wF": wF,
            "g1": g1.reshape(1, D), "g3": g3.reshape(1, D),
            "g2": g2.reshape(1, D), "g4": g4.reshape(1, D),
            "constF": cFc, "constB": cBc,
        })
    ncores = int(os.environ.get("MK_CORES", "8"))
    nc = build_program()
    res = run_bass_kernel_spmd(nc, in_maps[:ncores], core_ids=list(range(ncores)))
    R = list(res.results)
    while len(R) < 8:
        R.append(R[0])

    y_prompt = np.empty((2, 8192, D), f32)
    y_sample = np.empty((32, 1, D), f32)
    kvp = [np.empty((1, 2, WIN[g], 2, 8, 64), f32) for g in range(3)]
    conv_prompt = np.empty((1, 2, 2, D), f32)
    kvs = [np.empty((1, 32, WIN[g], 2, 8, 64), f32) for g in range(3)]
    conv_sample = np.empty((1, 32, 2, D), f32)
    names_p = ("kvp128", "kvp512", "kvp2048")
    names_s = ("kvs128", "kvs512", "kvs2048")
    for core in range(8):
        b, j = core // 4, core % 4
        r = R[core]
        y_prompt[b, NT * j:NT * (j + 1)] = r["y"]
        sl = slice(NS * core, NS * core + NS)
        y_sample[sl, 0] = r["ys"]
        if j == 3:
            for g in range(3):
                kvp[g][0, b] = r[names_p[g]].reshape(WIN[g], 2, 8, 64)
            conv_prompt[0, b] = r["convp"].reshape(128, 8, 2).transpose(2, 1, 0).reshape(2, D)
        for g in range(3):
            kvs[g][0, sl] = r[names_s[g]].reshape(NS, WIN[g], 2, 8, 64)
        conv_sample[0, sl, 0] = r["convs0"]
        conv_sample[0, sl, 1] = r["convs1"].reshape(128, 8, NS).transpose(2, 1, 0).reshape(NS, D)
    return (y_prompt, y_sample, kvp[0], kvp[1], kvp[2], conv_prompt,
            kvs[0], kvs[1], kvs[2], conv_sample)
```

```python
import contextlib
import os
import types
import numpy as np
import ml_dtypes
import concourse.bass as bass
import concourse.mybir as mybir
from concourse.bass_utils import run_bass_kernel_spmd

F32 = mybir.dt.float32
BF16 = mybir.dt.bfloat16
AF = mybir.ActivationFunctionType
ALU = mybir.AluOpType
AP = bass.AP

NT = 2048
NH = 2048
NS = 4
HW = NH + NT + NS
D = 1024
RS = (1, 4, 16)
NQ = (16, 4, 1)
WIN = (128, 512, 2048)


class _Op:
    __slots__ = ("eng", "fn", "reads", "writes", "idx", "dma", "deps", "sig", "sigval",
                 "dsem", "dval", "k")

    def __init__(self, eng, fn, reads, writes, dma):
        self.eng = eng
        self.fn = fn
        self.reads = reads
        self.writes = writes
        self.dma = dma
        self.deps = set()
        self.sig = False
        self.sigval = 0
        self.dsem = None
        self.dval = 0
        self.k = 0


class Sched:
    ENG = ("pe", "act", "dve", "pool", "sp")
    RING = int(os.environ.get("MK_RING", "12"))

    def __init__(self, nc):
        self.nc = nc
        self.ops = []
        self.res_w = {}
        self.res_r = {}
        self.pending_barrier = {}
        self._bar_start = 0

    def _add(self, eng, fn, reads, writes, dma):
        isb = lambda r: isinstance(r, tuple) and r[0] == "bank"
        excl = [r for r in tuple(reads) + tuple(writes) if isb(r)]
        reads = [r for r in reads if not isb(r)]
        writes = [r for r in writes if not isb(r)]
        if fn.__closure__:
            cells = []
            for c in fn.__closure__:
                try:
                    cells.append(types.CellType(c.cell_contents))
                except ValueError:
                    cells.append(c)
            fn = types.FunctionType(fn.__code__, fn.__globals__, fn.__name__, fn.__defaults__,
                                    tuple(cells))
        op = _Op(eng, fn, tuple(reads), tuple(writes), dma)
        op.idx = len(self.ops)
        deps = set()
        for r in op.reads:
            w = self.res_w.get(r)
            if w is not None:
                deps.add(w)
        for w_ in op.writes:
            w = self.res_w.get(w_)
            if w is not None:
                deps.add(w)
            for r in self.res_r.get(w_, ()):
                deps.add(r)
        for x in excl:
            w = self.res_w.get(x)
            if w is not None:
                deps.add(w)
        if eng in self.pending_barrier:
            deps |= self.pending_barrier.pop(eng)
        deps.discard(op.idx)
        op.deps = deps
        for r in op.reads:
            self.res_r.setdefault(r, []).append(op.idx)
        for w_ in op.writes:
            self.res_w[w_] = op.idx
            self.res_r[w_] = []
        for x in excl:
            self.res_w[x] = op.idx
        self.ops.append(op)
        return op

    enabled = True

    def op(self, eng, fn, reads=(), writes=()):
        if not self.enabled:
            return None
        return self._add(eng, fn, reads, writes, False)

    def dma(self, eng, fn, reads=(), writes=()):
        if not self.enabled:
            return None
        return self._add(eng, fn, reads, writes, True)

    def barrier_one(self, eng):
        last = {}
        dmas = set()
        for o in self.ops:
            if o.dma:
                dmas.add(o.idx)
            else:
                last[o.eng] = o.idx
        self.pending_barrier[eng] = set(last.values()) | dmas | self.pending_barrier.get(eng, set())

    def barrier(self):
        last = {}
        dmas = set()
        for o in self.ops[self._bar_start:]:
            if o.dma:
                dmas.add(o.idx)
            else:
                last[o.eng] = o.idx
        self._bar_start = len(self.ops)
        deps = set(last.values()) | dmas
        for e in self.ENG:
            self.pending_barrier[e] = set(deps) | self.pending_barrier.get(e, set())

    def build(self):
        nc = self.nc
        ops = self.ops
        for o in ops:
            nd = set()
            best = {}
            for d in o.deps:
                p = ops[d]
                if p.dma:
                    nd.add(d)
                    continue
                if p.eng == o.eng and not o.dma:
                    if p.eng == "pe":
                        continue
                    if not (set(p.writes) & (set(o.reads) | set(o.writes))):
                        continue
                if p.eng not in best or best[p.eng] < d:
                    best[p.eng] = d
            nd |= set(best.values())
            o.deps = nd
        for o in ops:
            for d in o.deps:
                ops[d].sig = True
        cnt = {e: 0 for e in self.ENG}
        dcnt = {e: 0 for e in self.ENG}
        for o in ops:
            if o.dma:
                o.k = dcnt[o.eng]
                dcnt[o.eng] += 1
            elif o.sig:
                cnt[o.eng] += 1
                o.sigval = cnt[o.eng]
        self.cnt = dict(cnt)
        self.dcnt = dict(dcnt)
        if os.environ.get("MK_VERBOSE"):
            print("SIGNALS", cnt, "DMAS", dcnt, "OPS", len(ops))
            import collections
            print("PERENG", dict(collections.Counter(o.eng for o in ops)))
        with contextlib.ExitStack() as st:
            esem = {e: st.enter_context(nc.semaphore("s_" + e)) for e in self.ENG}
            rings = {}
            for e in self.ENG:
                if dcnt[e]:
                    rings[e] = [st.enter_context(nc.semaphore("d_%s_%d" % (e, i)))
                                for i in range(min(self.RING, dcnt[e]))]
            for o in ops:
                if o.dma:
                    R = len(rings[o.eng])
                    o.dsem = rings[o.eng][o.k % R]
                    o.dval = 16 * (o.k // R + 1)
            block = st.enter_context(nc.Block())
            engobj = {"pe": "tensor", "act": "scalar", "dve": "vector", "pool": "gpsimd",
                      "sp": "sync"}
            ENG = self.ENG

            def make(ename):
                def body(eng):
                    waited = {}
                    dwaited = set()
                    for o in ops:
                        if o.eng != ename:
                            continue
                        if o.dma and o.dval > 16:
                            key = (id(o.dsem), o.dval - 16)
                            if key not in dwaited:
                                eng.wait_ge(o.dsem, o.dval - 16)
                                dwaited.add(key)
                        for d in sorted(o.deps):
                            p = ops[d]
                            if p.dma:
                                key = (id(p.dsem), p.dval)
                                if key in dwaited:
                                    continue
                                eng.wait_ge(p.dsem, p.dval)
                                dwaited.add(key)
                            else:
                                if waited.get(p.eng, 0) >= p.sigval:
                                    continue
                                eng.wait_ge(esem[p.eng], p.sigval)
                                waited[p.eng] = p.sigval
                        ins = o.fn(eng)
                        if o.dma:
                            ins.then_inc(o.dsem, 16)
                        elif o.sig:
                            ins.then_inc(esem[ename], 1)
                    if ename == "sp":
                        for e2 in ENG:
                            if dcnt[e2]:
                                R = len(rings[e2])
                                for i in range(R):
                                    n = (dcnt[e2] - 1 - i) // R + 1
                                    if n > 0:
                                        eng.wait_ge(rings[e2][i], 16 * n)
                            if cnt[e2] and e2 != "sp":
                                eng.wait_ge(esem[e2], cnt[e2])
                return body

            for e in ENG:
                getattr(block, engobj[e])(make(e))


NCF = 256
NCB = 1536
CF_G1 = 0
CF_G3 = 8
CF_CW = 16
CF_ST = 40
CF_HV = 104
CF_OH = 105
CF_EPS = 109
CF_BM = 112
CB_MASK = 0
CB_ID = 256
CB_ONE = 384
CB_ZERO = 512
CB_BLK = 1024
CB_END = 1536


def build_program():
    nc = bass.Bass("TRN2", target_bir_lowering=False)
    S = Sched(nc)
    STOP = os.environ.get("MK_STOP", "")
    ONLY = os.environ.get("MK_ONLY", "")
    if ONLY:
        S.enabled = False
    BSTOP = os.environ.get("MK_BSTOP", "")
    ESTOP = os.environ.get("MK_ESTOP", "")

    def fin():
        for i in range(int(os.environ.get("MK_DUMMY", "0"))):
            S.op("pe", lambda e: e.matmul(pb(7, 0, [[1, 8]]), lhsT=cBa(CB_ZERO, 128), rhs=cBa(CB_ZERO, 8),
                                          start=True, stop=True), reads=["cB"], writes=[("bank", 7)])
        S.build()
        return nc

    def din(name, shape, dt=F32):
        return nc.dram_tensor(name, list(shape), dt, kind="ExternalInput")

    def dout(name, shape, dt=F32):
        return nc.dram_tensor(name, list(shape), dt, kind="ExternalOutput")

    x_all = din("x_all", [NH + NT, D])
    xs_d = din("xs", [NS, D])
    c128_d = din("c128", [NS * 128, 1024])
    c512_d = din("c512", [NS * 512, 1024])
    c2048_d = din("c2048", [NS * 2048, 1024])
    cache_d = (c128_d, c512_d, c2048_d)
    st1_d = din("st1", [NS, D])
    wB_d = din("wB", [4, 128, 9216])
    wC1_d = din("wC1", [8, 128, 3072])
    wC2_d = din("wC2", [8, 128, 3584])
    wO_d = din("wO", [128, 8192])
    wF_d = din("wF", [8, 128, 8192])
    g1_d = din("g1", [1, D])
    g3_d = din("g3", [1, D])
    g2_d = din("g2", [1, D])
    g4_d = din("g4", [1, D])
    cF_d = din("constF", [128, NCF])
    cB_d = din("constB", [128, NCB], BF16)

    y_d = dout("y", [NT, D])
    ys_d = dout("ys", [NS, D])
    kvp_d = (dout("kvp128", [128, 1024]), dout("kvp512", [512, 1024]), dout("kvp2048", [2048, 1024]))
    convp_d = dout("convp", [128, 16])
    kvs_d = (dout("kvs128", [NS * 128, 1024]), dout("kvs512", [NS * 512, 1024]),
             dout("kvs2048", [NS * 2048, 1024]))
    convs0_d = dout("convs0", [NS, D])
    convs1_d = dout("convs1", [128, 32])

    def sbt(name, cols, dt, off):
        return nc.alloc_sbuf_tensor_at(name, [128, cols], dt, offset=off + 16384)

    R0 = 0
    R1 = 65600
    R2 = 171200
    R3 = 187616
    HT_F = 8 * HW
    hT = sbt("hT", HT_F, BF16, R0)
    facc = sbt("facc", 16 * 1024, F32, R0)
    o = R1
    QF = 3 * 2048
    qT = sbt("qT", QF, BF16, o); o += QF * 2
    KOFF = (0, 2176, 4736)
    LK = (2176, 640, 256)
    KF = 8832
    kT = sbt("kT", KF, BF16, o); o += KF * 2
    NBLK = 69
    VF = NBLK * 192
    Vg = sbt("Vaug", VF, BF16, o); o += VF * 2
    PT2F = 32 * 128
    PT2 = sbt("PT2", PT2F, BF16, o); o += PT2F * 2
    PT0F = 10 * 256
    PT0 = sbt("PT0", PT0F, BF16, o); o += PT0F * 2
    PT1F = 4 * 3 * 256
    PT1 = sbt("PT1", PT1F, BF16, o); o += PT1F * 2
    WBF = 9216
    wB = sbt("wB", WBF, BF16, o); o += WBF * 2
    vT = sbt("vT", 4096, BF16, o); o += 8192
    kvst_extra = [sbt("kvstx%d" % i, 256, F32, o + i * 1024) for i in range(2)]; o += 2048
    assert o <= R2, o
    CF_ = 8 * 2052
    cbvT = sbt("cbvT", CF_, BF16, R1)
    mixT = sbt("mixT", CF_, BF16, R1 + 32832)
    h2T = sbt("h2T", CF_, BF16, R1 + 65664)
    R1E = R1 + 98496
    KVs2 = [sbt("KVs%d" % i, 1024, F32, R1 + i * 4096) for i in range(2)]
    KVb2 = [sbt("KVb%d" % i, 1024, BF16, R1 + 8192 + i * 2048) for i in range(2)]
    KTs2 = [sbt("KTs%d" % i, 512, BF16, R1 + 12288 + i * 1024) for i in range(2)]
    Qbd = sbt("Qbd", 4 * 3 * 4 * 8, BF16, R1 + 14336)
    PTs2 = [sbt("PTs%d" % i, 8, BF16, R1 + 15104 + i * 32) for i in range(2)]
    PTn2 = [sbt("PTn%d" % i, 8, F32, R1 + 15168 + i * 32) for i in range(2)]
    PTnm2 = [sbt("PTnm%d" % i, 8, BF16, R1 + 15232 + i * 32) for i in range(2)]
    rdn = sbt("rdn", 1, F32, R1 + 15296)
    Om = sbt("Om", 512, BF16, R1 + 15328)
    o = R1 + 32832
    wC1 = [sbt("wC1_%d" % i, 3072, BF16, o + i * 6144) for i in range(2)]; o += 12288
    UF = 2 + 2048 + 4
    ub = [sbt("ub%d" % i, UF, F32, o + i * 8224) for i in range(2)]; o += 16448
    ccs = [sbt("ccs%d" % i, 512, F32, o + i * 2048) for i in range(2)]; o += 4096
    t1b = [sbt("t1b%d" % i, 512, F32, o + i * 2048) for i in range(2)]; o += 4096
    assert o <= R2
    o = R1 + 69888
    wC2 = [sbt("wC2_%d" % i, 3584, BF16, o + i * 7168) for i in range(2)]; o += 14336
    sab = [sbt("sa%d" % i, 512, F32, o + i * 2048) for i in range(2)]; o += 4096
    scb = [sbt("sc%d" % i, 512, F32, o + i * 2048) for i in range(2)]; o += 4096
    tab = [sbt("ta%d" % i, 512, F32, o + i * 2048) for i in range(2)]; o += 4096
    tcb = [sbt("tc%d" % i, 512, F32, o + i * 2048) for i in range(2)]; o += 4096
    assert o <= R2
    o = R0
    gB2 = sbt("gB2", 1024, F32, o); o += 4096
    xtD = [sbt("xtD%d" % i, 1024, F32, o + i * 4096) for i in range(4)]; o += 16384
    tDb = [sbt("tD%d" % i, 1024, F32, o + i * 4096) for i in range(3)]; o += 12288
    x1t = [sbt("x1t%d" % i, 1024, F32, o + i * 4096) for i in range(4)]; o += 16384
    xn2 = [sbt("xn2_%d" % i, 1024, BF16, o + i * 2048) for i in range(4)]; o += 8192
    junkD = sbt("junkD", 1024, BF16, o); o += 2048
    gB3 = sbt("gB3", 1024, F32, o); o += 4096
    assert o <= R1
    o = R1
    wF = [sbt("wF0", 8192, BF16, R2), sbt("wF1", 8192, BF16, o)]; o += 16384
    f1T = sbt("f1T", 4 * 2052, BF16, o); o += 16416
    gB4 = sbt("gB4", 1024, F32, o); o += 4096
    xrE = [sbt("xrE%d" % i, 1024, F32, o + i * 4096) for i in range(2)]; o += 8192
    tE = sbt("tE", 1024, F32, o); o += 4096
    ytE = [sbt("ytE%d" % i, 1024, F32, o + i * 4096) for i in range(2)]; o += 8192
    faccs = sbt("faccs", 1024, F32, o); o += 4096
    junkE = sbt("junkE", 1024, BF16, o); o += 2048
    assert o <= R1 + 65664
    rlb = [sbt("rl%d" % i, 512, F32, R1E + i * 2048) for i in range(2)]
    assert R1E + 4096 <= R2
    AT_F = 4 * 2052
    attnT = sbt("attnT", AT_F, BF16, R2)
    o = R3
    cF = sbt("cF", NCF, F32, o); o += NCF * 4
    cB = sbt("cB", NCB, BF16, o); o += NCB * 2
    stat2 = sbt("stat2", 256, F32, o); o += 1024
    qkvs = sbt("qkvs", 4 * 6 * 4, BF16, o); o += 192
    vnb = sbt("vnb", 3 * 512, BF16, o); o += 3072
    convp_sb = sbt("convp_sb", 16, F32, o); o += 64
    convs_sb = sbt("convs_sb", 32, F32, o); o += 128
    o3 = o
    RA = R1 + 12288 + 17664 + 26496
    xt = [sbt("xt%d" % i, 1024, F32, R2 + i * 4096) for i in range(4)] + [sbt("xt4", 1024, F32, RA + 12288)]
    xn = [sbt("xn%d" % i, 1024, BF16, RA + i * 2048) for i in range(3)]
    junk = sbt("junk", 1024, BF16, RA + 6144)
    gB1 = sbt("gB1", 1024, F32, RA + 8192)
    assert 16384 <= 19456
    kvst = [sbt("kvst%d" % i, 256, F32, o3 + i * 1024) for i in range(2)] + kvst_extra
    dnb = sbt("dnb", 512, F32, o3 + 2048)
    hib = sbt("hib", 512, BF16, o3 + 4096)
    lob = sbt("lob", 512, BF16, o3 + 5120)
    bcs = sbt("bcs", 512, F32, o3 + 6144)
    maskh = sbt("maskh", 128, BF16, o3 + 8192)
    kvn_st = sbt("kvn_st", 768, F32, o3 + 8448)
    vTb = sbt("vTb", 2560, BF16, o3 + 11520)
    assert o3 + 11520 + 5120 <= 212992
    assert 8448 + 3072 <= 12288
    wo = sbt("wo", 8192, BF16, o3)
    assert o3 + 16384 <= 212992, o3

    PS = [nc.alloc_psum_tensor("ps%d" % i, [128, 1024], F32) for i in range(4)]
    PSB = [p.bitcast(BF16) for p in PS]

    def pb(bank, col, dims, p0=0, np_=128):
        return AP(PS[bank // 2], p0 * 1024 + (bank % 2) * 512 + col, [[1024, np_]] + dims)

    def pbb(bank, col, dims, p0=0, np_=128):
        return AP(PSB[bank // 2], p0 * 2048 + (bank % 2) * 1024 + col, [[2048, np_]] + dims)

    def sb(t, col, dims, p0=0, np_=128):
        F = t.shape[1]
        return AP(t, p0 * F + col, [[F, np_]] + dims)

    def hTa(kc, col, dims, np_=128):
        return sb(hT, kc * HW + col, dims, 0, np_)

    def cFa(col, n=1, p0=0, np_=128):
        return sb(cF, col, [[1, n]], p0, np_)

    def cBa(col, n, p0=0, np_=128):
        return sb(cB, col, [[1, n]], p0, np_)

    ident = lambda n: sb(cB, CB_ID, [[1, n]], 0, n)
    eps_ap = lambda n: cFa(CF_EPS, 1, 0, n)

    def hT_tiles(col0, n, stride=1):
        lo = col0 // 128
        hi = (col0 + (n - 1) * stride) // 128
        return [("hT", t) for t in range(lo, hi + 1)]

    def load_wB(hp):
        S.dma("pool", lambda e, hp=hp: e.dma_start(
            out=sb(wB, 0, [[1536, 6], [1, 1536]]),
            in_=AP(wB_d, hp * 128 * 9216, [[9216, 128], [1536, 6], [1, 1536]])),
            writes=["wB"])

    load_wB(0)
    S.dma("sp", lambda e: e.dma_start(out=cF[:], in_=cF_d.ap()), writes=["cF"])
    S.dma("sp", lambda e: e.dma_start(out=cB[:], in_=cB_d.ap()), writes=["cB"])
    S.dma("sp", lambda e: e.dma_start(out=convs0_d.ap(), in_=st1_d.ap()))

    cur_junk = [junk]

    def norm_p1(src_ap_fn, src_res, np_, xnb, xn_res, gB, gB_res, statcol):
        jk = cur_junk[0]
        ss = sb(stat2, statcol, [[1, 1]], 0, np_)
        sd = sb(stat2, statcol + 1, [[1, 1]], 0, np_)
        rs = sb(stat2, statcol + 2, [[1, 1]], 0, np_)
        kss = ("st", statcol)
        S.op("act", lambda e: e.activation(out=sb(jk, 0, [[1, 1024]], 0, np_), in_=src_ap_fn(),
                                           func=AF.Square, accum_out=ss),
             reads=[src_res], writes=[kss, "junk"])
        S.op("act", lambda e: e.activation(out=sd, in_=ss, func=AF.Sqrt, bias=eps_ap(np_),
                                           scale=1.0 / D),
             reads=[kss, "cF"], writes=[("sd", statcol)])
        S.op("dve", lambda e: e.reciprocal(out=rs, in_=sd), reads=[("sd", statcol)],
             writes=[("rs", statcol)])
        S.op("dve", lambda e: e.scalar_tensor_tensor(out=sb(xnb, 0, [[1, 1024]], 0, np_), in0=src_ap_fn(),
                                                     scalar=rs, in1=sb(gB, 0, [[1, 1024]], 0, np_),
                                                     op0=ALU.mult, op1=ALU.mult),
             reads=[src_res, ("rs", statcol), gB_res], writes=[xn_res])

    D_P2_COUNT = [0]

    def norm_p2(np_, xnb, xn_res, tpbank, dst_fn, dst_res, idx):
        if dst_res[0] == "h2T":
            D_P2_COUNT[0] += 1
        for c in range(8):
            S.op("pe", lambda e, c=c: e.transpose(out=pbb(tpbank, c * 128, [[1, np_]]),
                                                  in_=sb(xnb, c * 128, [[1, 128]], 0, np_),
                                                  identity=ident(np_)),
                 reads=[xn_res, "cB"], writes=[("bank", tpbank)])
        if idx % 2 == 0:
            S.op("act", lambda e: e.activation(out=dst_fn(), in_=pbb(tpbank, 0, [[128, 8], [1, np_]]),
                                               func=AF.Copy),
                 reads=[("bank", tpbank)], writes=[dst_res])
        else:
            S.op("dve", lambda e: e.tensor_copy(out=dst_fn(), in_=pbb(tpbank, 0, [[128, 8], [1, np_]])),
                 reads=[("bank", tpbank)], writes=[dst_res])

    S.dma("sp", lambda e: e.dma_start(out=gB1[:], in_=AP(g1_d, 0, [[0, 128], [1, D]])), writes=["gB1"])
    a_done = []

    def emit_A_tile(ti):
        it_ = len(a_done)
        a_done.append(ti)
        np_ = 128 if ti < 32 else NS
        b = it_ % 3
        bx = it_ % 5
        if ti < 32:
            src = AP(x_all, ti * 128 * D, [[D, 128], [1, D]])
        else:
            src = xs_d.ap()
        S.dma("sp", lambda e: e.dma_start(out=sb(xt[bx], 0, [[1, 1024]], 0, np_), in_=src),
              writes=[("xt", bx)])
        norm_p1(lambda: sb(xt[bx], 0, [[1, 1024]], 0, np_), ("xt", bx), np_,
                xn[b], ("xn", b), gB1, "gB1", (it_ % 8) * 4)
        norm_p2(np_, xn[b], ("xn", b), 6 + it_ % 2,
                lambda: hTa(0, ti * 128, [[HW, 8], [1, np_]]), ("hT", ti), it_)

    A_ORDER = list(range(16, 32)) + [15, 12, 13, 14] + list(range(12)) + [32]
    A_LOOK = int(os.environ.get("MK_ALOOK", "6"))

    def need_tiles(res_list):
        for r in res_list:
            if isinstance(r, tuple) and r[0] == "hT":
                pos = A_ORDER.index(r[1])
                for t in A_ORDER[:min(len(A_ORDER), pos + 1 + A_LOOK)]:
                    if t not in a_done:
                        emit_A_tile(t)

    if STOP == "A" or ONLY:
        for ti in range(33):
            need_tiles([("hT", ti)])
    if STOP == "A":
        return fin()
    if ONLY == "B":
        S.enabled = True
    S.op("dve", lambda e: e.tensor_scalar(out=maskh[:], in0=cBa(CB_MASK + 128, 128),
                                          scalar1=cFa(CF_HV), scalar2=None, op0=ALU.mult),
         reads=["cF", "cB"], writes=["maskh"])

    S.op("pool", lambda e: e.memset(sb(Vg, 64, [[192, NBLK], [1, 64]]), 1.0), writes=["Vones"])

    blocks = []
    vi = 0
    for g in range(3):
        for rho in range(RS[g]):
            for kb in range(NQ[g] + 1):
                blocks.append((g, rho, kb, vi))
                vi += 1
    assert vi == NBLK
    VI = {(g, rho, kb): v for (g, rho, kb, v) in blocks}
    all_own = [("hT", t) for t in range(16, 32)]
    sc_slot = [0]
    proj_bank = [0]
    kv_bank = [0]

    def evac(i, fn_out, fn_in, reads, writes):
        if i % 2 == 0:
            S.op("act", lambda e: e.activation(out=fn_out(), in_=fn_in(), func=AF.Copy),
                 reads=reads, writes=writes)
        else:
            S.op("dve", lambda e: e.tensor_copy(out=fn_out(), in_=fn_in()), reads=reads, writes=writes)

    def issue_cache_copies():
        for g in range(3):
            W = WIN[g]
            for b in range(NS):
                if os.environ.get("MK_NOCOPY"):
                    continue
                n = (W - 1) * 64
                S.dma("act", lambda e, g=g, b=b, W=W, n=n: e.dma_start(
                    out=AP(kvs_d[g], b * W * 1024, [[n, 16], [1, n]]),
                    in_=AP(cache_d[g], b * W * 1024 + 1024, [[n, 16], [1, n]])),
                    writes=[("kvs", g, b)])

    ring_owner = {}
    kvst_i = [0]
    ev_i = [0]
    BSKIP = os.environ.get("MK_BSKIP", "")
    for hp in range(int(os.environ.get("MK_HPS", "4"))):
        def proj(blkcol, hcol, n, out_fn, in_dims, reads, writes):
            need_tiles(reads)
            bank = proj_bank[0] % 2
            proj_bank[0] += 1
            for kc in range(8):
                S.op("pe", lambda e, kc=kc: e.matmul(pb(bank, 0, [[1, n]]),
                                                     lhsT=sb(wB, kc * 1152 + blkcol, [[1, 128]]),
                                                     rhs=hTa(kc, hcol, [[1, n]]),
                                                     start=(kc == 0), stop=(kc == 7)),
                     reads=["wB"] + reads, writes=[("bank", bank)])
            ev_i[0] += 1
            evac(ev_i[0], out_fn, lambda: pb(bank, 0, in_dims), [("bank", bank)], writes)

        for w in range(4):
            hres = hT_tiles(NH + 512 * w, 512)
            for g in range(3):
                r = RS[g]
                proj(g * 128, NH + 512 * w, 512,
                     lambda g=g, r=r, w=w: sb(qT, g * 2048 + 512 * w // r, [[2048 // r, r], [1, 512 // r]]),
                     [[1, r], [r, 512 // r]], hres, [("qT", g, w)])
                proj(384 + g * 128, NH + 512 * w, 512,
                     lambda g=g, r=r, w=w: sb(kT, KOFF[g] + 128 + 512 * w // r, [[LK[g], r], [1, 512 // r]]),
                     [[1, r], [r, 512 // r]], hres, [("kT", g, w)])

        def v_list(g):
            r = RS[g]
            vTg = vTb if g == 1 else vT
            vid = 1 if g == 1 else 0
            lst = []
            for w in range(4):
                lst.append(lambda w=w: proj(
                    768 + g * 128, NH + 512 * w, 512,
                    lambda: sb(vTg, 128 + 512 * w // r, [[LK[g], r], [1, 512 // r]]),
                    [[1, r], [r, 512 // r]], hT_tiles(NH + 512 * w, 512), [("vT", vid, w)]))
            if g == 0:
                lst.append(lambda: proj(768, NH - 128, 128, lambda: sb(vT, 0, [[1, 128]]), [[1, 128]],
                                        hT_tiles(NH - 128, 128), [("vTh", vid, 0)]))
            elif g == 1:
                lst.append(lambda: proj(768 + 128, NH - 512, 512, lambda: sb(vTb, 0, [[640, 4], [1, 128]]),
                                        [[1, 4], [4, 128]], hT_tiles(NH - 512, 512), [("vTh", vid, 0)]))
            else:
                for hw in range(4):
                    lst.append(lambda hw=hw: proj(
                        768 + 256, 512 * hw, 512, lambda: sb(vT, 32 * hw, [[256, 16], [1, 32]]),
                        [[1, 16], [16, 32]], hT_tiles(512 * hw, 512), [("vTh", vid, hw)]))
            return lst

        kh_list = [
            lambda: proj(384, NH - 128, 128, lambda: sb(kT, KOFF[0], [[1, 128]]), [[1, 128]],
                         hT_tiles(NH - 128, 128), [("kTh", 0, 0)]),
            lambda: proj(384 + 128, NH - 512, 512, lambda: sb(kT, KOFF[1], [[640, 4], [1, 128]]),
                         [[1, 4], [4, 128]], hT_tiles(NH - 512, 512), [("kTh", 1, 0)])]
        for hw in range(4):
            kh_list.append(lambda hw=hw: proj(
                384 + 256, 512 * hw, 512, lambda: sb(kT, KOFF[2] + 32 * hw, [[256, 16], [1, 32]]),
                [[1, 16], [16, 32]], hT_tiles(512 * hw, 512), [("kTh", 2, hw)]))

        def tblock(g, rho, kb, v):
            r = RS[g]
            vTg = vTb if g == 1 else vT
            vid = 1 if g == 1 else 0
            vres = [("vT", vid, w) for w in range(4)] + [("vTh", vid, hw) for hw in range(4)]
            kres_ = [("kT", g, w) for w in range(4)]
            needk = (kb == NQ[g])
            bank = 2 + kv_bank[0] % 4
            kv_bank[0] += 1
            ccol = rho * LK[g] + 128 * kb
            vcol = 128 if needk else 0
            if needk:
                S.op("pe", lambda e: e.transpose(
                    out=pbb(bank, 0, [[1, 128]]), in_=sb(kT, KOFF[g] + ccol, [[1, 128]]), identity=ident(128)),
                    reads=kres_ + ["cB"], writes=[("bank", bank)])
            S.op("pe", lambda e: e.transpose(
                out=pbb(bank, vcol, [[1, 128]]), in_=sb(vTg, ccol, [[1, 128]]), identity=ident(128)),
                reads=vres + ["cB"], writes=[("bank", bank)])
            S.op("dve", lambda e: e.tensor_copy(
                out=sb(Vg, v * 192, [[128, 2], [1, 64]]), in_=pbb(bank, vcol, [[64, 2], [1, 64]])),
                reads=[("bank", bank), "Vones"], writes=[("V", v)])
            if needk:
                sbuf_i = kvst_i[0] % 4
                kvst_i[0] += 1
                S.op("act", lambda e: e.activation(
                    out=kvst[sbuf_i][:], in_=pbb(bank, 0, [[1, 256]]), func=AF.Copy),
                    reads=[("bank", bank)], writes=[("kvst", sbuf_i)])
                S.dma("sp", lambda e: e.dma_start(
                    out=AP(kvp_d[g], rho * 1024 + hp * 128, [[r * 1024, 128], [512, 2], [1, 128]]),
                    in_=sb(kvst[sbuf_i], 0, [[128, 2], [1, 128]])),
                    reads=[("kvst", sbuf_i)])

        def interleave(plist, blist):
            per = -(-len(blist) // max(1, len(plist)))
            for i, pj in enumerate(plist):
                pj()
                for bl in blist[i * per:(i + 1) * per]:
                    tblock(*bl)
            for bl in blist[len(plist) * per:]:
                tblock(*bl)

        for pj in v_list(0):
            pj()
        interleave(v_list(1), [bl for bl in blocks if bl[0] == 0])
        interleave(v_list(2), [bl for bl in blocks if bl[0] == 1])
        interleave(kh_list, [bl for bl in blocks if bl[0] == 2])
        if BSTOP == "proj":
            return fin()
        need_tiles([("hT", 32)])
        bank = proj_bank[0] % 2
        proj_bank[0] += 1
        for blk in range(6):
            for kc in range(8):
                S.op("pe", lambda e, kc=kc, blk=blk: e.matmul(
                    pb(bank, blk * 4, [[1, 4]]), lhsT=sb(wB, kc * 1152 + blk * 128, [[1, 128]]),
                    rhs=hTa(kc, NH + NT, [[1, 4]]), start=(kc == 0), stop=(kc == 7)),
                    reads=["wB", ("hT", 32)], writes=[("bank", bank)])
        S.op("act", lambda e, hp=hp, bank=bank: e.activation(out=sb(qkvs, hp * 24, [[1, 24]]),
                                                          in_=pb(bank, 0, [[1, 24]]), func=AF.Copy),
             reads=[("bank", bank)], writes=[("qkvs", hp)])
        bankK = 2 + kv_bank[0] % 2
        kv_bank[0] += 1
        bankV = 2 + kv_bank[0] % 2
        kv_bank[0] += 1
        for (bk, c0) in ((bankK, 384), (bankV, 768)):
            for kc in range(8):
                S.op("pe", lambda e, kc=kc, bk=bk, c0=c0: e.matmul(
                    pb(bk, 0, [[1, 384]], 0, NS), lhsT=hTa(kc, NH + NT, [[1, 4]]),
                    rhs=sb(wB, kc * 1152 + c0, [[1, 384]]), start=(kc == 0), stop=(kc == 7)),
                    reads=["wB", ("hT", 32)], writes=[("bank", bk)])
        S.op("act", lambda e, bankK=bankK: e.activation(out=sb(kvn_st, 0, [[1, 384]], 0, NS),
                                                        in_=pb(bankK, 0, [[1, 384]], 0, NS), func=AF.Copy),
             reads=[("bank", bankK)], writes=["kvn_k"])
        S.op("dve", lambda e, bankV=bankV: e.tensor_copy(out=sb(kvn_st, 384, [[1, 384]], 0, NS),
                                                         in_=pb(bankV, 0, [[1, 384]], 0, NS)),
             reads=[("bank", bankV)], writes=["kvn_v"])
        S.op("dve", lambda e, bankV=bankV, hp=hp: e.tensor_copy(
            out=sb(vnb, hp * 128, [[512, 3], [1, 128]], 0, NS), in_=pb(bankV, 0, [[128, 3], [1, 128]], 0, NS)),
            reads=[("bank", bankV)], writes=[("vnb", hp)])
        for g in range(3):
            W = WIN[g]
            S.dma("sp", lambda e, g=g, W=W, hp=hp: e.dma_start(
                out=AP(kvs_d[g], (W - 1) * 1024 + hp * 128, [[W * 1024, NS], [512, 2], [1, 128]]),
                in_=sb(kvn_st, g * 128, [[384, 2], [1, 128]], 0, NS)),
                reads=["kvn_k", "kvn_v"])


        if BSTOP == "smp":
            return fin()
        if hp == 0:
            assert len(a_done) == 33, a_done
            S.barrier()
            issue_cache_copies()
        if hp + 1 < 4:
            load_wB(hp + 1)
        def score(g, rho, kb, X):
            r = RS[g]
            nq = NQ[g]
            slot = sc_slot[0] % 5
            sc_slot[0] += 1
            bank = slot
            pcol = 0
            kcol = KOFF[g] + rho * LK[g] + 128 * kb
            if kb == 0:
                qcol, n, half0 = 0, 128, 1
            elif kb == nq:
                qcol, n, half0 = 128 * (nq - 1), 128, 0
            else:
                qcol, n, half0 = 128 * (kb - 1), 256, 0
            qcol += g * 2048 + rho * (2048 // r)
            qw = [("qT", g, w) for w in range(4)]
            kres = [("kT", g, w) for w in range(4)] + [("kTh", g, hw) for hw in range(4)]
            S.op("pe", lambda e: e.matmul(
                pb(bank, pcol, [[1, n]]), lhsT=sb(kT, kcol, [[1, 128]], 64 * X, 64),
                rhs=sb(qT, qcol, [[1, n]], 64 * X, 64), start=True, stop=True),
                reads=qw + kres, writes=[("bank", bank)])
            if g == 2:
                t, tcol = PT2, (rho * 2 + kb) * 128
                res = ("PT2", rho, kb)
            elif g == 1:
                t, tcol = PT1, (rho * 3 + kb % 3) * 256 + half0 * 128
                res = ("PT1", rho, kb % 3)
            else:
                t, tcol = PT0, (kb % 10) * 256 + half0 * 128
                res = ("PT0", kb % 10)
            ring_owner[res] = kb
            S.op("act", lambda e: e.activation(
                out=sb(t, tcol, [[1, n]]), in_=pb(bank, pcol, [[1, n]]), func=AF.Exp, scale=0.125),
                reads=[("bank", bank), "Vones"], writes=[res])
            if kb == 0:
                mk = lambda: maskh[:]
                mres = ["maskh"]
            else:
                mk = lambda: cBa(CB_MASK, n)
                mres = ["cB"]
            S.op("dve", lambda e: e.tensor_tensor(
                out=sb(t, tcol, [[1, n]]), in0=sb(t, tcol, [[1, n]]), in1=mk(), op=ALU.mult),
                reads=[res] + mres, writes=[res])

        def ptap(g, rho, kb, half, c0, n):
            if g == 2:
                return sb(PT2, (rho * 2 + kb) * 128 + c0, [[1, n]]), ("PT2", rho, kb)
            if g == 1:
                assert ring_owner[("PT1", rho, kb % 3)] == kb
                return (sb(PT1, (rho * 3 + kb % 3) * 256 + half * 128 + c0, [[1, n]]), ("PT1", rho, kb % 3))
            assert ring_owner[("PT0", kb % 10)] == kb
            return sb(PT0, (kb % 10) * 256 + half * 128 + c0, [[1, n]]), ("PT0", kb % 10)

        for X in range(0 if "sc" in BSKIP else 2):
            dp = 64 if X == 0 else 0
            u0 = 0 if X == 0 else 64

            def Sw(w, X=X):
                lo_kb = 0 if w == 0 else 4 * w + 1
                for kb in range(lo_kb, 4 * w + 5):
                    score(0, 0, kb, X)
                for rho in range(4):
                    if w == 0:
                        score(1, rho, 0, X)
                    score(1, rho, w + 1, X)

            def PVw(w, X=X):
                ab = 6 + w % 2
                ares = ("bank", ab)
                S.op("pe", lambda e: e.matmul(pb(ab, 0, [[1, 512]]), lhsT=cBa(CB_ZERO, 128),
                                              rhs=cBa(CB_ZERO, 512), start=True, stop=False),
                     reads=["cB"], writes=[ares])

                def pv(g, rho, kb, half, c0, n, out_ap, last=False):
                    rhs, res = ptap(g, rho, kb, half, c0, n)
                    v = VI[(g, rho, kb)]
                    S.op("pe", lambda e: e.matmul(out_ap, lhsT=sb(Vg, v * 192 + 64 * X, [[1, 128]]), rhs=rhs,
                                                  start=False, stop=last),
                         reads=[res, ("V", v), "Vones"], writes=[ares])

                for qb in range(4 * w, 4 * w + 4):
                    oc = 128 * (qb - 4 * w)
                    pv(0, 0, qb + 1, 0, 0, 128, pb(ab, oc, [[1, 128]]))
                    pv(0, 0, qb, 1, 0, 128, pb(ab, oc, [[1, 128]]))
                for rho in range(4):
                    pv(1, rho, w + 1, 0, 0, 128, pb(ab, rho, [[4, 128]]))
                    pv(1, rho, w, 1, 0, 128, pb(ab, rho, [[4, 128]]))
                for rho in range(16):
                    pv(2, rho, 1, 0, 32 * w, 32, pb(ab, rho, [[16, 32]]))
                    pv(2, rho, 0, 0, 32 * w, 32, pb(ab, rho, [[16, 32]]), last=(rho == 15))

            def N1(w):
                ab = 6 + w % 2
                ares = ("bank", ab)
                S.op("act", lambda e: e.activation(out=sb(dnb, 0, [[1, 512]], dp, 1),
                                                   in_=pb(ab, 0, [[1, 512]], dp, 1), func=AF.Ln),
                     reads=[ares], writes=["dn"])
                S.op("act", lambda e: e.activation(out=sb(dnb, 0, [[1, 512]], dp, 1),
                                                   in_=sb(dnb, 0, [[1, 512]], dp, 1), func=AF.Exp, scale=-1.0),
                     reads=["dn"], writes=["dn"])
                S.op("dve", lambda e: e.tensor_copy(out=sb(hib, 0, [[1, 512]], dp, 1),
                                                    in_=sb(dnb, 0, [[1, 512]], dp, 1)),
                     reads=["dn"], writes=["hi"])
                S.op("dve", lambda e: e.tensor_tensor(out=sb(lob, 0, [[1, 512]], dp, 1),
                                                      in0=sb(dnb, 0, [[1, 512]], dp, 1),
                                                      in1=sb(hib, 0, [[1, 512]], dp, 1), op=ALU.subtract),
                     reads=["dn", "hi"], writes=["lo"])

            def N2(w, X=X):
                ab = 6 + w % 2
                ares = ("bank", ab)
                bb = 5
                S.op("pe", lambda e: e.matmul(pb(bb, 0, [[1, 512]]), lhsT=cBa(CB_ONE, 128, dp, 1),
                                              rhs=sb(hib, 0, [[1, 512]], dp, 1), start=True, stop=False),
                     reads=["hi", "cB"], writes=[("bank", bb)])
                S.op("pe", lambda e: e.matmul(pb(bb, 0, [[1, 512]]), lhsT=cBa(CB_ONE, 128, dp, 1),
                                              rhs=sb(lob, 0, [[1, 512]], dp, 1), start=False, stop=True),
                     reads=["lo", "cB"], writes=[("bank", bb)])
                S.op("act", lambda e: e.activation(out=sb(bcs, 0, [[1, 512]], u0, 64),
                                                   in_=pb(bb, 0, [[1, 512]], u0, 64), func=AF.Copy),
                     reads=[("bank", bb)], writes=["bcs"])
                S.op("dve", lambda e: e.tensor_tensor(
                    out=sb(attnT, hp * 2052 + 512 * w, [[1, 512]], u0, 64),
                    in0=pb(ab, 0, [[1, 512]], u0, 64), in1=sb(bcs, 0, [[1, 512]], u0, 64), op=ALU.mult),
                    reads=[ares, "bcs"], writes=[("attnT", hp, w, X)])

            g2_tiles = [(rho, kb) for rho in range(16) for kb in range(2)]
            n_pre = 10 if X == 1 else 0
            for (rho, kb) in g2_tiles[n_pre:]:
                score(2, rho, kb, X)
            Sw(0)
            Sw(1)
            PVw(0); N1(0); Sw(2); N2(0)
            PVw(1); N1(1); Sw(3); N2(1)
            PVw(2); N1(2); PVw(3); N2(2)
            N1(3)
            if X == 0:
                for (rho, kb) in g2_tiles[:10]:
                    score(2, rho, kb, 1)
            N2(3)

    if STOP == "B":
        return fin()
    if ONLY == "Bs":
        S.enabled = True
    S.barrier()
    for b in range(NS):
        S.op("dve", lambda e, b=b: e.tensor_tensor(
            out=sb(Qbd, b * 8, [[96, 4], [32, 3], [1, 8]]),
            in0=sb(cF, CF_BM, [[8, 4], [0, 3], [1, 8]]),
            in1=sb(qkvs, b, [[24, 4], [4, 3], [0, 8]]), op=ALU.mult),
            reads=[("qkvs", hp) for hp in range(4)] + ["cF"], writes=[("Qbd", b)])
    for b in range(NS):
        for g in range(3):
            W, dil = WIN[g], RS[g]
            si = (b * 3 + g) % 2
            KVs, KVb, KTs, PTs, PTn, PTnm = KVs2[si], KVb2[si], KTs2[si], PTs2[si], PTn2[si], PTnm2[si]
            tb_, sbk = (0, 1) if si == 0 else (5, 6)
            S.dma("sp", lambda e, g=g, b=b, W=W, dil=dil: e.dma_start(
                out=KVs[:], in_=AP(cache_d[g], b * W * 1024, [[dil * 1024, 128], [1, 1024]])),
                writes=[("KVs", si)])
            S.op("dve", lambda e: e.tensor_copy(out=KVb[:], in_=KVs[:]), reads=[("KVs", si)], writes=[("KVb", si)])
            for c in range(4):
                S.op("pe", lambda e, c=c: e.transpose(out=pbb(tb_, c * 128, [[1, 128]]),
                                                      in_=sb(KVb, c * 128, [[1, 128]]), identity=ident(128)),
                     reads=[("KVb", si), "cB"], writes=[("bank", tb_)])
            S.op("act", lambda e: e.activation(out=KTs[:], in_=pbb(tb_, 0, [[1, 512]]), func=AF.Copy),
                 reads=[("bank", tb_)], writes=[("KTs", si)])
            for c in range(4):
                S.op("pe", lambda e, c=c, g=g, b=b: e.matmul(
                    pb(sbk, 0, [[1, 8]]), lhsT=sb(KTs, c * 128, [[1, 128]]),
                    rhs=sb(Qbd, c * 96 + g * 32 + b * 8, [[1, 8]]), start=(c == 0), stop=(c == 3)),
                    reads=[("KTs", si), ("Qbd", b)], writes=[("bank", sbk)])
            for c in range(4):
                S.op("pe", lambda e, c=c, g=g, b=b: e.matmul(
                    pb(sbk, 8, [[1, 8]], 0, NS), lhsT=sb(qkvs, c * 24 + (3 + g) * 4, [[1, 4]]),
                    rhs=sb(Qbd, c * 96 + g * 32 + b * 8, [[1, 8]]), start=(c == 0), stop=(c == 3)),
                    reads=[("qkvs", hp) for hp in range(4)] + [("Qbd", b)], writes=[("bank", sbk)])
            S.op("act", lambda e: e.activation(out=PTs[:], in_=pb(sbk, 0, [[1, 8]]), func=AF.Exp, scale=0.125),
                 reads=[("bank", sbk)], writes=[("PTs", si)])
            S.op("act", lambda e: e.activation(out=sb(PTn, 0, [[1, 8]], 0, NS), in_=pb(sbk, 8, [[1, 8]], 0, NS),
                                               func=AF.Exp, scale=0.125),
                 reads=[("bank", sbk)], writes=[("PTn", si)])
            S.op("dve", lambda e, b=b: e.tensor_scalar(out=sb(PTnm, 0, [[1, 8]], 0, NS),
                                                       in0=sb(PTn, 0, [[1, 8]], 0, NS),
                                                       scalar1=cFa(CF_OH + b, 1, 0, NS), scalar2=None, op0=ALU.mult),
                 reads=[("PTn", si), "cF"], writes=[("PTnm", si)])
            S.op("pe", lambda e, g=g: e.matmul(pb(2, 0, [[1, 512]], 0, 8), lhsT=PTs[:],
                                               rhs=sb(KVb, 512, [[1, 512]]), start=(g == 0), stop=False),
                 reads=[("PTs", si), ("KVb", si)], writes=[("bank", 2)])
            S.op("pe", lambda e, g=g: e.matmul(pb(2, 0, [[1, 512]], 0, 8), lhsT=sb(PTnm, 0, [[1, 8]], 0, NS),
                                               rhs=sb(vnb, g * 512, [[1, 512]], 0, NS), start=False, stop=(g == 2)),
                 reads=[("PTnm", si)] + [("vnb", hp) for hp in range(4)], writes=[("bank", 2)])
            S.op("pe", lambda e, g=g: e.matmul(pb(3, 0, [[1, 1]], 0, 8), lhsT=PTs[:], rhs=cBa(CB_ONE, 1),
                                               start=(g == 0), stop=False),
                 reads=[("PTs", si), "cB"], writes=[("bank", 3)])
            S.op("pe", lambda e, g=g: e.matmul(pb(3, 0, [[1, 1]], 0, 8), lhsT=sb(PTnm, 0, [[1, 8]], 0, NS),
                                               rhs=cBa(CB_ONE, 1, 0, NS), start=False, stop=(g == 2)),
                 reads=[("PTnm", si), "cB"], writes=[("bank", 3)])
        S.op("dve", lambda e: e.reciprocal(out=sb(rdn, 0, [[1, 1]], 0, 8), in_=pb(3, 0, [[1, 1]], 0, 8)),
             reads=[("bank", 3)], writes=["rdn"])
        S.op("dve", lambda e: e.scalar_tensor_tensor(out=sb(Om, 0, [[1, 512]], 0, 8), in0=pb(2, 0, [[1, 512]], 0, 8),
                                                     scalar=sb(rdn, 0, [[1, 1]], 0, 8),
                                                     in1=cBa(CB_BLK, 512, 0, 8), op0=ALU.mult, op1=ALU.mult),
             reads=[("bank", 2), "rdn", "cB"], writes=["Om"])
        for c in range(4):
            S.op("pe", lambda e, c=c: e.matmul(pb(4, c, [[1, 1]]), lhsT=sb(Om, c * 128, [[1, 128]], 0, 8),
                                               rhs=cBa(CB_ONE, 1, 0, 8), start=True, stop=True),
                 reads=["Om", "cB"], writes=[("bank", 4)])
        S.op("act", lambda e, b=b: e.activation(out=sb(attnT, 2048 + b, [[2052, 4]]), in_=pb(4, 0, [[1, 4]]),
                                                func=AF.Copy),
             reads=[("bank", 4)], writes=[("attnTs", b)])

    if STOP == "Bs":
        return fin()
    if ONLY == "C1":
        S.enabled = True
    S.barrier()
    def load_wC2(oc):
        S.dma("pool", lambda e: e.dma_start(
            out=sb(wC2[oc % 2], 0, [[1792, 2], [1, 1792]]),
            in_=AP(wC2_d, oc * 128 * 3584, [[3584, 128], [1792, 2], [1, 1792]])), writes=[("wC2", oc % 2)])

    def load_wC1(fc):
        S.dma("pool", lambda e: e.dma_start(
            out=sb(wC1[fc % 2], 0, [[1536, 2], [1, 1536]]),
            in_=AP(wC1_d, fc * 128 * 3072, [[3072, 128], [1536, 2], [1, 1536]])), writes=[("wC1", fc % 2)])

    wins = [(NH - 2, 2, "h")] + [(NH + 512 * w, 512, w) for w in range(4)] + [(NH + NT, NS, "s")]
    wi = 0
    for fc in range(8):
        wb = wC1[fc % 2]
        if fc == 0:
            load_wC1(0)
        if fc + 1 < 8:
            load_wC1(fc + 1)
        if fc == 6:
            load_wC2(0)
        u = ub[fc % 2]
        ures = ("u", fc % 2)
        for (hcol, n, kind) in wins:
            bks = {0: wi % 3, 1: 3 + wi % 2, 2: 5 + wi % 2}
            wi += 1
            blks = (1, 2) if kind == "h" else (1, 2, 0)
            for bi in blks:
                for kc in range(8):
                    S.op("pe", lambda e, kc=kc, bi=bi, bk=bks[bi], hcol=hcol, n=n, wb=wb: e.matmul(
                        pb(bk, 0, [[1, n]]), lhsT=sb(wb, kc * 384 + bi * 128, [[1, 128]]),
                        rhs=hTa(kc, hcol, [[1, n]]), start=(kc == 0), stop=(kc == 7)),
                        reads=[("wC1", fc % 2)] + hT_tiles(hcol, n), writes=[("bank", bks[bi])])
            cs = ccs[wi % 2]
            csr = ("ccs", wi % 2)
            S.op("act", lambda e, cs=cs, bk=bks[1], n=n: e.activation(out=sb(cs, 0, [[1, n]]), in_=pb(bk, 0, [[1, n]]),
                                                                func=AF.Copy),
                 reads=[("bank", bks[1])], writes=[csr])
            ucol = 0 if kind == "h" else (2050 if kind == "s" else 2 + 512 * kind)
            uw = (ures, kind)
            S.op("dve", lambda e, u=u, ucol=ucol, n=n, bk=bks[2], cs=cs: e.tensor_tensor(
                out=sb(u, ucol, [[1, n]]), in0=pb(bk, 0, [[1, n]]), in1=sb(cs, 0, [[1, n]]), op=ALU.mult),
                reads=[("bank", bks[2]), csr], writes=[uw])
            if kind == "h":
                continue
            tb = t1b[wi % 2]
            tr = ("t1", wi % 2)
            cw = lambda j, fc=fc: cFa(CF_CW + fc * 3 + j)
            if kind == "s":
                prev = [ures, "s"]
                S.op("dve", lambda e, u=u, tb=tb, cw=cw: e.tensor_scalar(
                    out=sb(tb, 0, [[1, NS]]), in0=sb(u, 2050, [[1, NS]]), scalar1=cw(2), scalar2=None, op0=ALU.mult),
                    reads=[uw, "cF"], writes=[tr])
                for j in range(2):
                    S.op("dve", lambda e, tb=tb, cw=cw, j=j, fc=fc: e.scalar_tensor_tensor(
                        out=sb(tb, 0, [[1, NS]]), in0=sb(cF, CF_ST + fc * 8 + j * 4, [[1, NS]]), scalar=cw(j),
                        in1=sb(tb, 0, [[1, NS]]), op0=ALU.mult, op1=ALU.add),
                        reads=[tr, "cF"], writes=[tr])
                ocol = fc * 2052 + 2048
            else:
                w = kind
                pr = [(ures, w - 1)] if w > 0 else [(ures, "h")]
                S.op("act", lambda e, u=u, tb=tb, cw=cw, ucol=ucol: e.activation(
                    out=sb(tb, 0, [[1, 512]]), in_=sb(u, ucol, [[1, 512]]), func=AF.Copy, scale=cw(2)),
                    reads=[uw, "cF"], writes=[tr])
                for j in (1, 0):
                    S.op("dve", lambda e, u=u, tb=tb, cw=cw, ucol=ucol, j=j: e.scalar_tensor_tensor(
                        out=sb(tb, 0, [[1, 512]]), in0=sb(u, ucol - (2 - j), [[1, 512]]), scalar=cw(j),
                        in1=sb(tb, 0, [[1, 512]]), op0=ALU.mult, op1=ALU.add),
                        reads=[uw, tr, "cF"] + pr, writes=[tr])
                ocol = fc * 2052 + 512 * w
            S.op("dve", lambda e, bk=bks[0], n=n, tb=tb, ocol=ocol: e.tensor_tensor(
                out=sb(cbvT, ocol, [[1, n]]), in0=pb(bk, 0, [[1, n]]), in1=sb(tb, 0, [[1, n]]), op=ALU.mult),
                reads=[("bank", bks[0]), tr], writes=[("cbvT", fc, kind)])
        S.op("pool", lambda e, u=u, fc=fc: e.tensor_copy(out=sb(convp_sb, fc * 2, [[1, 2]]), in_=sb(u, 2048, [[1, 2]])),
             reads=[(ures, 3)], writes=["convp_sb"])
        S.op("pool", lambda e, u=u, fc=fc: e.tensor_copy(out=sb(convs_sb, fc * 4, [[1, 4]]), in_=sb(u, 2050, [[1, 4]])),
             reads=[(ures, "s")], writes=["convs_sb"])
    S.dma("sp", lambda e: e.dma_start(out=convp_d.ap(), in_=convp_sb[:]), reads=["convp_sb"])
    S.dma("sp", lambda e: e.dma_start(out=convs1_d.ap(), in_=convs_sb[:]), reads=["convs_sb"])

    if STOP == "C1":
        return fin()
    if ONLY == "C2":
        S.enabled = True
    S.barrier()
    S.dma("pool", lambda e: e.dma_start(out=sb(wo, 0, [[2048, 4], [1, 2048]]),
                                        in_=AP(wO_d, 0, [[8192, 128], [2048, 4], [1, 2048]])), writes=["wo"])
    wins2 = [(512 * w, 512) for w in range(4)] + [(NT, NS)]
    wi = 0
    for oc in range(8):
        wb = wC2[oc % 2]
        wr = ("wC2", oc % 2)
        if oc + 1 < 8:
            load_wC2(oc + 1)
        for (c0, n) in wins2:
            s4 = 4 * (wi % 2)
            wi += 1
            k2 = wi % 2
            hres = hT_tiles(NH + c0, n)
            for gi in range(2):
                for kc in range(8):
                    S.op("pe", lambda e, kc=kc, gi=gi, s4=s4, c0=c0, n=n, wb=wb: e.matmul(
                        pb(s4 + gi, 0, [[1, n]]), lhsT=sb(wb, kc * 256 + gi * 128, [[1, 128]]),
                        rhs=hTa(kc, NH + c0, [[1, n]]), start=(kc == 0), stop=(kc == 7)),
                        reads=[wr] + hres, writes=[("bank", s4 + gi)])
            if n == 512:
                ares_ = [("attnT", hp, c0 // 512, X) for hp in range(4) for X in range(2)]
            else:
                ares_ = [("attnTs", b) for b in range(NS)]
            for kc in range(4):
                S.op("pe", lambda e, kc=kc, s4=s4, c0=c0, n=n, wb=wb: e.matmul(
                    pb(s4 + 3, 0, [[1, n]]), lhsT=sb(wb, 3072 + kc * 128, [[1, 128]]),
                    rhs=sb(attnT, kc * 2052 + c0, [[1, n]]), start=(kc == 0), stop=(kc == 3)),
                    reads=[wr] + ares_, writes=[("bank", s4 + 3)])
            cres = [("cbvT", fc, k) for fc in range(8) for k in ((c0 // 512,) if n == 512 else ("s",))]
            for kc in range(8):
                S.op("pe", lambda e, kc=kc, s4=s4, c0=c0, n=n, wb=wb: e.matmul(
                    pb(s4 + 2, 0, [[1, n]]), lhsT=sb(wb, 2048 + kc * 128, [[1, 128]]),
                    rhs=sb(cbvT, kc * 2052 + c0, [[1, n]]), start=(kc == 0), stop=(kc == 7)),
                    reads=[wr] + cres, writes=[("bank", s4 + 2)])
            S.op("act", lambda e, s4=s4, n=n, k2=k2: e.activation(out=sb(sab[k2], 0, [[1, n]]), in_=pb(s4, 0, [[1, n]]),
                                                              func=AF.Sigmoid),
                 reads=[("bank", s4)], writes=[("sa", k2)])
            S.op("act", lambda e, s4=s4, n=n, k2=k2: e.activation(out=sb(scb[k2], 0, [[1, n]]), in_=pb(s4 + 1, 0, [[1, n]]),
                                                              func=AF.Sigmoid),
                 reads=[("bank", s4 + 1)], writes=[("sc_", k2)])
            S.op("dve", lambda e, s4=s4, n=n, k2=k2: e.tensor_tensor(
                out=sb(tab[k2], 0, [[1, n]]), in0=pb(s4 + 3, 0, [[1, n]]), in1=sb(sab[k2], 0, [[1, n]]), op=ALU.mult),
                reads=[("bank", s4 + 3), ("sa", k2)], writes=[("ta", k2)])
            S.op("dve", lambda e, s4=s4, n=n, k2=k2: e.tensor_tensor(
                out=sb(tcb[k2], 0, [[1, n]]), in0=pb(s4 + 2, 0, [[1, n]]), in1=sb(scb[k2], 0, [[1, n]]), op=ALU.mult),
                reads=[("bank", s4 + 2), ("sc_", k2)], writes=[("tc", k2)])
            S.op("pool", lambda e, n=n, k2=k2, oc=oc, c0=c0: e.tensor_tensor(
                out=sb(mixT, oc * 2052 + c0, [[1, n]]), in0=sb(tab[k2], 0, [[1, n]]), in1=sb(tcb[k2], 0, [[1, n]]),
                op=ALU.add),
                reads=[("ta", k2), ("tc", k2)], writes=[("mixT", oc, c0)])

    if STOP == "C2":
        return fin()
    if ONLY == "D":
        S.enabled = True
    S.barrier()
    cur_junk[0] = junkD
    p2_args = []
    S.dma("sp", lambda e: e.dma_start(out=gB3[:], in_=AP(g3_d, 0, [[0, 128], [1, D]])), writes=["gB3"])
    S.dma("sp", lambda e: e.dma_start(out=gB2[:], in_=AP(g2_d, 0, [[0, 128], [1, D]])), writes=["gB2"])
    S.dma("pool", lambda e: e.dma_start(out=sb(wF[0], 0, [[2048, 4], [1, 2048]]),
                                        in_=AP(wF_d, 0, [[8192, 128], [2048, 4], [1, 2048]])), writes=[("wF", 0)])

    def rstd_from_psum(tt, np_, pt, col):
        for h in range(2):
            S.op("act", lambda e, h=h: e.activation(out=sb(junkD, 0, [[1, 512]], 0, np_),
                                                   in_=AP(PS[pt], h * 512, [[1024, np_], [1, 512]]),
                                                   func=AF.Square, accum_out=sb(stat2, col + h, [[1, 1]], 0, np_)),
                 reads=[("bank", 2 * pt + h)], writes=[("st", col + h), "junk"])
        S.op("dve", lambda e: e.tensor_tensor(out=sb(stat2, col + 2, [[1, 1]], 0, np_),
                                              in0=sb(stat2, col, [[1, 1]], 0, np_),
                                              in1=sb(stat2, col + 1, [[1, 1]], 0, np_), op=ALU.add),
             reads=[("st", col), ("st", col + 1)], writes=[("st", col + 2)])
        S.op("act", lambda e: e.activation(out=sb(stat2, col + 2, [[1, 1]], 0, np_),
                                           in_=sb(stat2, col + 2, [[1, 1]], 0, np_), func=AF.Sqrt,
                                           bias=eps_ap(np_), scale=1.0 / D),
             reads=[("st", col + 2), "cF"], writes=[("st", col + 2)])
        S.op("dve", lambda e: e.reciprocal(out=sb(stat2, col + 3, [[1, 1]], 0, np_),
                                           in_=sb(stat2, col + 2, [[1, 1]], 0, np_)),
             reads=[("st", col + 2)], writes=[("st", col + 3)])
        return sb(stat2, col + 3, [[1, 1]], 0, np_), ("st", col + 3)

    for tt in range(17):
        np_ = 128 if tt < 16 else NS
        c0 = tt * 128
        pt = tt % 3
        b2 = tt % 4
        bx = tt % 4
        tD = tDb[tt % 3]
        mres = [("mixT", oc, (c0 // 512) * 512 if tt < 16 else NT) for oc in range(8)]
        for h in range(2):
            for kc in range(8):
                S.op("pe", lambda e, kc=kc, h=h, pt=pt, c0=c0, np_=np_: e.matmul(
                    AP(PS[pt], h * 512, [[1024, np_], [1, 512]]), lhsT=sb(mixT, kc * 2052 + c0, [[1, np_]]),
                    rhs=sb(wo, kc * 1024 + h * 512, [[1, 512]]), start=(kc == 0), stop=(kc == 7)),
                    reads=["wo"] + mres, writes=[("bank", 2 * pt + h)])
        if tt >= 3 and tt % 2 == 1:
            norm_p2(*p2_args[tt - 3])
            norm_p2(*p2_args[tt - 2])
        rs_ap, rs_res = rstd_from_psum(tt, np_, pt, 64 + (tt % 4) * 8)
        src = AP(x_all, (NH + c0) * D, [[D, 128], [1, D]]) if tt < 16 else xs_d.ap()
        S.dma("sp", lambda e, bx=bx, np_=np_, src=src: e.dma_start(out=sb(xtD[bx], 0, [[1, 1024]], 0, np_), in_=src),
              writes=[("xtD", bx)])
        S.op("dve", lambda e, pt=pt, np_=np_, rs_ap=rs_ap: e.scalar_tensor_tensor(
            out=sb(tD, 0, [[1, 1024]], 0, np_), in0=AP(PS[pt], 0, [[1024, np_], [1, 1024]]), scalar=rs_ap,
            in1=sb(gB2, 0, [[1, 1024]], 0, np_), op0=ALU.mult, op1=ALU.mult),
            reads=[("bank", 2 * pt), ("bank", 2 * pt + 1), rs_res, "gB2"], writes=[("tD", tt % 3)])
        S.op("pool", lambda e, b2=b2, bx=bx, np_=np_: e.tensor_tensor(
            out=sb(x1t[b2], 0, [[1, 1024]], 0, np_), in0=sb(tD, 0, [[1, 1024]], 0, np_),
            in1=sb(xtD[bx], 0, [[1, 1024]], 0, np_), op=ALU.add),
            reads=[("tD", tt % 3), ("xtD", bx)], writes=[("x1t", b2)])
        dst = AP(y_d, c0 * D, [[D, 128], [1, D]]) if tt < 16 else ys_d.ap()
        S.dma("sp", lambda e, b2=b2, np_=np_, dst=dst: e.dma_start(out=dst, in_=sb(x1t[b2], 0, [[1, 1024]], 0, np_)),
              reads=[("x1t", b2)], writes=[("yd", tt)])
        norm_p1(lambda b2=b2, np_=np_: sb(x1t[b2], 0, [[1, 1024]], 0, np_), ("x1t", b2), np_,
                xn2[b2], ("xn2", b2), gB3, "gB3", 128 + (tt % 4) * 4)
        p2_args.append((np_, xn2[b2], ("xn2", b2), 6 + (tt % 2),
                        (lambda c0=c0, np_=np_: sb(h2T, c0, [[2052, 8], [1, np_]])), ("h2T", tt), tt))
    norm_p2(*p2_args[14])
    norm_p2(*p2_args[15])
    norm_p2(*p2_args[16])
    assert D_P2_COUNT[0] == 17

    if STOP == "D":
        return fin()
    if ONLY == "E":
        S.enabled = True
    S.barrier()
    S.dma("sp", lambda e: e.dma_start(out=gB4[:], in_=AP(g4_d, 0, [[0, 128], [1, D]])), writes=["gB4"])
    winsE = [(512 * w, 512, [("h2T", 4 * w + i) for i in range(4)]) for w in range(4)] + [(NT, NS, [("h2T", 16)])]
    fb = 0
    fin_q = []
    ytE3 = [tE] + ytE
    for cg in range(8):
        wb = wF[cg % 2]
        wr = ("wF", cg % 2)
        if cg + 1 < 8:
            S.dma("pool", lambda e, cg=cg: e.dma_start(
                out=sb(wF[(cg + 1) % 2], 0, [[2048, 4], [1, 2048]]),
                in_=AP(wF_d, (cg + 1) * 128 * 8192, [[8192, 128], [2048, 4], [1, 2048]])),
                writes=[("wF", (cg + 1) % 2)])
        if ESTOP == "f0":
            return fin()
        for (c0, n, hres) in winsE:
            if ESTOP == "f1a" and c0 > 0:
                return fin()
            for c in range(4):
                bank = fb % 4
                fb += 1
                for kc in range(8):
                    if ESTOP == "f1b" and (kc > 0 or c > 0):
                        return fin()
                    if ESTOP.startswith("n") and (c * 8 + kc) >= int(ESTOP[1:]):
                        return fin()
                    S.op("pe", lambda e, kc=kc, c=c, bank=bank, c0=c0, n=n, wb=wb: e.matmul(
                        pb(bank, 0, [[1, n]]), lhsT=sb(wb, kc * 512 + c * 128, [[1, 128]]),
                        rhs=sb(h2T, kc * 2052 + c0, [[1, n]]), start=(kc == 0), stop=(kc == 7)),
                        reads=[wr] + hres, writes=[("bank", bank)])
                rb = fb % 2
                if os.environ.get("MK_EV") == "c":
                    continue
                S.op("act", lambda e, bank=bank, n=n, rb=rb: e.activation(out=sb(rlb[rb], 0, [[1, n]]), in_=pb(bank, 0, [[1, n]]),
                                                                func=(AF.Copy if os.environ.get("MK_EV") == "b" else AF.Relu)),
                     reads=[("bank", bank)], writes=[("rl", rb)])
                S.op(("dve" if os.environ.get("MK_EV") == "a" else "pool"), lambda e, c=c, c0=c0, n=n, rb=rb: e.tensor_tensor(
                    out=sb(f1T, c * 2052 + c0, [[1, n]]), in0=sb(rlb[rb], 0, [[1, n]]), in1=sb(rlb[rb], 0, [[1, n]]),
                    op=ALU.mult),
                    reads=[("rl", rb)], writes=[("f1T", c, c0)])
        if ESTOP == "f1":
            return fin()
        for tt in range(17):
            if ESTOP == "f2" and cg == 1:
                return fin()
            np_ = 128 if tt < 16 else NS
            c0 = tt * 128
            pt = 2 + tt % 2
            fres = [("f1T", c, (c0 // 512) * 512 if tt < 16 else NT) for c in range(4)]
            for h in range(2):
                for c in range(4):
                    S.op("pe", lambda e, c=c, h=h, pt=pt, c0=c0, np_=np_, wb=wb: e.matmul(
                        AP(PS[pt], h * 512, [[1024, np_], [1, 512]]), lhsT=sb(f1T, c * 2052 + c0, [[1, np_]]),
                        rhs=sb(wb, 4096 + c * 1024 + h * 512, [[1, 512]]), start=(c == 0), stop=(c == 3)),
                        reads=[wr] + fres, writes=[("bank", 2 * pt + h)])
            fa = (lambda tt=tt, np_=np_: sb(facc, tt * 1024, [[1, 1024]], 0, np_)) if tt < 16 else \
                 (lambda np_=np_: sb(faccs, 0, [[1, 1024]], 0, np_))
            fr = ("facc", tt)
            pin = lambda pt=pt, np_=np_: AP(PS[pt], 0, [[1024, np_], [1, 1024]])
            breads = [("bank", 2 * pt), ("bank", 2 * pt + 1)]
            if cg == 0:
                S.op("act", lambda e, fa=fa, pin=pin: e.activation(out=fa(), in_=pin(), func=AF.Copy),
                     reads=breads, writes=[fr])
            else:
                S.op("dve", lambda e, fa=fa, pin=pin: e.tensor_tensor(out=fa(), in0=pin(), in1=fa(), op=ALU.add),
                     reads=breads + [fr], writes=[fr])
            if cg == 7 and ESTOP != "nofin":
                col = 192 + (tt % 4) * 8
                ss = sb(stat2, col, [[1, 1]], 0, np_)
                S.op("act", lambda e, fa=fa, np_=np_, ss=ss: e.activation(out=sb(junkE, 0, [[1, 1024]], 0, np_), in_=fa(),
                                                                     func=AF.Square, accum_out=ss),
                     reads=[fr], writes=[("st", col), "junk"])
                S.op("act", lambda e, ss=ss, np_=np_: e.activation(out=ss, in_=ss, func=AF.Sqrt, bias=eps_ap(np_),
                                                              scale=1.0 / D),
                     reads=[("st", col), "cF"], writes=[("st", col)])
                rs = sb(stat2, col + 1, [[1, 1]], 0, np_)
                S.op("dve", lambda e, ss=ss, rs=rs: e.reciprocal(out=rs, in_=ss), reads=[("st", col)],
                     writes=[("st", col + 1)])
                src = AP(y_d, c0 * D, [[D, 128], [1, D]]) if tt < 16 else ys_d.ap()
                bx = tt % 2
                by = tt % 3
                S.dma("sp", lambda e, bx=bx, np_=np_, src=src: e.dma_start(out=sb(xrE[bx], 0, [[1, 1024]], 0, np_), in_=src),
                      reads=[("yd", tt)], writes=[("xrE", bx)])

                def fin2(tt=tt, np_=np_, fa=fa, fr=fr, rs=rs, col=col, bx=bx, by=by, c0=c0):
                    S.op("dve", lambda e: e.scalar_tensor_tensor(
                        out=sb(ytE3[by], 0, [[1, 1024]], 0, np_), in0=fa(), scalar=rs, in1=sb(gB4, 0, [[1, 1024]], 0, np_),
                        op0=ALU.mult, op1=ALU.mult),
                        reads=[fr, ("st", col + 1), "gB4"], writes=[("ytE", by)])
                    S.op("pool", lambda e: e.tensor_tensor(
                        out=sb(ytE3[by], 0, [[1, 1024]], 0, np_), in0=sb(ytE3[by], 0, [[1, 1024]], 0, np_),
                        in1=sb(xrE[bx], 0, [[1, 1024]], 0, np_), op=ALU.add),
                        reads=[("ytE", by), ("xrE", bx)], writes=[("ytE", by)])
                    dst = AP(y_d, c0 * D, [[D, 128], [1, D]]) if tt < 16 else ys_d.ap()
                    S.dma("sp", lambda e: e.dma_start(out=dst, in_=sb(ytE3[by], 0, [[1, 1024]], 0, np_)),
                          reads=[("ytE", by), ("xrE", bx)], writes=[("yd", tt)])
                fin_q.append(fin2)
                if len(fin_q) >= 2:
                    fin_q.pop(0)()
    for f in fin_q:
        f()
    S.build()
    return nc


_NC_CACHE = {}


def _consts():
    cB = np.zeros((128, NCB), np.float32)
    p = np.arange(128)[:, None]
    c = np.arange(256)[None, :]
    m = np.where(c < 128, p <= c, p >= c - 128)
    cB[:, CB_MASK:CB_MASK + 256] = m
    cB[:, CB_ID:CB_ID + 128] = np.eye(128)
    cB[:, CB_ONE:CB_ONE + 128] = 1.0
    hh = np.arange(8)[:, None]
    cc = np.arange(512)[None, :]
    cB[0:8, CB_BLK:CB_BLK + 512] = (cc // 64 == hh)
    return cB.astype(ml_dtypes.bfloat16)


def kernel(x_prompt, x_sample, cache_kv_w128, cache_kv_w512, cache_kv_w2048, state_conv,
           w_in, conv_w, w_attn_o, w_conv_o, w_o, w_ff1, w_ff2,
           g_mix_pre, g_mix_post, g_ffn_pre, g_ffn_post):
    f32 = np.float32
    x_prompt = np.asarray(x_prompt, f32)
    x_sample = np.asarray(x_sample, f32)
    caches = [np.asarray(c, f32) for c in (cache_kv_w128, cache_kv_w512, cache_kv_w2048)]
    state_conv = np.asarray(state_conv, f32)
    w_in = np.asarray(w_in, f32)[0]
    conv_w = np.asarray(conv_w, f32)[0]
    w_attn_o = np.asarray(w_attn_o, f32)[0]
    w_conv_o = np.asarray(w_conv_o, f32)[0]
    w_o = np.asarray(w_o, f32)[0]
    w_ff1 = np.asarray(w_ff1, f32)[0]
    w_ff2 = np.asarray(w_ff2, f32)[0]
    g1 = np.asarray(g_mix_pre, f32)[0]
    g2 = np.asarray(g_mix_post, f32)[0]
    g3 = np.asarray(g_ffn_pre, f32)[0]
    g4 = np.asarray(g_ffn_post, f32)[0]

    def pack_kc(w):
        K, N = w.shape
        return np.ascontiguousarray(w.reshape(K // 128, 128, N).transpose(1, 0, 2))

    wB = np.empty((4, 128, 8, 1152), f32)
    for hp in range(4):
        cols = []
        for base in (0, 1536, 3072):
            for g in range(3):
                cols.append(np.arange(base + g * 512 + hp * 128, base + g * 512 + hp * 128 + 128))
        wB[hp] = pack_kc(w_in[:, np.concatenate(cols)])
    wB = wB.reshape(4, 128, 9216)
    wC1 = np.empty((8, 128, 8, 384), f32)
    for fc in range(8):
        cols = np.concatenate([np.arange(b0 + fc * 128, b0 + fc * 128 + 128) for b0 in (4608, 5632, 6656)])
        wC1[fc] = pack_kc(w_in[:, cols])
    wC1 = wC1.reshape(8, 128, 3072)
    wC2 = np.empty((8, 128, 3584), f32)
    for oc in range(8):
        cols = np.concatenate([np.arange(b0 + oc * 128, b0 + oc * 128 + 128) for b0 in (7680, 8704)])
        wC2[oc, :, 0:2048] = pack_kc(w_in[:, cols]).reshape(128, 2048)
        wC2[oc, :, 2048:3072] = pack_kc(w_conv_o[:, oc * 128:(oc + 1) * 128]).reshape(128, 1024)
        wC2[oc, :, 3072:3584] = pack_kc(w_attn_o[:, oc * 128:(oc + 1) * 128]).reshape(128, 512)
    wO = pack_kc(w_o).reshape(128, 8192)
    wF = np.empty((8, 128, 8192), f32)
    for cg in range(8):
        wF[cg, :, 0:4096] = pack_kc(w_ff1[:, cg * 512:(cg + 1) * 512]).reshape(128, 4096)
        wF[cg, :, 4096:8192] = pack_kc(w_ff2[cg * 512:(cg + 1) * 512, :]).reshape(128, 4096)
    cBc = _consts()

    in_maps = []
    for core in range(8):
        b, j = core // 4, core % 4
        xa = np.zeros((NH + NT, D), f32)
        if j > 0:
            xa[:NH] = x_prompt[b, NT * j - NH:NT * j]
        xa[NH:] = x_prompt[b, NT * j:NT * (j + 1)]
        sl = slice(NS * core, NS * core + NS)
        cFc = np.zeros((128, NCF), f32)
        cFc[:, CF_G1:CF_G1 + 8] = g1.reshape(8, 128).T
        cFc[:, CF_G3:CF_G3 + 8] = g3.reshape(8, 128).T
        cFc[:, CF_CW:CF_CW + 24] = conv_w.reshape(3, 8, 128).transpose(2, 1, 0).reshape(128, 24)
        stc = state_conv[0, sl]
        cFc[:, CF_ST:CF_ST + 64] = stc.reshape(NS, 2, 8, 128).transpose(3, 2, 1, 0).reshape(128, 64)
        cFc[:, CF_HV] = 1.0 if j > 0 else 0.0
        cFc[0:4, CF_OH:CF_OH + 4] = np.eye(4)
        cFc[:, CF_EPS] = 1e-6
        pp = np.arange(128)[:, None, None]
        cch = np.arange(4)[None, :, None]
        hh = np.arange(8)[None, None, :]
        cFc[:, CF_BM:CF_BM + 32] = (hh == 2 * cch + pp // 64).reshape(128, 32)
        in_maps.append({
            "x_all": xa,
            "xs": np.ascontiguousarray(x_sample[sl, 0]),
            "c128": np.ascontiguousarray(caches[0][0, sl]).reshape(NS * 128, 1024),
            "c512": np.ascontiguousarray(caches[1][0, sl]).reshape(NS * 512, 1024),
            "c2048": np.ascontiguousarray(caches[2][0, sl]).reshape(NS * 2048, 1024),
            "st1": np.ascontiguousarray(state_conv[0, sl, 1]),
            "wB": wB, "wC1": wC1, "wC2": wC2, "wO": wO, "wF": wF,
            "g1": g1.reshape(1, D), "g3": g3.reshape(1, D),
            "g2": g2.reshape(1, D), "g4": g4.reshape(1, D),
            "constF": cFc, "constB": cBc,
        })
    ncores = int(os.environ.get("MK_CORES", "8"))
    nc = build_program()
    res = run_bass_kernel_spmd(nc, in_maps[:ncores], core_ids=list(range(ncores)))
    R = list(res.results)
    while len(R) < 8:
        R.append(R[0])

    y_prompt = np.empty((2, 8192, D), f32)
    y_sample = np.empty((32, 1, D), f32)
    kvp = [np.empty((1, 2, WIN[g], 2, 8, 64), f32) for g in range(3)]
    conv_prompt = np.empty((1, 2, 2, D), f32)
    kvs = [np.empty((1, 32, WIN[g], 2, 8, 64), f32) for g in range(3)]
    conv_sample = np.empty((1, 32, 2, D), f32)
    names_p = ("kvp128", "kvp512", "kvp2048")
    names_s = ("kvs128", "kvs512", "kvs2048")
    for core in range(8):
        b, j = core // 4, core % 4
        r = R[core]
        y_prompt[b, NT * j:NT * (j + 1)] = r["y"]
        sl = slice(NS * core, NS * core + NS)
        y_sample[sl, 0] = r["ys"]
        if j == 3:
            for g in range(3):
                kvp[g][0, b] = r[names_p[g]].reshape(WIN[g], 2, 8, 64)
            conv_prompt[0, b] = r["convp"].reshape(128, 8, 2).transpose(2, 1, 0).reshape(2, D)
        for g in range(3):
            kvs[g][0, sl] = r[names_s[g]].reshape(NS, WIN[g], 2, 8, 64)
        conv_sample[0, sl, 0] = r["convs0"]
        conv_sample[0, sl, 1] = r["convs1"].reshape(128, 8, NS).transpose(2, 1, 0).reshape(NS, D)
    return (y_prompt, y_sample, kvp[0], kvp[1], kvp[2], conv_prompt,
            kvs[0], kvs[1], kvs[2], conv_sample)
```

```python
import contextlib
import os
import types
import numpy as np
import ml_dtypes
import concourse.bass as bass
import concourse.mybir as mybir
from concourse.bass_utils import run_bass_kernel_spmd

F32 = mybir.dt.float32
BF16 = mybir.dt.bfloat16
AF = mybir.ActivationFunctionType
ALU = mybir.AluOpType
AP = bass.AP

NT = 2048
NH = 2048
NS = 4
HW = NH + NT + NS
D = 1024
RS = (1, 4, 16)
NQ = (16, 4, 1)
WIN = (128, 512, 2048)


class _Op:
    __slots__ = ("eng", "fn", "reads", "writes", "idx", "dma", "deps", "sig", "sigval",
                 "dsem", "dval", "k")

    def __init__(self, eng, fn, reads, writes, dma):
        self.eng = eng
        self.fn = fn
        self.reads = reads
        self.writes = writes
        self.dma = dma
        self.deps = set()
        self.sig = False
        self.sigval = 0
        self.dsem = None
        self.dval = 0
        self.k = 0


class Sched:
    ENG = ("pe", "act", "dve", "pool", "sp")
    RING = int(os.environ.get("MK_RING", "12"))

    def __init__(self, nc):
        self.nc = nc
        self.ops = []
        self.res_w = {}
        self.res_r = {}
        self.pending_barrier = {}
        self._bar_start = 0

    def _add(self, eng, fn, reads, writes, dma):
        isb = lambda r: isinstance(r, tuple) and r[0] == "bank"
        excl = [r for r in tuple(reads) + tuple(writes) if isb(r)]
        reads = [r for r in reads if not isb(r)]
        writes = [r for r in writes if not isb(r)]
        if fn.__closure__:
            cells = []
            for c in fn.__closure__:
                try:
                    cells.append(types.CellType(c.cell_contents))
                except ValueError:
                    cells.append(c)
            fn = types.FunctionType(fn.__code__, fn.__globals__, fn.__name__, fn.__defaults__,
                                    tuple(cells))
        op = _Op(eng, fn, tuple(reads), tuple(writes), dma)
        op.idx = len(self.ops)
        deps = set()
        for r in op.reads:
            w = self.res_w.get(r)
            if w is not None:
                deps.add(w)
        for w_ in op.writes:
            w = self.res_w.get(w_)
            if w is not None:
                deps.add(w)
            for r in self.res_r.get(w_, ()):
                deps.add(r)
        for x in excl:
            w = self.res_w.get(x)
            if w is not None:
                deps.add(w)
        if eng in self.pending_barrier:
            deps |= self.pending_barrier.pop(eng)
        deps.discard(op.idx)
        op.deps = deps
        for r in op.reads:
            self.res_r.setdefault(r, []).append(op.idx)
        for w_ in op.writes:
            self.res_w[w_] = op.idx
            self.res_r[w_] = []
        for x in excl:
            self.res_w[x] = op.idx
        self.ops.append(op)
        return op

    enabled = True

    def op(self, eng, fn, reads=(), writes=()):
        if not self.enabled:
            return None
        return self._add(eng, fn, reads, writes, False)

    def dma(self, eng, fn, reads=(), writes=()):
        if not self.enabled:
            return None
        return self._add(eng, fn, reads, writes, True)

    def barrier_one(self, eng):
        last = {}
        dmas = set()
        for o in self.ops:
            if o.dma:
                dmas.add(o.idx)
            else:
                last[o.eng] = o.idx
        self.pending_barrier[eng] = set(last.values()) | dmas | self.pending_barrier.get(eng, set())

    def barrier(self):
        last = {}
        dmas = set()
        for o in self.ops[self._bar_start:]:
            if o.dma:
                dmas.add(o.idx)
            else:
                last[o.eng] = o.idx
        self._bar_start = len(self.ops)
        deps = set(last.values()) | dmas
        for e in self.ENG:
            self.pending_barrier[e] = set(deps) | self.pending_barrier.get(e, set())

    def build(self):
        nc = self.nc
        ops = self.ops
        for o in ops:
            nd = set()
            best = {}
            for d in o.deps:
                p = ops[d]
                if p.dma:
                    nd.add(d)
                    continue
                if p.eng == o.eng and not o.dma:
                    if p.eng == "pe":
                        continue
                    if not (set(p.writes) & (set(o.reads) | set(o.writes))):
                        continue
                if p.eng not in best or best[p.eng] < d:
                    best[p.eng] = d
            nd |= set(best.values())
            o.deps = nd
        for o in ops:
            for d in o.deps:
                ops[d].sig = True
        cnt = {e: 0 for e in self.ENG}
        dcnt = {e: 0 for e in self.ENG}
        for o in ops:
            if o.dma:
                o.k = dcnt[o.eng]
                dcnt[o.eng] += 1
            elif o.sig:
                cnt[o.eng] += 1
                o.sigval = cnt[o.eng]
        self.cnt = dict(cnt)
        self.dcnt = dict(dcnt)
        if os.environ.get("MK_VERBOSE"):
            print("SIGNALS", cnt, "DMAS", dcnt, "OPS", len(ops))
            import collections
            print("PERENG", dict(collections.Counter(o.eng for o in ops)))
        with contextlib.ExitStack() as st:
            esem = {e: st.enter_context(nc.semaphore("s_" + e)) for e in self.ENG}
            rings = {}
            for e in self.ENG:
                if dcnt[e]:
                    rings[e] = [st.enter_context(nc.semaphore("d_%s_%d" % (e, i)))
                                for i in range(min(self.RING, dcnt[e]))]
            for o in ops:
                if o.dma:
                    R = len(rings[o.eng])
                    o.dsem = rings[o.eng][o.k % R]
                    o.dval = 16 * (o.k // R + 1)
            block = st.enter_context(nc.Block())
            engobj = {"pe": "tensor", "act": "scalar", "dve": "vector", "pool": "gpsimd",
                      "sp": "sync"}
            ENG = self.ENG

            def make(ename):
                def body(eng):
                    waited = {}
                    dwaited = set()
                    for o in ops:
                        if o.eng != ename:
                            continue
                        if o.dma and o.dval > 16:
                            key = (id(o.dsem), o.dval - 16)
                            if key not in dwaited:
                                eng.wait_ge(o.dsem, o.dval - 16)
                                dwaited.add(key)
                        for d in sorted(o.deps):
                            p = ops[d]
                            if p.dma:
                                key = (id(p.dsem), p.dval)
                                if key in dwaited:
                                    continue
                                eng.wait_ge(p.dsem, p.dval)
                                dwaited.add(key)
                            else:
                                if waited.get(p.eng, 0) >= p.sigval:
                                    continue
                                eng.wait_ge(esem[p.eng], p.sigval)
                                waited[p.eng] = p.sigval
                        ins = o.fn(eng)
                        if o.dma:
                            ins.then_inc(o.dsem, 16)
                        elif o.sig:
                            ins.then_inc(esem[ename], 1)
                    if ename == "sp":
                        for e2 in ENG:
                            if dcnt[e2]:
                                R = len(rings[e2])
                                for i in range(R):
                                    n = (dcnt[e2] - 1 - i) // R + 1
                                    if n > 0:
                                        eng.wait_ge(rings[e2][i], 16 * n)
                            if cnt[e2] and e2 != "sp":
                                eng.wait_ge(esem[e2], cnt[e2])
                return body

            for e in ENG:
                getattr(block, engobj[e])(make(e))


NCF = 256
NCB = 1536
CF_G1 = 0
CF_G3 = 8
CF_CW = 16
CF_ST = 40
CF_HV = 104
CF_OH = 105
CF_EPS = 109
CF_BM = 112
CB_MASK = 0
CB_ID = 256
CB_ONE = 384
CB_ZERO = 512
CB_BLK = 1024
CB_END = 1536


def build_program():
    nc = bass.Bass("TRN2", target_bir_lowering=False)
    S = Sched(nc)
    STOP = os.environ.get("MK_STOP", "")
    ONLY = os.environ.get("MK_ONLY", "")
    if ONLY:
        S.enabled = False
    BSTOP = os.environ.get("MK_BSTOP", "")
    ESTOP = os.environ.get("MK_ESTOP", "")

    def fin():
        for i in range(int(os.environ.get("MK_DUMMY", "0"))):
            S.op("pe", lambda e: e.matmul(pb(7, 0, [[1, 8]]), lhsT=cBa(CB_ZERO, 128), rhs=cBa(CB_ZERO, 8),
                                          start=True, stop=True), reads=["cB"], writes=[("bank", 7)])
        S.build()
        return nc

    def din(name, shape, dt=F32):
        return nc.dram_tensor(name, list(shape), dt, kind="ExternalInput")

    def dout(name, shape, dt=F32):
        return nc.dram_tensor(name, list(shape), dt, kind="ExternalOutput")

    x_all = din("x_all", [NH + NT, D])
    xs_d = din("xs", [NS, D])
    c128_d = din("c128", [NS * 128, 1024])
    c512_d = din("c512", [NS * 512, 1024])
    c2048_d = din("c2048", [NS * 2048, 1024])
    cache_d = (c128_d, c512_d, c2048_d)
    st1_d = din("st1", [NS, D])
    wB_d = din("wB", [4, 128, 9216])
    wC1_d = din("wC1", [8, 128, 3072])
    wC2_d = din("wC2", [8, 128, 3584])
    wO_d = din("wO", [128, 8192])
    wF_d = din("wF", [8, 128, 8192])
    g1_d = din("g1", [1, D])
    g3_d = din("g3", [1, D])
    g2_d = din("g2", [1, D])
    g4_d = din("g4", [1, D])
    cF_d = din("constF", [128, NCF])
    cB_d = din("constB", [128, NCB], BF16)

    y_d = dout("y", [NT, D])
    ys_d = dout("ys", [NS, D])
    kvp_d = (dout("kvp128", [128, 1024]), dout("kvp512", [512, 1024]), dout("kvp2048", [2048, 1024]))
    convp_d = dout("convp", [128, 16])
    kvs_d = (dout("kvs128", [NS * 128, 1024]), dout("kvs512", [NS * 512, 1024]),
             dout("kvs2048", [NS * 2048, 1024]))
    convs0_d = dout("convs0", [NS, D])
    convs1_d = dout("convs1", [128, 32])

    def sbt(name, cols, dt, off):
        return nc.alloc_sbuf_tensor_at(name, [128, cols], dt, offset=off + 16384)

    R0 = 0
    R1 = 65600
    R2 = 171200
    R3 = 187616
    HT_F = 8 * HW
    hT = sbt("hT", HT_F, BF16, R0)
    facc = sbt("facc", 16 * 1024, F32, R0)
    o = R1
    QF = 3 * 2048
    qT = sbt("qT", QF, BF16, o); o += QF * 2
    KOFF = (0, 2176, 4736)
    LK = (2176, 640, 256)
    KF = 8832
    kT = sbt("kT", KF, BF16, o); o += KF * 2
    NBLK = 69
    VF = NBLK * 192
    Vg = sbt("Vaug", VF, BF16, o); o += VF * 2
    PT2F = 32 * 128
    PT2 = sbt("PT2", PT2F, BF16, o); o += PT2F * 2
    PT0F = 10 * 256
    PT0 = sbt("PT0", PT0F, BF16, o); o += PT0F * 2
    PT1F = 4 * 3 * 256
    PT1 = sbt("PT1", PT1F, BF16, o); o += PT1F * 2
    WBF = 9216
    wB = sbt("wB", WBF, BF16, o); o += WBF * 2
    vT = sbt("vT", 4096, BF16, o); o += 8192
    kvst_extra = [sbt("kvstx%d" % i, 256, F32, o + i * 1024) for i in range(2)]; o += 2048
    assert o <= R2, o
    CF_ = 8 * 2052
    cbvT = sbt("cbvT", CF_, BF16, R1)
    mixT = sbt("mixT", CF_, BF16, R1 + 32832)
    h2T = sbt("h2T", CF_, BF16, R1 + 65664)
    R1E = R1 + 98496
    KVs2 = [sbt("KVs%d" % i, 1024, F32, R1 + i * 4096) for i in range(2)]
    KVb2 = [sbt("KVb%d" % i, 1024, BF16, R1 + 8192 + i * 2048) for i in range(2)]
    KTs2 = [sbt("KTs%d" % i, 512, BF16, R1 + 12288 + i * 1024) for i in range(2)]
    Qbd = sbt("Qbd", 4 * 3 * 4 * 8, BF16, R1 + 14336)
    PTs2 = [sbt("PTs%d" % i, 8, BF16, R1 + 15104 + i * 32) for i in range(2)]
    PTn2 = [sbt("PTn%d" % i, 8, F32, R1 + 15168 + i * 32) for i in range(2)]
    PTnm2 = [sbt("PTnm%d" % i, 8, BF16, R1 + 15232 + i * 32) for i in range(2)]
    rdn = sbt("rdn", 1, F32, R1 + 15296)
    Om = sbt("Om", 512, BF16, R1 + 15328)
    o = R1 + 32832
    wC1 = [sbt("wC1_%d" % i, 3072, BF16, o + i * 6144) for i in range(2)]; o += 12288
    UF = 2 + 2048 + 4
    ub = [sbt("ub%d" % i, UF, F32, o + i * 8224) for i in range(2)]; o += 16448
    ccs = [sbt("ccs%d" % i, 512, F32, o + i * 2048) for i in range(2)]; o += 4096
    t1b = [sbt("t1b%d" % i, 512, F32, o + i * 2048) for i in range(2)]; o += 4096
    assert o <= R2
    o = R1 + 69888
    wC2 = [sbt("wC2_%d" % i, 3584, BF16, o + i * 7168) for i in range(2)]; o += 14336
    sab = [sbt("sa%d" % i, 512, F32, o + i * 2048) for i in range(2)]; o += 4096
    scb = [sbt("sc%d" % i, 512, F32, o + i * 2048) for i in range(2)]; o += 4096
    tab = [sbt("ta%d" % i, 512, F32, o + i * 2048) for i in range(2)]; o += 4096
    tcb = [sbt("tc%d" % i, 512, F32, o + i * 2048) for i in range(2)]; o += 4096
    assert o <= R2
    o = R0
    gB2 = sbt("gB2", 1024, F32, o); o += 4096
    xtD = [sbt("xtD%d" % i, 1024, F32, o + i * 4096) for i in range(4)]; o += 16384
    tDb = [sbt("tD%d" % i, 1024, F32, o + i * 4096) for i in range(3)]; o += 12288
    x1t = [sbt("x1t%d" % i, 1024, F32, o + i * 4096) for i in range(4)]; o += 16384
    xn2 = [sbt("xn2_%d" % i, 1024, BF16, o + i * 2048) for i in range(4)]; o += 8192
    junkD = sbt("junkD", 1024, BF16, o); o += 2048
    gB3 = sbt("gB3", 1024, F32, o); o += 4096
    assert o <= R1
    o = R1
    wF = [sbt("wF0", 8192, BF16, R2), sbt("wF1", 8192, BF16, o)]; o += 16384
    f1T = sbt("f1T", 4 * 2052, BF16, o); o += 16416
    gB4 = sbt("gB4", 1024, F32, o); o += 4096
    xrE = [sbt("xrE%d" % i, 1024, F32, o + i * 4096) for i in range(2)]; o += 8192
    tE = sbt("tE", 1024, F32, o); o += 4096
    ytE = [sbt("ytE%d" % i, 1024, F32, o + i * 4096) for i in range(2)]; o += 8192
    faccs = sbt("faccs", 1024, F32, o); o += 4096
    junkE = sbt("junkE", 1024, BF16, o); o += 2048
    assert o <= R1 + 65664
    rlb = [sbt("rl%d" % i, 512, F32, R1E + i * 2048) for i in range(2)]
    assert R1E + 4096 <= R2
    AT_F = 4 * 2052
    attnT = sbt("attnT", AT_F, BF16, R2)
    o = R3
    cF = sbt("cF", NCF, F32, o); o += NCF * 4
    cB = sbt("cB", NCB, BF16, o); o += NCB * 2
    stat2 = sbt("stat2", 256, F32, o); o += 1024
    qkvs = sbt("qkvs", 4 * 6 * 4, BF16, o); o += 192
    vnb = sbt("vnb", 3 * 512, BF16, o); o += 3072
    convp_sb = sbt("convp_sb", 16, F32, o); o += 64
    convs_sb = sbt("convs_sb", 32, F32, o); o += 128
    o3 = o
    NXT = 12
    xt = [sbt("xt%d" % i, 1024, F32, R1 + i * 4096) for i in range(NXT)]
    xn = [sbt("xn%d" % i, 1024, BF16, R1 + NXT * 4096 + i * 2048) for i in range(4)]
    junk = sbt("junk", 1024, BF16, R1 + NXT * 4096 + 8192)
    gB1 = sbt("gB1", 1024, F32, R1 + NXT * 4096 + 10240)
    assert NXT * 4096 + 14336 <= 12288 + 17664 + 26496 + 8192 + 5120 + 6144
    kvst = [sbt("kvst%d" % i, 256, F32, o3 + i * 1024) for i in range(2)] + kvst_extra
    dnb = sbt("dnb", 512, F32, o3 + 2048)
    hib = sbt("hib", 512, BF16, o3 + 4096)
    lob = sbt("lob", 512, BF16, o3 + 5120)
    bcs = sbt("bcs", 512, F32, o3 + 6144)
    maskh = sbt("maskh", 128, BF16, o3 + 8192)
    kvn_st = sbt("kvn_st", 768, F32, o3 + 8448)
    vTb = sbt("vTb", 2560, BF16, o3 + 11520)
    assert o3 + 11520 + 5120 <= 212992
    assert 8448 + 3072 <= 12288
    wo = sbt("wo", 8192, BF16, o3)
    assert o3 + 16384 <= 212992, o3

    PS = [nc.alloc_psum_tensor("ps%d" % i, [128, 1024], F32) for i in range(4)]
    PSB = [p.bitcast(BF16) for p in PS]

    def pb(bank, col, dims, p0=0, np_=128):
        return AP(PS[bank // 2], p0 * 1024 + (bank % 2) * 512 + col, [[1024, np_]] + dims)

    def pbb(bank, col, dims, p0=0, np_=128):
        return AP(PSB[bank // 2], p0 * 2048 + (bank % 2) * 1024 + col, [[2048, np_]] + dims)

    def sb(t, col, dims, p0=0, np_=128):
        F = t.shape[1]
        return AP(t, p0 * F + col, [[F, np_]] + dims)

    def hTa(kc, col, dims, np_=128):
        return sb(hT, kc * HW + col, dims, 0, np_)

    def cFa(col, n=1, p0=0, np_=128):
        return sb(cF, col, [[1, n]], p0, np_)

    def cBa(col, n, p0=0, np_=128):
        return sb(cB, col, [[1, n]], p0, np_)

    ident = lambda n: sb(cB, CB_ID, [[1, n]], 0, n)
    eps_ap = lambda n: cFa(CF_EPS, 1, 0, n)

    def hT_tiles(col0, n, stride=1):
        lo = col0 // 128
        hi = (col0 + (n - 1) * stride) // 128
        return [("hT", t) for t in range(lo, hi + 1)]

    def load_wB(hp):
        S.dma("pool", lambda e, hp=hp: e.dma_start(
            out=sb(wB, 0, [[1536, 6], [1, 1536]]),
            in_=AP(wB_d, hp * 128 * 9216, [[9216, 128], [1536, 6], [1, 1536]])),
            writes=["wB"])

    load_wB(0)
    S.dma("sp", lambda e: e.dma_start(out=cF[:], in_=cF_d.ap()), writes=["cF"])
    S.dma("sp", lambda e: e.dma_start(out=cB[:], in_=cB_d.ap()), writes=["cB"])
    S.dma("sp", lambda e: e.dma_start(out=convs0_d.ap(), in_=st1_d.ap()))

    cur_junk = [junk]

    def norm_p1(src_ap_fn, src_res, np_, xnb, xn_res, gB, gB_res, statcol):
        jk = cur_junk[0]
        ss = sb(stat2, statcol, [[1, 1]], 0, np_)
        sd = sb(stat2, statcol + 1, [[1, 1]], 0, np_)
        rs = sb(stat2, statcol + 2, [[1, 1]], 0, np_)
        kss = ("st", statcol)
        S.op("act", lambda e: e.activation(out=sb(jk, 0, [[1, 1024]], 0, np_), in_=src_ap_fn(),
                                           func=AF.Square, accum_out=ss),
             reads=[src_res], writes=[kss, "junk"])
        S.op("act", lambda e: e.activation(out=sd, in_=ss, func=AF.Sqrt, bias=eps_ap(np_),
                                           scale=1.0 / D),
             reads=[kss, "cF"], writes=[("sd", statcol)])
        S.op("dve", lambda e: e.reciprocal(out=rs, in_=sd), reads=[("sd", statcol)],
             writes=[("rs", statcol)])
        S.op("dve", lambda e: e.scalar_tensor_tensor(out=sb(xnb, 0, [[1, 1024]], 0, np_), in0=src_ap_fn(),
                                                     scalar=rs, in1=sb(gB, 0, [[1, 1024]], 0, np_),
                                                     op0=ALU.mult, op1=ALU.mult),
             reads=[src_res, ("rs", statcol), gB_res], writes=[xn_res])

    D_P2_COUNT = [0]

    def norm_p2(np_, xnb, xn_res, tpbank, dst_fn, dst_res, idx):
        if dst_res[0] == "h2T":
            D_P2_COUNT[0] += 1
        for c in range(8):
            S.op("pe", lambda e, c=c: e.transpose(out=pbb(tpbank, c * 128, [[1, np_]]),
                                                  in_=sb(xnb, c * 128, [[1, 128]], 0, np_),
                                                  identity=ident(np_)),
                 reads=[xn_res, "cB"], writes=[("bank", tpbank)])
        if idx % 2 == 0:
            S.op("act", lambda e: e.activation(out=dst_fn(), in_=pbb(tpbank, 0, [[128, 8], [1, np_]]),
                                               func=AF.Copy),
                 reads=[("bank", tpbank)], writes=[dst_res])
        else:
            S.op("dve", lambda e: e.tensor_copy(out=dst_fn(), in_=pbb(tpbank, 0, [[128, 8], [1, np_]])),
                 reads=[("bank", tpbank)], writes=[dst_res])

    S.dma("sp", lambda e: e.dma_start(out=gB1[:], in_=AP(g1_d, 0, [[0, 128], [1, D]])), writes=["gB1"])
    a_done = []
    a_pend = []

    def emit_A_tile(ti):
        it_ = len(a_done)
        a_done.append(ti)
        np_ = 128 if ti < 32 else NS
        b = it_ % 4
        bx = it_ % NXT
        if ti < 32:
            src = AP(x_all, ti * 128 * D, [[D, 128], [1, D]])
        else:
            src = xs_d.ap()
        S.dma("sp", lambda e: e.dma_start(out=sb(xt[bx], 0, [[1, 1024]], 0, np_), in_=src),
              writes=[("xt", bx)])
        norm_p1(lambda: sb(xt[bx], 0, [[1, 1024]], 0, np_), ("xt", bx), np_,
                xn[b], ("xn", b), gB1, "gB1", (it_ % 8) * 4)
        a_pend.append(lambda: norm_p2(np_, xn[b], ("xn", b), 4 + it_ % 4,
                                      lambda: hTa(0, ti * 128, [[HW, 8], [1, np_]]), ("hT", ti), it_))
        if len(a_pend) > 2:
            a_pend.pop(0)()

    A_ORDER = list(range(16, 32)) + [15, 12, 13, 14] + list(range(12)) + [32]
    A_LOOK = int(os.environ.get("MK_ALOOK", "6"))

    def need_tiles(res_list):
        for r in res_list:
            if isinstance(r, tuple) and r[0] == "hT":
                pos = A_ORDER.index(r[1])
                for t in A_ORDER[:min(len(A_ORDER), pos + 1 + A_LOOK)]:
                    if t not in a_done:
                        emit_A_tile(t)

    for ti in A_ORDER:
        if ti not in a_done:
            emit_A_tile(ti)
    while a_pend:
        a_pend.pop(0)()
    S.barrier()
    if STOP == "A":
        return fin()
    if ONLY == "B":
        S.enabled = True
    S.op("dve", lambda e: e.tensor_scalar(out=maskh[:], in0=cBa(CB_MASK + 128, 128),
                                          scalar1=cFa(CF_HV), scalar2=None, op0=ALU.mult),
         reads=["cF", "cB"], writes=["maskh"])

    S.op("pool", lambda e: e.memset(sb(Vg, 64, [[192, NBLK], [1, 64]]), 1.0), writes=["Vones"])

    blocks = []
    vi = 0
    for g in range(3):
        for rho in range(RS[g]):
            for kb in range(NQ[g] + 1):
                blocks.append((g, rho, kb, vi))
                vi += 1
    assert vi == NBLK
    VI = {(g, rho, kb): v for (g, rho, kb, v) in blocks}
    all_own = [("hT", t) for t in range(16, 32)]
    sc_slot = [0]
    proj_bank = [0]
    kv_bank = [0]

    def evac(i, fn_out, fn_in, reads, writes):
        if i % 2 == 0:
            S.op("act", lambda e: e.activation(out=fn_out(), in_=fn_in(), func=AF.Copy),
                 reads=reads, writes=writes)
        else:
            S.op("dve", lambda e: e.tensor_copy(out=fn_out(), in_=fn_in()), reads=reads, writes=writes)

    def issue_cache_copies():
        for g in range(3):
            W = WIN[g]
            for b in range(NS):
                if os.environ.get("MK_NOCOPY"):
                    continue
                n = (W - 1) * 64
                S.dma("act", lambda e, g=g, b=b, W=W, n=n: e.dma_start(
                    out=AP(kvs_d[g], b * W * 1024, [[n, 16], [1, n]]),
                    in_=AP(cache_d[g], b * W * 1024 + 1024, [[n, 16], [1, n]])),
                    writes=[("kvs", g, b)])

    ring_owner = {}
    kvst_i = [0]
    ev_i = [0]
    BSKIP = os.environ.get("MK_BSKIP", "")
    for hp in range(int(os.environ.get("MK_HPS", "4"))):
        def proj(blkcol, hcol, n, out_fn, in_dims, reads, writes):
            need_tiles(reads)
            bank = proj_bank[0] % 2
            proj_bank[0] += 1
            for kc in range(8):
                S.op("pe", lambda e, kc=kc: e.matmul(pb(bank, 0, [[1, n]]),
                                                     lhsT=sb(wB, kc * 1152 + blkcol, [[1, 128]]),
                                                     rhs=hTa(kc, hcol, [[1, n]]),
                                                     start=(kc == 0), stop=(kc == 7)),
                     reads=["wB"] + reads, writes=[("bank", bank)])
            ev_i[0] += 1
            evac(ev_i[0], out_fn, lambda: pb(bank, 0, in_dims), [("bank", bank)], writes)

        for w in range(4):
            hres = hT_tiles(NH + 512 * w, 512)
            for g in range(3):
                r = RS[g]
                proj(g * 128, NH + 512 * w, 512,
                     lambda g=g, r=r, w=w: sb(qT, g * 2048 + 512 * w // r, [[2048 // r, r], [1, 512 // r]]),
                     [[1, r], [r, 512 // r]], hres, [("qT", g, w)])
                proj(384 + g * 128, NH + 512 * w, 512,
                     lambda g=g, r=r, w=w: sb(kT, KOFF[g] + 128 + 512 * w // r, [[LK[g], r], [1, 512 // r]]),
                     [[1, r], [r, 512 // r]], hres, [("kT", g, w)])

        def v_list(g):
            r = RS[g]
            vTg = vTb if g == 1 else vT
            vid = 1 if g == 1 else 0
            lst = []
            for w in range(4):
                lst.append(lambda w=w: proj(
                    768 + g * 128, NH + 512 * w, 512,
                    lambda: sb(vTg, 128 + 512 * w // r, [[LK[g], r], [1, 512 // r]]),
                    [[1, r], [r, 512 // r]], hT_tiles(NH + 512 * w, 512), [("vT", vid, w)]))
            if g == 0:
                lst.append(lambda: proj(768, NH - 128, 128, lambda: sb(vT, 0, [[1, 128]]), [[1, 128]],
                                        hT_tiles(NH - 128, 128), [("vTh", vid, 0)]))
            elif g == 1:
                lst.append(lambda: proj(768 + 128, NH - 512, 512, lambda: sb(vTb, 0, [[640, 4], [1, 128]]),
                                        [[1, 4], [4, 128]], hT_tiles(NH - 512, 512), [("vTh", vid, 0)]))
            else:
                for hw in range(4):
                    lst.append(lambda hw=hw: proj(
                        768 + 256, 512 * hw, 512, lambda: sb(vT, 32 * hw, [[256, 16], [1, 32]]),
                        [[1, 16], [16, 32]], hT_tiles(512 * hw, 512), [("vTh", vid, hw)]))
            return lst

        kh_list = [
            lambda: proj(384, NH - 128, 128, lambda: sb(kT, KOFF[0], [[1, 128]]), [[1, 128]],
                         hT_tiles(NH - 128, 128), [("kTh", 0, 0)]),
            lambda: proj(384 + 128, NH - 512, 512, lambda: sb(kT, KOFF[1], [[640, 4], [1, 128]]),
                         [[1, 4], [4, 128]], hT_tiles(NH - 512, 512), [("kTh", 1, 0)])]
        for hw in range(4):
            kh_list.append(lambda hw=hw: proj(
                384 + 256, 512 * hw, 512, lambda: sb(kT, KOFF[2] + 32 * hw, [[256, 16], [1, 32]]),
                [[1, 16], [16, 32]], hT_tiles(512 * hw, 512), [("kTh", 2, hw)]))

        def tblock(g, rho, kb, v):
            r = RS[g]
            vTg = vTb if g == 1 else vT
            vid = 1 if g == 1 else 0
            vres = [("vT", vid, w) for w in range(4)] + [("vTh", vid, hw) for hw in range(4)]
            kres_ = [("kT", g, w) for w in range(4)]
            needk = (kb == NQ[g])
            bank = 2 + kv_bank[0] % 4
            kv_bank[0] += 1
            ccol = rho * LK[g] + 128 * kb
            vcol = 128 if needk else 0
            if needk:
                S.op("pe", lambda e: e.transpose(
                    out=pbb(bank, 0, [[1, 128]]), in_=sb(kT, KOFF[g] + ccol, [[1, 128]]), identity=ident(128)),
                    reads=kres_ + ["cB"], writes=[("bank", bank)])
            S.op("pe", lambda e: e.transpose(
                out=pbb(bank, vcol, [[1, 128]]), in_=sb(vTg, ccol, [[1, 128]]), identity=ident(128)),
                reads=vres + ["cB"], writes=[("bank", bank)])
            S.op("dve", lambda e: e.tensor_copy(
                out=sb(Vg, v * 192, [[128, 2], [1, 64]]), in_=pbb(bank, vcol, [[64, 2], [1, 64]])),
                reads=[("bank", bank), "Vones"], writes=[("V", v)])
            if needk:
                sbuf_i = kvst_i[0] % 4
                kvst_i[0] += 1
                S.op("act", lambda e: e.activation(
                    out=kvst[sbuf_i][:], in_=pbb(bank, 0, [[1, 256]]), func=AF.Copy),
                    reads=[("bank", bank)], writes=[("kvst", sbuf_i)])
                S.dma("sp", lambda e: e.dma_start(
                    out=AP(kvp_d[g], rho * 1024 + hp * 128, [[r * 1024, 128], [512, 2], [1, 128]]),
                    in_=sb(kvst[sbuf_i], 0, [[128, 2], [1, 128]])),
                    reads=[("kvst", sbuf_i)])

        def interleave(plist, blist):
            per = -(-len(blist) // max(1, len(plist)))
            for i, pj in enumerate(plist):
                pj()
                for bl in blist[i * per:(i + 1) * per]:
                    tblock(*bl)
            for bl in blist[len(plist) * per:]:
                tblock(*bl)

        for pj in v_list(0):
            pj()
        interleave(v_list(1), [bl for bl in blocks if bl[0] == 0])
        interleave(v_list(2), [bl for bl in blocks if bl[0] == 1])
        interleave(kh_list, [bl for bl in blocks if bl[0] == 2])
        if BSTOP == "proj":
            return fin()
        need_tiles([("hT", 32)])
        bank = proj_bank[0] % 2
        proj_bank[0] += 1
        for blk in range(6):
            for kc in range(8):
                S.op("pe", lambda e, kc=kc, blk=blk: e.matmul(
                    pb(bank, blk * 4, [[1, 4]]), lhsT=sb(wB, kc * 1152 + blk * 128, [[1, 128]]),
                    rhs=hTa(kc, NH + NT, [[1, 4]]), start=(kc == 0), stop=(kc == 7)),
                    reads=["wB", ("hT", 32)], writes=[("bank", bank)])
        S.op("act", lambda e, hp=hp, bank=bank: e.activation(out=sb(qkvs, hp * 24, [[1, 24]]),
                                                          in_=pb(bank, 0, [[1, 24]]), func=AF.Copy),
             reads=[("bank", bank)], writes=[("qkvs", hp)])
        bankK = 2 + kv_bank[0] % 2
        kv_bank[0] += 1
        bankV = 2 + kv_bank[0] % 2
        kv_bank[0] += 1
        for (bk, c0) in ((bankK, 384), (bankV, 768)):
            for kc in range(8):
                S.op("pe", lambda e, kc=kc, bk=bk, c0=c0: e.matmul(
                    pb(bk, 0, [[1, 384]], 0, NS), lhsT=hTa(kc, NH + NT, [[1, 4]]),
                    rhs=sb(wB, kc * 1152 + c0, [[1, 384]]), start=(kc == 0), stop=(kc == 7)),
                    reads=["wB", ("hT", 32)], writes=[("bank", bk)])
        S.op("act", lambda e, bankK=bankK: e.activation(out=sb(kvn_st, 0, [[1, 384]], 0, NS),
                                                        in_=pb(bankK, 0, [[1, 384]], 0, NS), func=AF.Copy),
             reads=[("bank", bankK)], writes=["kvn_k"])
        S.op("dve", lambda e, bankV=bankV: e.tensor_copy(out=sb(kvn_st, 384, [[1, 384]], 0, NS),
                                                         in_=pb(bankV, 0, [[1, 384]], 0, NS)),
             reads=[("bank", bankV)], writes=["kvn_v"])
        S.op("dve", lambda e, bankV=bankV, hp=hp: e.tensor_copy(
            out=sb(vnb, hp * 128, [[512, 3], [1, 128]], 0, NS), in_=pb(bankV, 0, [[128, 3], [1, 128]], 0, NS)),
            reads=[("bank", bankV)], writes=[("vnb", hp)])
        for g in range(3):
            W = WIN[g]
            S.dma("sp", lambda e, g=g, W=W, hp=hp: e.dma_start(
                out=AP(kvs_d[g], (W - 1) * 1024 + hp * 128, [[W * 1024, NS], [512, 2], [1, 128]]),
                in_=sb(kvn_st, g * 128, [[384, 2], [1, 128]], 0, NS)),
                reads=["kvn_k", "kvn_v"])


        if BSTOP == "smp":
            return fin()
        if hp == 0:
            assert len(a_done) == 33, a_done
            S.barrier()
            issue_cache_copies()
        if hp + 1 < 4:
            load_wB(hp + 1)
        def score(g, rho, kb, X):
            r = RS[g]
            nq = NQ[g]
            slot = sc_slot[0] % 5
            sc_slot[0] += 1
            bank = slot
            pcol = 0
            kcol = KOFF[g] + rho * LK[g] + 128 * kb
            if kb == 0:
                qcol, n, half0 = 0, 128, 1
            elif kb == nq:
                qcol, n, half0 = 128 * (nq - 1), 128, 0
            else:
                qcol, n, half0 = 128 * (kb - 1), 256, 0
            qcol += g * 2048 + rho * (2048 // r)
            qw = [("qT", g, w) for w in range(4)]
            kres = [("kT", g, w) for w in range(4)] + [("kTh", g, hw) for hw in range(4)]
            S.op("pe", lambda e: e.matmul(
                pb(bank, pcol, [[1, n]]), lhsT=sb(kT, kcol, [[1, 128]], 64 * X, 64),
                rhs=sb(qT, qcol, [[1, n]], 64 * X, 64), start=True, stop=True),
                reads=qw + kres, writes=[("bank", bank)])
            if g == 2:
                t, tcol = PT2, (rho * 2 + kb) * 128
                res = ("PT2", rho, kb)
            elif g == 1:
                t, tcol = PT1, (rho * 3 + kb % 3) * 256 + half0 * 128
                res = ("PT1", rho, kb % 3)
            else:
                t, tcol = PT0, (kb % 10) * 256 + half0 * 128
                res = ("PT0", kb % 10)
            ring_owner[res] = kb
            S.op("act", lambda e: e.activation(
                out=sb(t, tcol, [[1, n]]), in_=pb(bank, pcol, [[1, n]]), func=AF.Exp, scale=0.125),
                reads=[("bank", bank), "Vones"], writes=[res])
            if kb == 0:
                mk = lambda: maskh[:]
                mres = ["maskh"]
            else:
                mk = lambda: cBa(CB_MASK, n)
                mres = ["cB"]
            S.op("dve", lambda e: e.tensor_tensor(
                out=sb(t, tcol, [[1, n]]), in0=sb(t, tcol, [[1, n]]), in1=mk(), op=ALU.mult),
                reads=[res] + mres, writes=[res])

        def ptap(g, rho, kb, half, c0, n):
            if g == 2:
                return sb(PT2, (rho * 2 + kb) * 128 + c0, [[1, n]]), ("PT2", rho, kb)
            if g == 1:
                assert ring_owner[("PT1", rho, kb % 3)] == kb
                return (sb(PT1, (rho * 3 + kb % 3) * 256 + half * 128 + c0, [[1, n]]), ("PT1", rho, kb % 3))
            assert ring_owner[("PT0", kb % 10)] == kb
            return sb(PT0, (kb % 10) * 256 + half * 128 + c0, [[1, n]]), ("PT0", kb % 10)

        for X in range(0 if "sc" in BSKIP else 2):
            dp = 64 if X == 0 else 0
            u0 = 0 if X == 0 else 64

            def Sw(w, X=X):
                lo_kb = 0 if w == 0 else 4 * w + 1
                for kb in range(lo_kb, 4 * w + 5):
                    score(0, 0, kb, X)
                for rho in range(4):
                    if w == 0:
                        score(1, rho, 0, X)
                    score(1, rho, w + 1, X)

            def PVw(w, X=X):
                ab = 6 + w % 2
                ares = ("bank", ab)
                S.op("pe", lambda e: e.matmul(pb(ab, 0, [[1, 512]]), lhsT=cBa(CB_ZERO, 128),
                                              rhs=cBa(CB_ZERO, 512), start=True, stop=False),
                     reads=["cB"], writes=[ares])

                def pv(g, rho, kb, half, c0, n, out_ap, last=False):
                    rhs, res = ptap(g, rho, kb, half, c0, n)
                    v = VI[(g, rho, kb)]
                    S.op("pe", lambda e: e.matmul(out_ap, lhsT=sb(Vg, v * 192 + 64 * X, [[1, 128]]), rhs=rhs,
                                                  start=False, stop=last),
                         reads=[res, ("V", v), "Vones"], writes=[ares])

                for qb in range(4 * w, 4 * w + 4):
                    oc = 128 * (qb - 4 * w)
                    pv(0, 0, qb + 1, 0, 0, 128, pb(ab, oc, [[1, 128]]))
                    pv(0, 0, qb, 1, 0, 128, pb(ab, oc, [[1, 128]]))
                for rho in range(4):
                    pv(1, rho, w + 1, 0, 0, 128, pb(ab, rho, [[4, 128]]))
                    pv(1, rho, w, 1, 0, 128, pb(ab, rho, [[4, 128]]))
                for rho in range(16):
                    pv(2, rho, 1, 0, 32 * w, 32, pb(ab, rho, [[16, 32]]))
                    pv(2, rho, 0, 0, 32 * w, 32, pb(ab, rho, [[16, 32]]), last=(rho == 15))

            def N1(w):
                ab = 6 + w % 2
                ares = ("bank", ab)
                S.op("act", lambda e: e.activation(out=sb(dnb, 0, [[1, 512]], dp, 1),
                                                   in_=pb(ab, 0, [[1, 512]], dp, 1), func=AF.Ln),
                     reads=[ares], writes=["dn"])
                S.op("act", lambda e: e.activation(out=sb(dnb, 0, [[1, 512]], dp, 1),
                                                   in_=sb(dnb, 0, [[1, 512]], dp, 1), func=AF.Exp, scale=-1.0),
                     reads=["dn"], writes=["dn"])
                S.op("dve", lambda e: e.tensor_copy(out=sb(hib, 0, [[1, 512]], dp, 1),
                                                    in_=sb(dnb, 0, [[1, 512]], dp, 1)),
                     reads=["dn"], writes=["hi"])
                S.op("dve", lambda e: e.tensor_tensor(out=sb(lob, 0, [[1, 512]], dp, 1),
                                                      in0=sb(dnb, 0, [[1, 512]], dp, 1),
                                                      in1=sb(hib, 0, [[1, 512]], dp, 1), op=ALU.subtract),
                     reads=["dn", "hi"], writes=["lo"])

            def N2(w, X=X):
                ab = 6 + w % 2
                ares = ("bank", ab)
                bb = 5
                S.op("pe", lambda e: e.matmul(pb(bb, 0, [[1, 512]]), lhsT=cBa(CB_ONE, 128, dp, 1),
                                              rhs=sb(hib, 0, [[1, 512]], dp, 1), start=True, stop=False),
                     reads=["hi", "cB"], writes=[("bank", bb)])
                S.op("pe", lambda e: e.matmul(pb(bb, 0, [[1, 512]]), lhsT=cBa(CB_ONE, 128, dp, 1),
                                              rhs=sb(lob, 0, [[1, 512]], dp, 1), start=False, stop=True),
                     reads=["lo", "cB"], writes=[("bank", bb)])
                S.op("act", lambda e: e.activation(out=sb(bcs, 0, [[1, 512]], u0, 64),
                                                   in_=pb(bb, 0, [[1, 512]], u0, 64), func=AF.Copy),
                     reads=[("bank", bb)], writes=["bcs"])
                S.op("dve", lambda e: e.tensor_tensor(
                    out=sb(attnT, hp * 2052 + 512 * w, [[1, 512]], u0, 64),
                    in0=pb(ab, 0, [[1, 512]], u0, 64), in1=sb(bcs, 0, [[1, 512]], u0, 64), op=ALU.mult),
                    reads=[ares, "bcs"], writes=[("attnT", hp, w, X)])

            g2_tiles = [(rho, kb) for rho in range(16) for kb in range(2)]
            n_pre = 10 if X == 1 else 0
            for (rho, kb) in g2_tiles[n_pre:]:
                score(2, rho, kb, X)
            Sw(0)
            Sw(1)
            PVw(0); N1(0); Sw(2); N2(0)
            PVw(1); N1(1); Sw(3); N2(1)
            PVw(2); N1(2); PVw(3); N2(2)
            N1(3)
            if X == 0:
                for (rho, kb) in g2_tiles[:10]:
                    score(2, rho, kb, 1)
            N2(3)

    if STOP == "B":
        return fin()
    if ONLY == "Bs":
        S.enabled = True
    S.barrier()
    for b in range(NS):
        S.op("dve", lambda e, b=b: e.tensor_tensor(
            out=sb(Qbd, b * 8, [[96, 4], [32, 3], [1, 8]]),
            in0=sb(cF, CF_BM, [[8, 4], [0, 3], [1, 8]]),
            in1=sb(qkvs, b, [[24, 4], [4, 3], [0, 8]]), op=ALU.mult),
            reads=[("qkvs", hp) for hp in range(4)] + ["cF"], writes=[("Qbd", b)])
    for b in range(NS):
        for g in range(3):
            W, dil = WIN[g], RS[g]
            si = (b * 3 + g) % 2
            KVs, KVb, KTs, PTs, PTn, PTnm = KVs2[si], KVb2[si], KTs2[si], PTs2[si], PTn2[si], PTnm2[si]
            tb_, sbk = (0, 1) if si == 0 else (5, 6)
            S.dma("sp", lambda e, g=g, b=b, W=W, dil=dil: e.dma_start(
                out=KVs[:], in_=AP(cache_d[g], b * W * 1024, [[dil * 1024, 128], [1, 1024]])),
                writes=[("KVs", si)])
            S.op("dve", lambda e: e.tensor_copy(out=KVb[:], in_=KVs[:]), reads=[("KVs", si)], writes=[("KVb", si)])
            for c in range(4):
                S.op("pe", lambda e, c=c: e.transpose(out=pbb(tb_, c * 128, [[1, 128]]),
                                                      in_=sb(KVb, c * 128, [[1, 128]]), identity=ident(128)),
                     reads=[("KVb", si), "cB"], writes=[("bank", tb_)])
            S.op("act", lambda e: e.activation(out=KTs[:], in_=pbb(tb_, 0, [[1, 512]]), func=AF.Copy),
                 reads=[("bank", tb_)], writes=[("KTs", si)])
            for c in range(4):
                S.op("pe", lambda e, c=c, g=g, b=b: e.matmul(
                    pb(sbk, 0, [[1, 8]]), lhsT=sb(KTs, c * 128, [[1, 128]]),
                    rhs=sb(Qbd, c * 96 + g * 32 + b * 8, [[1, 8]]), start=(c == 0), stop=(c == 3)),
                    reads=[("KTs", si), ("Qbd", b)], writes=[("bank", sbk)])
            for c in range(4):
                S.op("pe", lambda e, c=c, g=g, b=b: e.matmul(
                    pb(sbk, 8, [[1, 8]], 0, NS), lhsT=sb(qkvs, c * 24 + (3 + g) * 4, [[1, 4]]),
                    rhs=sb(Qbd, c * 96 + g * 32 + b * 8, [[1, 8]]), start=(c == 0), stop=(c == 3)),
                    reads=[("qkvs", hp) for hp in range(4)] + [("Qbd", b)], writes=[("bank", sbk)])
            S.op("act", lambda e: e.activation(out=PTs[:], in_=pb(sbk, 0, [[1, 8]]), func=AF.Exp, scale=0.125),
                 reads=[("bank", sbk)], writes=[("PTs", si)])
            S.op("act", lambda e: e.activation(out=sb(PTn, 0, [[1, 8]], 0, NS), in_=pb(sbk, 8, [[1, 8]], 0, NS),
                                               func=AF.Exp, scale=0.125),
                 reads=[("bank", sbk)], writes=[("PTn", si)])
            S.op("dve", lambda e, b=b: e.tensor_scalar(out=sb(PTnm, 0, [[1, 8]], 0, NS),
                                                       in0=sb(PTn, 0, [[1, 8]], 0, NS),
                                                       scalar1=cFa(CF_OH + b, 1, 0, NS), scalar2=None, op0=ALU.mult),
                 reads=[("PTn", si), "cF"], writes=[("PTnm", si)])
            S.op("pe", lambda e, g=g: e.matmul(pb(2, 0, [[1, 512]], 0, 8), lhsT=PTs[:],
                                               rhs=sb(KVb, 512, [[1, 512]]), start=(g == 0), stop=False),
                 reads=[("PTs", si), ("KVb", si)], writes=[("bank", 2)])
            S.op("pe", lambda e, g=g: e.matmul(pb(2, 0, [[1, 512]], 0, 8), lhsT=sb(PTnm, 0, [[1, 8]], 0, NS),
                                               rhs=sb(vnb, g * 512, [[1, 512]], 0, NS), start=False, stop=(g == 2)),
                 reads=[("PTnm", si)] + [("vnb", hp) for hp in range(4)], writes=[("bank", 2)])
            S.op("pe", lambda e, g=g: e.matmul(pb(3, 0, [[1, 1]], 0, 8), lhsT=PTs[:], rhs=cBa(CB_ONE, 1),
                                               start=(g == 0), stop=False),
                 reads=[("PTs", si), "cB"], writes=[("bank", 3)])
            S.op("pe", lambda e, g=g: e.matmul(pb(3, 0, [[1, 1]], 0, 8), lhsT=sb(PTnm, 0, [[1, 8]], 0, NS),
                                               rhs=cBa(CB_ONE, 1, 0, NS), start=False, stop=(g == 2)),
                 reads=[("PTnm", si), "cB"], writes=[("bank", 3)])
        S.op("dve", lambda e: e.reciprocal(out=sb(rdn, 0, [[1, 1]], 0, 8), in_=pb(3, 0, [[1, 1]], 0, 8)),
             reads=[("bank", 3)], writes=["rdn"])
        S.op("dve", lambda e: e.scalar_tensor_tensor(out=sb(Om, 0, [[1, 512]], 0, 8), in0=pb(2, 0, [[1, 512]], 0, 8),
                                                     scalar=sb(rdn, 0, [[1, 1]], 0, 8),
                                                     in1=cBa(CB_BLK, 512, 0, 8), op0=ALU.mult, op1=ALU.mult),
             reads=[("bank", 2), "rdn", "cB"], writes=["Om"])
        for c in range(4):
            S.op("pe", lambda e, c=c: e.matmul(pb(4, c, [[1, 1]]), lhsT=sb(Om, c * 128, [[1, 128]], 0, 8),
                                               rhs=cBa(CB_ONE, 1, 0, 8), start=True, stop=True),
                 reads=["Om", "cB"], writes=[("bank", 4)])
        S.op("act", lambda e, b=b: e.activation(out=sb(attnT, 2048 + b, [[2052, 4]]), in_=pb(4, 0, [[1, 4]]),
                                                func=AF.Copy),
             reads=[("bank", 4)], writes=[("attnTs", b)])

    if STOP == "Bs":
        return fin()
    if ONLY == "C1":
        S.enabled = True
    S.barrier()
    def load_wC2(oc):
        S.dma("pool", lambda e: e.dma_start(
            out=sb(wC2[oc % 2], 0, [[1792, 2], [1, 1792]]),
            in_=AP(wC2_d, oc * 128 * 3584, [[3584, 128], [1792, 2], [1, 1792]])), writes=[("wC2", oc % 2)])

    def load_wC1(fc):
        S.dma("pool", lambda e: e.dma_start(
            out=sb(wC1[fc % 2], 0, [[1536, 2], [1, 1536]]),
            in_=AP(wC1_d, fc * 128 * 3072, [[3072, 128], [1536, 2], [1, 1536]])), writes=[("wC1", fc % 2)])

    wins = [(NH - 2, 2, "h")] + [(NH + 512 * w, 512, w) for w in range(4)] + [(NH + NT, NS, "s")]
    wi = 0
    for fc in range(8):
        wb = wC1[fc % 2]
        if fc == 0:
            load_wC1(0)
        if fc + 1 < 8:
            load_wC1(fc + 1)
        if fc == 6:
            load_wC2(0)
        u = ub[fc % 2]
        ures = ("u", fc % 2)
        for (hcol, n, kind) in wins:
            bks = {0: wi % 3, 1: 3 + wi % 2, 2: 5 + wi % 2}
            wi += 1
            blks = (1, 2) if kind == "h" else (1, 2, 0)
            for bi in blks:
                for kc in range(8):
                    S.op("pe", lambda e, kc=kc, bi=bi, bk=bks[bi], hcol=hcol, n=n, wb=wb: e.matmul(
                        pb(bk, 0, [[1, n]]), lhsT=sb(wb, kc * 384 + bi * 128, [[1, 128]]),
                        rhs=hTa(kc, hcol, [[1, n]]), start=(kc == 0), stop=(kc == 7)),
                        reads=[("wC1", fc % 2)] + hT_tiles(hcol, n), writes=[("bank", bks[bi])])
            cs = ccs[wi % 2]
            csr = ("ccs", wi % 2)
            S.op("act", lambda e, cs=cs, bk=bks[1], n=n: e.activation(out=sb(cs, 0, [[1, n]]), in_=pb(bk, 0, [[1, n]]),
                                                                func=AF.Copy),
                 reads=[("bank", bks[1])], writes=[csr])
            ucol = 0 if kind == "h" else (2050 if kind == "s" else 2 + 512 * kind)
            uw = (ures, kind)
            S.op("dve", lambda e, u=u, ucol=ucol, n=n, bk=bks[2], cs=cs: e.tensor_tensor(
                out=sb(u, ucol, [[1, n]]), in0=pb(bk, 0, [[1, n]]), in1=sb(cs, 0, [[1, n]]), op=ALU.mult),
                reads=[("bank", bks[2]), csr], writes=[uw])
            if kind == "h":
                continue
            tb = t1b[wi % 2]
            tr = ("t1", wi % 2)
            cw = lambda j, fc=fc: cFa(CF_CW + fc * 3 + j)
            if kind == "s":
                prev = [ures, "s"]
                S.op("dve", lambda e, u=u, tb=tb, cw=cw: e.tensor_scalar(
                    out=sb(tb, 0, [[1, NS]]), in0=sb(u, 2050, [[1, NS]]), scalar1=cw(2), scalar2=None, op0=ALU.mult),
                    reads=[uw, "cF"], writes=[tr])
                for j in range(2):
                    S.op("dve", lambda e, tb=tb, cw=cw, j=j, fc=fc: e.scalar_tensor_tensor(
                        out=sb(tb, 0, [[1, NS]]), in0=sb(cF, CF_ST + fc * 8 + j * 4, [[1, NS]]), scalar=cw(j),
                        in1=sb(tb, 0, [[1, NS]]), op0=ALU.mult, op1=ALU.add),
                        reads=[tr, "cF"], writes=[tr])
                ocol = fc * 2052 + 2048
            else:
                w = kind
                pr = [(ures, w - 1)] if w > 0 else [(ures, "h")]
                S.op("act", lambda e, u=u, tb=tb, cw=cw, ucol=ucol: e.activation(
                    out=sb(tb, 0, [[1, 512]]), in_=sb(u, ucol, [[1, 512]]), func=AF.Copy, scale=cw(2)),
                    reads=[uw, "cF"], writes=[tr])
                for j in (1, 0):
                    S.op("dve", lambda e, u=u, tb=tb, cw=cw, ucol=ucol, j=j: e.scalar_tensor_tensor(
                        out=sb(tb, 0, [[1, 512]]), in0=sb(u, ucol - (2 - j), [[1, 512]]), scalar=cw(j),
                        in1=sb(tb, 0, [[1, 512]]), op0=ALU.mult, op1=ALU.add),
                        reads=[uw, tr, "cF"] + pr, writes=[tr])
                ocol = fc * 2052 + 512 * w
            S.op("dve", lambda e, bk=bks[0], n=n, tb=tb, ocol=ocol: e.tensor_tensor(
                out=sb(cbvT, ocol, [[1, n]]), in0=pb(bk, 0, [[1, n]]), in1=sb(tb, 0, [[1, n]]), op=ALU.mult),
                reads=[("bank", bks[0]), tr], writes=[("cbvT", fc, kind)])
        S.op("pool", lambda e, u=u, fc=fc: e.tensor_copy(out=sb(convp_sb, fc * 2, [[1, 2]]), in_=sb(u, 2048, [[1, 2]])),
             reads=[(ures, 3)], writes=["convp_sb"])
        S.op("pool", lambda e, u=u, fc=fc: e.tensor_copy(out=sb(convs_sb, fc * 4, [[1, 4]]), in_=sb(u, 2050, [[1, 4]])),
             reads=[(ures, "s")], writes=["convs_sb"])
    S.dma("sp", lambda e: e.dma_start(out=convp_d.ap(), in_=convp_sb[:]), reads=["convp_sb"])
    S.dma("sp", lambda e: e.dma_start(out=convs1_d.ap(), in_=convs_sb[:]), reads=["convs_sb"])

    if STOP == "C1":
        return fin()
    if ONLY == "C2":
        S.enabled = True
    S.barrier()
    S.dma("pool", lambda e: e.dma_start(out=sb(wo, 0, [[2048, 4], [1, 2048]]),
                                        in_=AP(wO_d, 0, [[8192, 128], [2048, 4], [1, 2048]])), writes=["wo"])
    wins2 = [(512 * w, 512) for w in range(4)] + [(NT, NS)]
    wi = 0
    for oc in range(8):
        wb = wC2[oc % 2]
        wr = ("wC2", oc % 2)
        if oc + 1 < 8:
            load_wC2(oc + 1)
        for (c0, n) in wins2:
            s4 = 4 * (wi % 2)
            wi += 1
            k2 = wi % 2
            hres = hT_tiles(NH + c0, n)
            for gi in range(2):
                for kc in range(8):
                    S.op("pe", lambda e, kc=kc, gi=gi, s4=s4, c0=c0, n=n, wb=wb: e.matmul(
                        pb(s4 + gi, 0, [[1, n]]), lhsT=sb(wb, kc * 256 + gi * 128, [[1, 128]]),
                        rhs=hTa(kc, NH + c0, [[1, n]]), start=(kc == 0), stop=(kc == 7)),
                        reads=[wr] + hres, writes=[("bank", s4 + gi)])
            if n == 512:
                ares_ = [("attnT", hp, c0 // 512, X) for hp in range(4) for X in range(2)]
            else:
                ares_ = [("attnTs", b) for b in range(NS)]
            for kc in range(4):
                S.op("pe", lambda e, kc=kc, s4=s4, c0=c0, n=n, wb=wb: e.matmul(
                    pb(s4 + 3, 0, [[1, n]]), lhsT=sb(wb, 3072 + kc * 128, [[1, 128]]),
                    rhs=sb(attnT, kc * 2052 + c0, [[1, n]]), start=(kc == 0), stop=(kc == 3)),
                    reads=[wr] + ares_, writes=[("bank", s4 + 3)])
            cres = [("cbvT", fc, k) for fc in range(8) for k in ((c0 // 512,) if n == 512 else ("s",))]
            for kc in range(8):
                S.op("pe", lambda e, kc=kc, s4=s4, c0=c0, n=n, wb=wb: e.matmul(
                    pb(s4 + 2, 0, [[1, n]]), lhsT=sb(wb, 2048 + kc * 128, [[1, 128]]),
                    rhs=sb(cbvT, kc * 2052 + c0, [[1, n]]), start=(kc == 0), stop=(kc == 7)),
                    reads=[wr] + cres, writes=[("bank", s4 + 2)])
            S.op("act", lambda e, s4=s4, n=n, k2=k2: e.activation(out=sb(sab[k2], 0, [[1, n]]), in_=pb(s4, 0, [[1, n]]),
                                                              func=AF.Sigmoid),
                 reads=[("bank", s4)], writes=[("sa", k2)])
            S.op("act", lambda e, s4=s4, n=n, k2=k2: e.activation(out=sb(scb[k2], 0, [[1, n]]), in_=pb(s4 + 1, 0, [[1, n]]),
                                                              func=AF.Sigmoid),
                 reads=[("bank", s4 + 1)], writes=[("sc_", k2)])
            S.op("dve", lambda e, s4=s4, n=n, k2=k2: e.tensor_tensor(
                out=sb(tab[k2], 0, [[1, n]]), in0=pb(s4 + 3, 0, [[1, n]]), in1=sb(sab[k2], 0, [[1, n]]), op=ALU.mult),
                reads=[("bank", s4 + 3), ("sa", k2)], writes=[("ta", k2)])
            S.op("dve", lambda e, s4=s4, n=n, k2=k2: e.tensor_tensor(
                out=sb(tcb[k2], 0, [[1, n]]), in0=pb(s4 + 2, 0, [[1, n]]), in1=sb(scb[k2], 0, [[1, n]]), op=ALU.mult),
                reads=[("bank", s4 + 2), ("sc_", k2)], writes=[("tc", k2)])
            S.op("pool", lambda e, n=n, k2=k2, oc=oc, c0=c0: e.tensor_tensor(
                out=sb(mixT, oc * 2052 + c0, [[1, n]]), in0=sb(tab[k2], 0, [[1, n]]), in1=sb(tcb[k2], 0, [[1, n]]),
                op=ALU.add),
                reads=[("ta", k2), ("tc", k2)], writes=[("mixT", oc, c0)])

    if STOP == "C2":
        return fin()
    if ONLY == "D":
        S.enabled = True
    S.barrier()
    cur_junk[0] = junkD
    p2_args = []
    S.dma("sp", lambda e: e.dma_start(out=gB3[:], in_=AP(g3_d, 0, [[0, 128], [1, D]])), writes=["gB3"])
    S.dma("sp", lambda e: e.dma_start(out=gB2[:], in_=AP(g2_d, 0, [[0, 128], [1, D]])), writes=["gB2"])
    S.dma("pool", lambda e: e.dma_start(out=sb(wF[0], 0, [[2048, 4], [1, 2048]]),
                                        in_=AP(wF_d, 0, [[8192, 128], [2048, 4], [1, 2048]])), writes=[("wF", 0)])

    def rstd_from_psum(tt, np_, pt, col):
        for h in range(2):
            S.op("act", lambda e, h=h: e.activation(out=sb(junkD, 0, [[1, 512]], 0, np_),
                                                   in_=AP(PS[pt], h * 512, [[1024, np_], [1, 512]]),
                                                   func=AF.Square, accum_out=sb(stat2, col + h, [[1, 1]], 0, np_)),
                 reads=[("bank", 2 * pt + h)], writes=[("st", col + h), "junk"])
        S.op("dve", lambda e: e.tensor_tensor(out=sb(stat2, col + 2, [[1, 1]], 0, np_),
                                              in0=sb(stat2, col, [[1, 1]], 0, np_),
                                              in1=sb(stat2, col + 1, [[1, 1]], 0, np_), op=ALU.add),
             reads=[("st", col), ("st", col + 1)], writes=[("st", col + 2)])
        S.op("act", lambda e: e.activation(out=sb(stat2, col + 2, [[1, 1]], 0, np_),
                                           in_=sb(stat2, col + 2, [[1, 1]], 0, np_), func=AF.Sqrt,
                                           bias=eps_ap(np_), scale=1.0 / D),
             reads=[("st", col + 2), "cF"], writes=[("st", col + 2)])
        S.op("dve", lambda e: e.reciprocal(out=sb(stat2, col + 3, [[1, 1]], 0, np_),
                                           in_=sb(stat2, col + 2, [[1, 1]], 0, np_)),
             reads=[("st", col + 2)], writes=[("st", col + 3)])
        return sb(stat2, col + 3, [[1, 1]], 0, np_), ("st", col + 3)

    for tt in range(17):
        np_ = 128 if tt < 16 else NS
        c0 = tt * 128
        pt = tt % 3
        b2 = tt % 4
        bx = tt % 4
        tD = tDb[tt % 3]
        mres = [("mixT", oc, (c0 // 512) * 512 if tt < 16 else NT) for oc in range(8)]
        for h in range(2):
            for kc in range(8):
                S.op("pe", lambda e, kc=kc, h=h, pt=pt, c0=c0, np_=np_: e.matmul(
                    AP(PS[pt], h * 512, [[1024, np_], [1, 512]]), lhsT=sb(mixT, kc * 2052 + c0, [[1, np_]]),
                    rhs=sb(wo, kc * 1024 + h * 512, [[1, 512]]), start=(kc == 0), stop=(kc == 7)),
                    reads=["wo"] + mres, writes=[("bank", 2 * pt + h)])
        if tt >= 3 and tt % 2 == 1:
            norm_p2(*p2_args[tt - 3])
            norm_p2(*p2_args[tt - 2])
        rs_ap, rs_res = rstd_from_psum(tt, np_, pt, 64 + (tt % 4) * 8)
        src = AP(x_all, (NH + c0) * D, [[D, 128], [1, D]]) if tt < 16 else xs_d.ap()
        S.dma("sp", lambda e, bx=bx, np_=np_, src=src: e.dma_start(out=sb(xtD[bx], 0, [[1, 1024]], 0, np_), in_=src),
              writes=[("xtD", bx)])
        S.op("dve", lambda e, pt=pt, np_=np_, rs_ap=rs_ap: e.scalar_tensor_tensor(
            out=sb(tD, 0, [[1, 1024]], 0, np_), in0=AP(PS[pt], 0, [[1024, np_], [1, 1024]]), scalar=rs_ap,
            in1=sb(gB2, 0, [[1, 1024]], 0, np_), op0=ALU.mult, op1=ALU.mult),
            reads=[("bank", 2 * pt), ("bank", 2 * pt + 1), rs_res, "gB2"], writes=[("tD", tt % 3)])
        S.op("pool", lambda e, b2=b2, bx=bx, np_=np_: e.tensor_tensor(
            out=sb(x1t[b2], 0, [[1, 1024]], 0, np_), in0=sb(tD, 0, [[1, 1024]], 0, np_),
            in1=sb(xtD[bx], 0, [[1, 1024]], 0, np_), op=ALU.add),
            reads=[("tD", tt % 3), ("xtD", bx)], writes=[("x1t", b2)])
        dst = AP(y_d, c0 * D, [[D, 128], [1, D]]) if tt < 16 else ys_d.ap()
        S.dma("sp", lambda e, b2=b2, np_=np_, dst=dst: e.dma_start(out=dst, in_=sb(x1t[b2], 0, [[1, 1024]], 0, np_)),
              reads=[("x1t", b2)], writes=[("yd", tt)])
        norm_p1(lambda b2=b2, np_=np_: sb(x1t[b2], 0, [[1, 1024]], 0, np_), ("x1t", b2), np_,
                xn2[b2], ("xn2", b2), gB3, "gB3", 128 + (tt % 4) * 4)
        p2_args.append((np_, xn2[b2], ("xn2", b2), 6 + (tt % 2),
                        (lambda c0=c0, np_=np_: sb(h2T, c0, [[2052, 8], [1, np_]])), ("h2T", tt), tt))
    norm_p2(*p2_args[14])
    norm_p2(*p2_args[15])
    norm_p2(*p2_args[16])
    assert D_P2_COUNT[0] == 17

    if STOP == "D":
        return fin()
    if ONLY == "E":
        S.enabled = True
    S.barrier()
    S.dma("sp", lambda e: e.dma_start(out=gB4[:], in_=AP(g4_d, 0, [[0, 128], [1, D]])), writes=["gB4"])
    winsE = [(512 * w, 512, [("h2T", 4 * w + i) for i in range(4)]) for w in range(4)] + [(NT, NS, [("h2T", 16)])]
    fb = 0
    fin_q = []
    ytE3 = [tE] + ytE
    for cg in range(8):
        wb = wF[cg % 2]
        wr = ("wF", cg % 2)
        if cg + 1 < 8:
            S.dma("pool", lambda e, cg=cg: e.dma_start(
                out=sb(wF[(cg + 1) % 2], 0, [[2048, 4], [1, 2048]]),
                in_=AP(wF_d, (cg + 1) * 128 * 8192, [[8192, 128], [2048, 4], [1, 2048]])),
                writes=[("wF", (cg + 1) % 2)])
        if ESTOP == "f0":
            return fin()
        for (c0, n, hres) in winsE:
            if ESTOP == "f1a" and c0 > 0:
                return fin()
            for c in range(4):
                bank = fb % 4
                fb += 1
                for kc in range(8):
                    if ESTOP == "f1b" and (kc > 0 or c > 0):
                        return fin()
                    if ESTOP.startswith("n") and (c * 8 + kc) >= int(ESTOP[1:]):
                        return fin()
                    S.op("pe", lambda e, kc=kc, c=c, bank=bank, c0=c0, n=n, wb=wb: e.matmul(
                        pb(bank, 0, [[1, n]]), lhsT=sb(wb, kc * 512 + c * 128, [[1, 128]]),
                        rhs=sb(h2T, kc * 2052 + c0, [[1, n]]), start=(kc == 0), stop=(kc == 7)),
                        reads=[wr] + hres, writes=[("bank", bank)])
                rb = fb % 2
                if os.environ.get("MK_EV") == "c":
                    continue
                S.op("act", lambda e, bank=bank, n=n, rb=rb: e.activation(out=sb(rlb[rb], 0, [[1, n]]), in_=pb(bank, 0, [[1, n]]),
                                                                func=(AF.Copy if os.environ.get("MK_EV") == "b" else AF.Relu)),
                     reads=[("bank", bank)], writes=[("rl", rb)])
                S.op(("dve" if os.environ.get("MK_EV") == "a" else "pool"), lambda e, c=c, c0=c0, n=n, rb=rb: e.tensor_tensor(
                    out=sb(f1T, c * 2052 + c0, [[1, n]]), in0=sb(rlb[rb], 0, [[1, n]]), in1=sb(rlb[rb], 0, [[1, n]]),
                    op=ALU.mult),
                    reads=[("rl", rb)], writes=[("f1T", c, c0)])
        if ESTOP == "f1":
            return fin()
        for tt in range(17):
            if ESTOP == "f2" and cg == 1:
                return fin()
            np_ = 128 if tt < 16 else NS
            c0 = tt * 128
            pt = 2 + tt % 2
            fres = [("f1T", c, (c0 // 512) * 512 if tt < 16 else NT) for c in range(4)]
            for h in range(2):
                for c in range(4):
                    S.op("pe", lambda e, c=c, h=h, pt=pt, c0=c0, np_=np_, wb=wb: e.matmul(
                        AP(PS[pt], h * 512, [[1024, np_], [1, 512]]), lhsT=sb(f1T, c * 2052 + c0, [[1, np_]]),
                        rhs=sb(wb, 4096 + c * 1024 + h * 512, [[1, 512]]), start=(c == 0), stop=(c == 3)),
                        reads=[wr] + fres, writes=[("bank", 2 * pt + h)])
            fa = (lambda tt=tt, np_=np_: sb(facc, tt * 1024, [[1, 1024]], 0, np_)) if tt < 16 else \
                 (lambda np_=np_: sb(faccs, 0, [[1, 1024]], 0, np_))
            fr = ("facc", tt)
            pin = lambda pt=pt, np_=np_: AP(PS[pt], 0, [[1024, np_], [1, 1024]])
            breads = [("bank", 2 * pt), ("bank", 2 * pt + 1)]
            if cg == 0:
                S.op("act", lambda e, fa=fa, pin=pin: e.activation(out=fa(), in_=pin(), func=AF.Copy),
                     reads=breads, writes=[fr])
            else:
                S.op("dve", lambda e, fa=fa, pin=pin: e.tensor_tensor(out=fa(), in0=pin(), in1=fa(), op=ALU.add),
                     reads=breads + [fr], writes=[fr])
            if cg == 7 and ESTOP != "nofin":
                col = 192 + (tt % 4) * 8
                ss = sb(stat2, col, [[1, 1]], 0, np_)
                S.op("act", lambda e, fa=fa, np_=np_, ss=ss: e.activation(out=sb(junkE, 0, [[1, 1024]], 0, np_), in_=fa(),
                                                                     func=AF.Square, accum_out=ss),
                     reads=[fr], writes=[("st", col), "junk"])
                S.op("act", lambda e, ss=ss, np_=np_: e.activation(out=ss, in_=ss, func=AF.Sqrt, bias=eps_ap(np_),
                                                              scale=1.0 / D),
                     reads=[("st", col), "cF"], writes=[("st", col)])
                rs = sb(stat2, col + 1, [[1, 1]], 0, np_)
                S.op("dve", lambda e, ss=ss, rs=rs: e.reciprocal(out=rs, in_=ss), reads=[("st", col)],
                     writes=[("st", col + 1)])
                src = AP(y_d, c0 * D, [[D, 128], [1, D]]) if tt < 16 else ys_d.ap()
                bx = tt % 2
                by = tt % 3
                S.dma("sp", lambda e, bx=bx, np_=np_, src=src: e.dma_start(out=sb(xrE[bx], 0, [[1, 1024]], 0, np_), in_=src),
                      reads=[("yd", tt)], writes=[("xrE", bx)])

                def fin2(tt=tt, np_=np_, fa=fa, fr=fr, rs=rs, col=col, bx=bx, by=by, c0=c0):
                    S.op("dve", lambda e: e.scalar_tensor_tensor(
                        out=sb(ytE3[by], 0, [[1, 1024]], 0, np_), in0=fa(), scalar=rs, in1=sb(gB4, 0, [[1, 1024]], 0, np_),
                        op0=ALU.mult, op1=ALU.mult),
                        reads=[fr, ("st", col + 1), "gB4"], writes=[("ytE", by)])
                    S.op("pool", lambda e: e.tensor_tensor(
                        out=sb(ytE3[by], 0, [[1, 1024]], 0, np_), in0=sb(ytE3[by], 0, [[1, 1024]], 0, np_),
                        in1=sb(xrE[bx], 0, [[1, 1024]], 0, np_), op=ALU.add),
                        reads=[("ytE", by), ("xrE", bx)], writes=[("ytE", by)])
                    dst = AP(y_d, c0 * D, [[D, 128], [1, D]]) if tt < 16 else ys_d.ap()
                    S.dma("sp", lambda e: e.dma_start(out=dst, in_=sb(ytE3[by], 0, [[1, 1024]], 0, np_)),
                          reads=[("ytE", by), ("xrE", bx)], writes=[("yd", tt)])
                fin_q.append(fin2)
                if len(fin_q) >= 2:
                    fin_q.pop(0)()
    for f in fin_q:
        f()
    S.build()
    return nc


_NC_CACHE = {}


def _consts():
    cB = np.zeros((128, NCB), np.float32)
    p = np.arange(128)[:, None]
    c = np.arange(256)[None, :]
    m = np.where(c < 128, p <= c, p >= c - 128)
    cB[:, CB_MASK:CB_MASK + 256] = m
    cB[:, CB_ID:CB_ID + 128] = np.eye(128)
    cB[:, CB_ONE:CB_ONE + 128] = 1.0
    hh = np.arange(8)[:, None]
    cc = np.arange(512)[None, :]
    cB[0:8, CB_BLK:CB_BLK + 512] = (cc // 64 == hh)
    return cB.astype(ml_dtypes.bfloat16)


def kernel(x_prompt, x_sample, cache_kv_w128, cache_kv_w512, cache_kv_w2048, state_conv,
           w_in, conv_w, w_attn_o, w_conv_o, w_o, w_ff1, w_ff2,
           g_mix_pre, g_mix_post, g_ffn_pre, g_ffn_post):
    f32 = np.float32
    x_prompt = np.asarray(x_prompt, f32)
    x_sample = np.asarray(x_sample, f32)
    caches = [np.asarray(c, f32) for c in (cache_kv_w128, cache_kv_w512, cache_kv_w2048)]
    state_conv = np.asarray(state_conv, f32)
    w_in = np.asarray(w_in, f32)[0]
    conv_w = np.asarray(conv_w, f32)[0]
    w_attn_o = np.asarray(w_attn_o, f32)[0]
    w_conv_o = np.asarray(w_conv_o, f32)[0]
    w_o = np.asarray(w_o, f32)[0]
    w_ff1 = np.asarray(w_ff1, f32)[0]
    w_ff2 = np.asarray(w_ff2, f32)[0]
    g1 = np.asarray(g_mix_pre, f32)[0]
    g2 = np.asarray(g_mix_post, f32)[0]
    g3 = np.asarray(g_ffn_pre, f32)[0]
    g4 = np.asarray(g_ffn_post, f32)[0]

    def pack_kc(w):
        K, N = w.shape
        return np.ascontiguousarray(w.reshape(K // 128, 128, N).transpose(1, 0, 2))

    wB = np.empty((4, 128, 8, 1152), f32)
    for hp in range(4):
        cols = []
        for base in (0, 1536, 3072):
            for g in range(3):
                cols.append(np.arange(base + g * 512 + hp * 128, base + g * 512 + hp * 128 + 128))
        wB[hp] = pack_kc(w_in[:, np.concatenate(cols)])
    wB = wB.reshape(4, 128, 9216)
    wC1 = np.empty((8, 128, 8, 384), f32)
    for fc in range(8):
        cols = np.concatenate([np.arange(b0 + fc * 128, b0 + fc * 128 + 128) for b0 in (4608, 5632, 6656)])
        wC1[fc] = pack_kc(w_in[:, cols])
    wC1 = wC1.reshape(8, 128, 3072)
    wC2 = np.empty((8, 128, 3584), f32)
    for oc in range(8):
        cols = np.concatenate([np.arange(b0 + oc * 128, b0 + oc * 128 + 128) for b0 in (7680, 8704)])
        wC2[oc, :, 0:2048] = pack_kc(w_in[:, cols]).reshape(128, 2048)
        wC2[oc, :, 2048:3072] = pack_kc(w_conv_o[:, oc * 128:(oc + 1) * 128]).reshape(128, 1024)
        wC2[oc, :, 3072:3584] = pack_kc(w_attn_o[:, oc * 128:(oc + 1) * 128]).reshape(128, 512)
    wO = pack_kc(w_o).reshape(128, 8192)
    wF = np.empty((8, 128, 8192), f32)
    for cg in range(8):
        wF[cg, :, 0:4096] = pack_kc(w_ff1[:, cg * 512:(cg + 1) * 512]).reshape(128, 4096)
        wF[cg, :, 4096:8192] = pack_kc(w_ff2[cg * 512:(cg + 1) * 512, :]).reshape(128, 4096)
    cBc = _consts()

    in_maps = []
    for core in range(8):
        b, j = core // 4, core % 4
        xa = np.zeros((NH + NT, D), f32)
        if j > 0:
            xa[:NH] = x_prompt[b, NT * j - NH:NT * j]
        xa[NH:] = x_prompt[b, NT * j:NT * (j + 1)]
        sl = slice(NS * core, NS * core + NS)
        cFc = np.zeros((128, NCF), f32)
        cFc[:, CF_G1:CF_G1 + 8] = g1.reshape(8, 128).T
        cFc[:, CF_G3:CF_G3 + 8] = g3.reshape(8, 128).T
        cFc[:, CF_CW:CF_CW + 24] = conv_w.reshape(3, 8, 128).transpose(2, 1, 0).reshape(128, 24)
        stc = state_conv[0, sl]
        cFc[:, CF_ST:CF_ST + 64] = stc.reshape(NS, 2, 8, 128).transpose(3, 2, 1, 0).reshape(128, 64)
        cFc[:, CF_HV] = 1.0 if j > 0 else 0.0
        cFc[0:4, CF_OH:CF_OH + 4] = np.eye(4)
        cFc[:, CF_EPS] = 1e-6
        pp = np.arange(128)[:, None, None]
        cch = np.arange(4)[None, :, None]
        hh = np.arange(8)[None, None, :]
        cFc[:, CF_BM:CF_BM + 32] = (hh == 2 * cch + pp // 64).reshape(128, 32)
        in_maps.append({
            "x_all": xa,
            "xs": np.ascontiguousarray(x_sample[sl, 0]),
            "c128": np.ascontiguousarray(caches[0][0, sl]).reshape(NS * 128, 1024),
            "c512": np.ascontiguousarray(caches[1][0, sl]).reshape(NS * 512, 1024),
            "c2048": np.ascontiguousarray(caches[2][0, sl]).reshape(NS * 2048, 1024),
            "st1": np.ascontiguousarray(state_conv[0, sl, 1]),
            "wB": wB, "wC1": wC1, "wC2": wC2, "wO": wO, "wF": wF,
            "g1": g1.reshape(1, D), "g3": g3.reshape(1, D),
            "g2": g2.reshape(1, D), "g4": g4.reshape(1, D),
            "constF": cFc, "constB": cBc,
        })
    ncores = int(os.environ.get("MK_CORES", "8"))
    nc = build_program()
    res = run_bass_kernel_spmd(nc, in_maps[:ncores], core_ids=list(range(ncores)))
    R = list(res.results)
    while len(R) < 8:
        R.append(R[0])

    y_prompt = np.empty((2, 8192, D), f32)
    y_sample = np.empty((32, 1, D), f32)
    kvp = [np.empty((1, 2, WIN[g], 2, 8, 64), f32) for g in range(3)]
    conv_prompt = np.empty((1, 2, 2, D), f32)
    kvs = [np.empty((1, 32, WIN[g], 2, 8, 64), f32) for g in range(3)]
    conv_sample = np.empty((1, 32, 2, D), f32)
    names_p = ("kvp128", "kvp512", "kvp2048")
    names_s = ("kvs128", "kvs512", "kvs2048")
    for core in range(8):
        b, j = core // 4, core % 4
        r = R[core]
        y_prompt[b, NT * j:NT * (j + 1)] = r["y"]
        sl = slice(NS * core, NS * core + NS)
        y_sample[sl, 0] = r["ys"]
        if j == 3:
            for g in range(3):
                kvp[g][0, b] = r[names_p[g]].reshape(WIN[g], 2, 8, 64)
            conv_prompt[0, b] = r["convp"].reshape(128, 8, 2).transpose(2, 1, 0).reshape(2, D)
        for g in range(3):
            kvs[g][0, sl] = r[names_s[g]].reshape(NS, WIN[g], 2, 8, 64)
        conv_sample[0, sl, 0] = r["convs0"]
        conv_sample[0, sl, 1] = r["convs1"].reshape(128, 8, NS).transpose(2, 1, 0).reshape(NS, D)
    return (y_prompt, y_sample, kvp[0], kvp[1], kvp[2], conv_prompt,
            kvs[0], kvs[1], kvs[2], conv_sample)
```
